# Optimizing a Trainium2 kernel written in Bass

```python
import math
import jax
import jax.numpy as jnp
from jax import lax
import numpy as np

D_MODEL = 1024
BATCH = 2
SEQ = 8192
DEPTH = 2

ROPE_THETA = 10000.0
NORM_EPS = 1e-6
QBLK = 128
NEG = -1e30

A_HEADS = 6
A_HEAD_DIM = 64
A_CONFIGS = ((128, 1), (512, 4), (2048, 16))
B_HEADS = 6
B_NOPE = 64
B_ROPE = 32
B_V = 64
B_Q_LORA = 384
B_KV_LORA = 128
C_HEADS = 4
C_QK = 32
C_V = 2 * C_QK
C_SUBLN_EPS = 1e-5

A_WIDTH = A_HEADS * A_HEAD_DIM
B_WIDTH = B_HEADS * B_V
C_WIDTH = C_HEADS * C_V
MIX_WIDTH = A_WIDTH + B_WIDTH + C_WIDTH

IN_SPLITS = (A_WIDTH, A_WIDTH, A_WIDTH,
             B_Q_LORA, B_KV_LORA, B_ROPE,
             C_HEADS * 2 * C_QK, C_HEADS * 2 * C_QK, C_WIDTH)
IN_COLS = int(sum(IN_SPLITS))
D_FF = 4 * D_MODEL

kernel_name = "hybrid_dilated_mla_diffattn_block"


def rmsnorm(x, g, eps=NORM_EPS):
    xf = x.astype(jnp.float32)
    y = xf * lax.rsqrt(jnp.mean(xf * xf, axis=-1, keepdims=True) + eps)
    return (y * g.astype(jnp.float32)).astype(x.dtype)


def rope_tables(seq, dim):
    half = dim // 2
    inv = 1.0 / (ROPE_THETA ** (jnp.arange(half, dtype=jnp.float32) / half))
    ang = jnp.arange(seq, dtype=jnp.float32)[:, None] * inv[None, :]
    return jnp.cos(ang), jnp.sin(ang)


def apply_rope(x, cos, sin):
    half = x.shape[-1] // 2
    xf = x.astype(jnp.float32)
    x1, x2 = xf[..., :half], xf[..., half:]
    return jnp.concatenate([x1 * cos - x2 * sin, x1 * sin + x2 * cos], axis=-1).astype(x.dtype)


def to_heads(t, n_heads):
    b, s, _ = t.shape
    return t.reshape(b, s, n_heads, -1).transpose(0, 2, 1, 3)


def from_heads(t):
    b, h, s, d = t.shape
    return t.transpose(0, 2, 1, 3).reshape(b, s, h * d)


def banded_attn(q, k, v, span, scale):
    b, g, n, dh = q.shape
    nb = n // span
    qb = q.reshape(b, g, nb, span, dh)
    kb = k.reshape(b, g, nb, span, dh)
    vb = v.reshape(b, g, nb, span, v.shape[-1])

    def with_prev(t):
        prev = jnp.pad(t[:, :, :-1], ((0, 0), (0, 0), (1, 0), (0, 0), (0, 0)))
        return jnp.concatenate([prev, t], axis=3)

    kk, vv = with_prev(kb), with_prev(vb)
    s = jnp.einsum('bgnqd,bgnkd->bgnqk', qb, kk).astype(jnp.float32) * scale
    qi = jnp.arange(span)[:, None]
    ki = jnp.arange(2 * span)[None, :]
    dist = qi + span - ki
    blk = jnp.arange(nb)[:, None, None]
    valid = (dist >= 0) & (dist <= span) & ((blk > 0) | (ki >= span))
    s = jnp.where(valid, s, NEG)
    lse = jax.nn.logsumexp(s, axis=-1)
    p = jnp.exp(s - lse[..., None])
    out = jnp.einsum('bgnqk,bgnkd->bgnqd', p.astype(v.dtype), vv)
    return out.reshape(b, g, n, -1), lse.reshape(b, g, n)


def dilated_attn(q, k, v, window, dilation, scale):
    b, h, s, dh = q.shape
    span = window // dilation
    unit = dilation * span
    sp = -(-s // unit) * unit
    sub = sp // dilation

    def split(t):
        t = jnp.pad(t, ((0, 0), (0, 0), (0, sp - s), (0, 0)))
        return t.reshape(b, h, sub, dilation, t.shape[-1]).transpose(0, 1, 3, 2, 4).reshape(b, h * dilation, sub, t.shape[-1])

    out, lse = banded_attn(split(q), split(k), split(v), span, scale)
    out = out.reshape(b, h, dilation, sub, -1).transpose(0, 1, 3, 2, 4).reshape(b, h, sp, -1)[:, :, :s]
    lse = lse.reshape(b, h, dilation, sub).transpose(0, 1, 3, 2).reshape(b, h, sp)[:, :, :s]
    return out, lse


def causal_attn(q, k, v, scale, coef):
    b, h, m, s, dq = q.shape
    nb = s // QBLK
    qb = q.reshape(b, h, m, nb, QBLK, dq).transpose(3, 0, 1, 2, 4, 5)
    kpos = jnp.arange(s)

    def one(args):
        qblk, i = args
        sc = jnp.einsum('bhmqd,bhmkd->bhmqk', qblk, k).astype(jnp.float32) * scale
        qpos = i * QBLK + jnp.arange(QBLK)
        sc = jnp.where(kpos[None, :] <= qpos[:, None], sc, NEG)
        p = jnp.einsum('bhmqk,m->bhqk', jax.nn.softmax(sc, axis=-1), coef)
        return jnp.einsum('bhqk,bhkd->bhqd', p.astype(v.dtype), v)

    out = lax.map(one, (qb, jnp.arange(nb)))
    return out.transpose(1, 2, 0, 3, 4).reshape(b, h, s, -1)


def mixer_dilated(qa, ka, va, cos64, sin64):
    q = apply_rope(to_heads(qa, A_HEADS), cos64, sin64)
    k = apply_rope(to_heads(ka, A_HEADS), cos64, sin64)
    v = to_heads(va, A_HEADS)
    scale = A_HEAD_DIM ** -0.5
    outs, lses = [], []
    for window, dilation in A_CONFIGS:
        o, l = dilated_attn(q, k, v, window, dilation, scale)
        outs.append(o)
        lses.append(l)
    w = jax.nn.softmax(jnp.stack(lses, axis=0), axis=0)
    out = jnp.sum(w[..., None].astype(v.dtype) * jnp.stack(outs, axis=0), axis=0)
    return from_heads(out)


def mixer_mla(cq, ckv, kr, q_norm_g, w_uq, kv_norm_g, w_ukv, cos32, sin32):
    b, s, _ = cq.shape
    qf = to_heads(rmsnorm(cq, q_norm_g) @ w_uq, B_HEADS)
    kvf = to_heads(rmsnorm(ckv, kv_norm_g) @ w_ukv, B_HEADS)
    q = jnp.concatenate([qf[..., :B_NOPE], apply_rope(qf[..., B_NOPE:], cos32, sin32)], axis=-1)
    k_rope = jnp.broadcast_to(apply_rope(kr, cos32, sin32)[:, None], (b, B_HEADS, s, B_ROPE))
    k = jnp.concatenate([kvf[..., :B_NOPE], k_rope], axis=-1)
    v = kvf[..., B_NOPE:]
    coef = jnp.ones((1,), jnp.float32)
    out = causal_attn(q[:, :, None], k[:, :, None], v, (B_NOPE + B_ROPE) ** -0.5, coef)
    return from_heads(out)


def mixer_diff(qc, kc, vc, lam_params, subln_g, lam_init, cos32, sin32):
    b, s, _ = qc.shape
    q = apply_rope(qc.reshape(b, s, C_HEADS, 2, C_QK).transpose(0, 2, 3, 1, 4), cos32, sin32)
    k = apply_rope(kc.reshape(b, s, C_HEADS, 2, C_QK).transpose(0, 2, 3, 1, 4), cos32, sin32)
    v = to_heads(vc, C_HEADS)
    lp = lam_params.astype(jnp.float32)
    lam = jnp.exp(jnp.sum(lp[0] * lp[1])) - jnp.exp(jnp.sum(lp[2] * lp[3])) + lam_init
    coef = jnp.stack([jnp.ones((), jnp.float32), -lam])
    out = causal_attn(q, k, v, C_QK ** -0.5, coef)
    out = rmsnorm(out, subln_g, C_SUBLN_EPS) * (1.0 - lam_init)
    return from_heads(out)


def setup_inputs(seed: int = 0) -> dict:
    key = jax.random.key(seed)
    ks = jax.random.split(key, 16)

    def nrm(k, shape, fan):
        return jax.random.normal(k, shape, jnp.float32) * (fan ** -0.5)

    def gain(k, shape):
        return 1.0 + 0.02 * jax.random.normal(k, shape, jnp.float32)

    return {
        "x": jax.random.normal(ks[0], (BATCH, SEQ, D_MODEL), jnp.float32),
        "ln1_g": gain(ks[1], (DEPTH, D_MODEL)),
        "w_in": nrm(ks[2], (DEPTH, D_MODEL, IN_COLS), D_MODEL),
        "mla_q_norm_g": gain(ks[3], (DEPTH, B_Q_LORA)),
        "mla_w_uq": nrm(ks[4], (DEPTH, B_Q_LORA, B_HEADS * (B_NOPE + B_ROPE)), B_Q_LORA),
        "mla_kv_norm_g": gain(ks[5], (DEPTH, B_KV_LORA)),
        "mla_w_ukv": nrm(ks[6], (DEPTH, B_KV_LORA, B_HEADS * (B_NOPE + B_V)), B_KV_LORA),
        "diff_lambda": 0.1 * jax.random.normal(ks[7], (DEPTH, 4, C_QK), jnp.float32),
        "diff_subln_g": gain(ks[8], (DEPTH, C_V)),
        "w_o": nrm(ks[9], (DEPTH, MIX_WIDTH, D_MODEL), MIX_WIDTH),
        "ln2_g": gain(ks[10], (DEPTH, D_MODEL)),
        "w_up": nrm(ks[11], (DEPTH, D_MODEL, D_FF), D_MODEL),
        "w_down": nrm(ks[12], (DEPTH, D_FF, D_MODEL), D_FF),
        "final_g": gain(ks[13], (D_MODEL,)),
    }


def reference(x, ln1_g, w_in, mla_q_norm_g, mla_w_uq, mla_kv_norm_g, mla_w_ukv,
              diff_lambda, diff_subln_g, w_o, ln2_g, w_up, w_down, final_g):
    s = x.shape[1]
    cos64, sin64 = rope_tables(s, A_HEAD_DIM)
    cos32, sin32 = rope_tables(s, C_QK)
    split_idx = [int(v) for v in np.cumsum(IN_SPLITS)[:-1]]

    for layer in range(DEPTH):
        lam_init = 0.8 - 0.6 * math.exp(-0.3 * layer)
        h = rmsnorm(x, ln1_g[layer])
        proj = h @ w_in[layer]
        qa, ka, va, cq, ckv, kr, qc, kc, vc = jnp.split(proj, split_idx, axis=-1)

        out_a = mixer_dilated(qa, ka, va, cos64, sin64)
        out_b = mixer_mla(cq, ckv, kr, mla_q_norm_g[layer], mla_w_uq[layer],
                          mla_kv_norm_g[layer], mla_w_ukv[layer], cos32, sin32)
        out_c = mixer_diff(qc, kc, vc, diff_lambda[layer], diff_subln_g[layer], lam_init, cos32, sin32)

        mixed = jnp.concatenate([out_a, out_b, out_c], axis=-1)
        x = x + mixed @ w_o[layer]

        u = rmsnorm(x, ln2_g[layer]) @ w_up[layer]
        x = x + jnp.square(jax.nn.relu(u)) @ w_down[layer]

    return rmsnorm(x, final_g)
```

```python
import contextlib
import math
import numpy as np
import ml_dtypes
import concourse.bass as bass
import concourse.mybir as mybir
from concourse.bass_utils import run_bass_kernel_spmd

F32 = mybir.dt.float32
BF16 = mybir.dt.bfloat16
ALU = mybir.AluOpType
AF = mybir.ActivationFunctionType

D = 1024
SEQ = 8192
NCORE = 8
TL = 2048
NBLK = 16
DFF = 4096
OFF = dict(qa=0, ka=384, va=768, cq=1152, ckv=1536, kr=1664, qc=1696, kc=1952, vc=2208)
W1COLS = 3904
NVEC = 160

import os
PROBE_K128 = bool(os.environ.get("PROBE_K128"))
COMPUTE = ("pe", "act", "dve", "pool")
SEM_WRAP = 30000


class Op:
    __slots__ = ("eng", "fn", "deps", "idx", "sig", "dma_sem", "dma_cnt", "is_dma", "sigidx")

    def __init__(self, eng, fn, is_dma=False):
        self.eng = eng
        self.fn = fn
        self.deps = []
        self.sig = False
        self.is_dma = is_dma
        self.dma_sem = None
        self.dma_cnt = 0
        self.sigidx = -1
        self.idx = -1


class Prog:
    def __init__(self, nc, ext_sems=None):
        self.nc = nc
        self.ops = {e: [] for e in ("pe", "act", "dve", "pool", "sp")}
        self.lastw = {}
        self.readers = {}
        self.dma_sems = {}
        self.ext_sems = dict(ext_sems or {})
        for name, (h, base) in self.ext_sems.items():
            self.dma_sems[name] = base

    def external(self, semname, value, writes):
        o = Op("pool", None, is_dma=True)
        o.dma_sem = semname
        o.dma_cnt = value
        for k in writes:
            self.lastw[k] = o
            self.readers[k] = []
        return o

    def _add(self, op, reads, writes):
        deps = []
        for k in reads:
            w = self.lastw.get(k)
            if w is not None:
                deps.append(w)
        for k in writes:
            w = self.lastw.get(k)
            if w is not None:
                deps.append(w)
            deps.extend(self.readers.get(k, ()))
        seen = set()
        for d in deps:
            if d is op or id(d) in seen:
                continue
            seen.add(id(d))
            op.deps.append(d)
        for k in reads:
            lst = self.readers.setdefault(k, [])
            if not op.is_dma:
                for i, r in enumerate(lst):
                    if (not r.is_dma) and r.eng == op.eng:
                        lst[i] = op
                        break
                else:
                    lst.append(op)
            else:
                lst.append(op)
        for k in writes:
            self.lastw[k] = op
            self.readers[k] = []
        op.idx = len(self.ops[op.eng])
        self.ops[op.eng].append(op)
        return op

    def op(self, eng, fn, reads=(), writes=()):
        return self._add(Op(eng, fn), reads, writes)

    def dma(self, eng, semname, fn, reads=(), writes=(), inc=16):
        o = Op(eng, fn, is_dma=True)
        self.dma_sems[semname] = self.dma_sems.get(semname, 0) + inc
        o.dma_sem = semname
        o.dma_cnt = self.dma_sems[semname]
        o.sigidx = inc
        return self._add(o, reads, writes)

    @staticmethod
    def _skip_same(d, o):
        return (not d.is_dma) and d.eng == o.eng and (d.eng == "pe" or o.idx - d.idx > 2)

    def emit(self, final_wait_ops=(), tag="", managed=True):
        nc = self.nc
        for e, lst in self.ops.items():
            for o in lst:
                for d in o.deps:
                    if d.is_dma or self._skip_same(d, o):
                        continue
                    d.sig = True
        for o in final_wait_ops:
            if not o.is_dma:
                o.sig = True
        nsig = {}
        for e in COMPUTE:
            c = 0
            for o in self.ops[e]:
                if o.sig and not o.is_dma:
                    o.sigidx = c
                    c += 1
            nsig[e] = c
        with contextlib.ExitStack() as st:
            if managed:
                mksem = lambda nm: st.enter_context(nc.semaphore(tag + nm))
            else:
                mksem = lambda nm: nc.alloc_semaphore(name=tag + nm)
            csem = {}
            for e in COMPUTE:
                n = max(1, (nsig[e] + SEM_WRAP - 1) // SEM_WRAP)
                csem[e] = [mksem(f"c_{e}_{i}") for i in range(n)]
            dsem = {name: (self.ext_sems[name][0] if name in self.ext_sems else mksem(f"d_{name}")) for name in self.dma_sems}
            block = st.enter_context(nc.Block())
            engobj = {"pe": "tensor", "act": "scalar", "dve": "vector", "pool": "gpsimd", "sp": "sync"}

            def run_engine(e, eng):
                waited = {}
                for o in self.ops[e]:
                    need = {}
                    for d in o.deps:
                        if d.is_dma:
                            key = ("d", d.dma_sem)
                            val = d.dma_cnt
                            sem = dsem[d.dma_sem]
                        else:
                            if self._skip_same(d, o):
                                continue
                            si = d.sigidx // SEM_WRAP
                            key = ("c", d.eng, si)
                            val = d.sigidx % SEM_WRAP + 1
                            sem = csem[d.eng][si]
                        if waited.get(key, 0) >= val:
                            continue
                        if key not in need or need[key][1] < val:
                            need[key] = (sem, val)
                    for key, (sem, val) in need.items():
                        eng.wait_ge(sem, val)
                        waited[key] = val
                    ins = o.fn(eng)
                    if o.is_dma:
                        ins.then_inc(dsem[o.dma_sem], o.sigidx)
                    elif o.sig:
                        ins.then_inc(csem[e][o.sigidx // SEM_WRAP], 1)
                if e == "sp":
                    fin = {}
                    for o in final_wait_ops:
                        if o.is_dma:
                            key, sem, val = ("d", o.dma_sem), dsem[o.dma_sem], o.dma_cnt
                        else:
                            si = o.sigidx // SEM_WRAP
                            key, sem, val = ("c", o.eng, si), csem[o.eng][si], o.sigidx % SEM_WRAP + 1
                        if key not in fin or fin[key][1] < val:
                            fin[key] = (sem, val)
                    for sem, val in fin.values():
                        eng.wait_ge(sem, val)

            for e in ("sp", "pool", "act", "dve", "pe"):
                getattr(block, engobj[e])(lambda eng, e=e: run_engine(e, eng))


class Ctx:
    def __init__(self, nc, st, tag="", ext_sems=None):
        self.nc = nc
        self.st = st
        self.tag = tag
        self.P = Prog(nc, ext_sems)
        self.pairs = [st.enter_context(nc.psum_tensor(f"{tag}pbank{i}", [128, 2, 512], F32)) for i in range(4)]
        self.banks = [self.pairs[i // 2][:, i % 2, :] for i in range(8)]
        self.uid = 0

    def sb(self, name, shape, dt):
        return self.st.enter_context(self.nc.sbuf_tensor(self.tag + "s_" + name, shape, dt))

    def dram(self, name, shape, dt, kind):
        return self.nc.dram_tensor(name, shape, dt, kind=kind).ap()

    def mm(self, out, lhsT, rhs, start, stop, reads, writes, **kw):
        return self.P.op("pe", lambda e: e.matmul(out, lhsT, rhs, start=start, stop=stop, **kw), reads, writes)

    def act(self, out, in_, func, reads, writes, bias=None, scale=None):
        kw = {}
        if bias is not None:
            kw["bias"] = bias
        if scale is not None:
            kw["scale"] = scale
        return self.P.op("act", lambda e: e.activation(out=out, in_=in_, func=func, **kw), reads, writes)

    def tt(self, eng, out, in0, in1, op, reads, writes):
        return self.P.op(eng, lambda e: e.tensor_tensor(out=out, in0=in0, in1=in1, op=op), reads, writes)

    def ts(self, eng, out, in0, s1, op0, reads, writes, s2=None, op1=None):
        if op1 is None:
            return self.P.op(eng, lambda e: e.tensor_scalar(out=out, in0=in0, scalar1=s1, scalar2=None, op0=op0), reads, writes)
        return self.P.op(eng, lambda e: e.tensor_scalar(out=out, in0=in0, scalar1=s1, scalar2=s2, op0=op0, op1=op1), reads, writes)

    def stt(self, eng, out, in0, scalar, in1, op0, op1, reads, writes):
        return self.P.op(eng, lambda e: e.scalar_tensor_tensor(out=out, in0=in0, scalar=scalar, in1=in1, op0=op0, op1=op1), reads, writes)

    def copy(self, eng, out, in_, reads, writes):
        if eng == "act":
            return self.P.op("act", lambda e: e.copy(out=out, in_=in_), reads, writes)
        return self.P.op(eng, lambda e: e.tensor_copy(out=out, in_=in_), reads, writes)

    def recip(self, out, in_, reads, writes):
        return self.P.op("dve", lambda e: e.reciprocal(out=out, in_=in_), reads, writes)

    def memset(self, eng, ap, val, writes):
        return self.P.op(eng, lambda e: e.memset(ap, val), (), writes)

    def dma(self, q, sem, out, in_, reads, writes):
        return self.P.dma(q, sem, lambda e: e.dma_start(out=out, in_=in_), reads, writes)


class WStream:
    def __init__(self, K, slots, keys, items):
        self.K, self.slots, self.keys, self.items = K, slots, keys, items
        self.issued = 0
        self.cur = 0

    def _issue(self):
        i = self.issued
        if i >= len(self.items):
            return
        src, shape = self.items[i]
        n = 1
        for d in shape[1:]:
            n *= d
        ns = len(self.slots)
        dst = self.slots[i % ns][:, 0:n]
        if len(shape) == 3:
            dst = dst.rearrange("p (a b) -> p a b", a=shape[1])
        key = self.keys[i % ns]
        self.K.P.dma("pool", key, lambda e: e.dma_start(out=dst, in_=src), [], [key])
        self.views = getattr(self, "views", {})
        self.views[i] = (dst, key)
        self.issued += 1

    def next(self):
        while self.issued <= min(self.cur + 1, len(self.items) - 1):
            self._issue()
        v = self.views[self.cur]
        self.cur += 1
        return v


def make_consts(K):
    c = {}
    for name, val in (("c1024", 1.0 / 1024), ("c384", 1.0 / 384), ("c128", 1.0 / 128), ("c64", 1.0 / 64), ("one", 1.0)):
        t = K.sb(name, [128, 128], F32)
        K.memset("pool", t[:], val, [name])
        c[name] = t
    for name, val in (("eps6", 1e-6), ("eps5", 1e-5)):
        t = K.sb(name, [128, 1], F32)
        K.memset("pool", t[:], val, [name])
        c[name] = t
    return c


def rms_stats(K, C, srcs, src_keys, rows, cname, epsname, bank, bank_key, rstd, rstd_key, sq, sqkeys):
    n = len(srcs)
    for i, s in enumerate(srcs):
        q = sq[i % 2]
        K.act(q[0:rows, :], s, AF.Square, [src_keys[i]], [sqkeys[i % 2]])
        K.mm(bank[:, :], C[cname][0:rows, :], q[0:rows, :], i == 0, i == n - 1, [sqkeys[i % 2], cname], [bank_key])
    K.act(rstd, bank[:, :], AF.Sqrt, [bank_key, epsname], [rstd_key], bias=C[epsname][:, 0:1], scale=1.0)
    K.recip(rstd, rstd, [rstd_key], [rstd_key])


FM_GROUPS = [
    (0, 512, [("ropeA", 0, 128, "QA", 0), ("ropeA", 256, 128, "QA", 128)]),
    (512, 512, [("ropeA", 0, 128, "QA", 256), ("ropeA", 256, 128, "KA", 0)]),
    (1024, 512, [("ropeA", 0, 128, "KA", 128), ("ropeA", 256, 128, "KA", 256)]),
    (1536, 512, [("ropeC", 0, 128, "QC", 0), ("ropeC", 256, 128, "QC", 128)]),
    (2048, 512, [("ropeC", 0, 128, "KC", 0), ("ropeC", 256, 128, "KC", 128)]),
    (2560, 512, [("cq", 0, 128, None, 0), ("cq", 128, 128, None, 1), ("cq", 256, 128, None, 2), ("ckv", 384, 128, None, 0)]),
    (3072, 192, [("kr", 0, 96, None, 0)]),
]
TM_GROUPS = [(3264, 384, 6, "VA"), (3648, 256, 4, "VC")]


def phase1(K, C, io):
    P = K.P
    banks = K.banks
    xT = io["xT"]
    W1 = io["W1"]
    vec = K.sb("vec", [128, NVEC], F32)
    K.dma("sp", "vec", vec[:], io["vecs"], [], ["vec"])
    xs = [K.sb(f"xs{i}", [128, 8, 512], F32) for i in range(2)]
    hT = K.sb("hT", [128, 8, TL], BF16)
    rstd = [K.sb(f"rstd{i}", [128, 512], F32) for i in range(2)]
    sq = [K.sb(f"sq{i}", [128, 512], F32) for i in range(2)]
    sqk = ["sq0", "sq1"]
    wsl = [K.sb(f"wsl{i}", [128, 4096], BF16) for i in range(2)]
    t64c = K.sb("t64c", [128, TL], F32)
    t64s = K.sb("t64s", [128, TL], F32)
    t32c = K.sb("t32c", [128, TL], F32)
    t32s = K.sb("t32s", [128, TL], F32)
    stg = [K.sb(f"stg{i}", [128, TL], BF16) for i in range(2)]
    r1 = [K.sb(f"r1_{i}", [128, 512], F32) for i in range(2)]
    r2 = [K.sb(f"r2_{i}", [128, 512], F32) for i in range(2)]
    cqT = K.sb("cqT", [128, 3, TL], F32)
    ckvT = K.sb("ckvT", [128, TL], F32)
    cqn = K.sb("cqn", [128, 3, TL], BF16)
    ckvn = K.sb("ckvn", [128, TL], BF16)
    krT = K.sb("krT", [128, TL], BF16)
    vst = [K.sb("vst0", [128, 6, NBLK, 65], BF16)] * 2
    K.memset("pool", vst[0][:], 1.0, ["vst0"])

    for t in range(4):
        x = xs[t % 2]
        xk = f"xs{t % 2}"
        K.dma("sp", xk, x[:], xT[:, :, t * 512:(t + 1) * 512], [], [xk])
        rs = rstd[t % 2]
        rk = f"rstd{t % 2}"
        rms_stats(K, C, [x[:, c, :] for c in range(8)], [xk] * 8, 128, "c1024", "eps6", banks[7], "bank7", rs[:], rk, sq, sqk)
        for c in range(8):
            K.stt("dve", hT[:, c, t * 512:(t + 1) * 512], x[:, c, :], vec[:, c:c + 1], rs[:], ALU.mult, ALU.mult,
                  [xk, rk, "vec"], [("hT", t)])

    K.dma("sp", "t64c", t64c[:], io["rope"][0], [], ["t64c"])
    K.dma("sp", "t64s", t64s[:], io["rope"][1], [], ["t64s"])
    K.dma("sp", "t32c", t32c[:], io["rope"][2], [], ["t32c"])
    K.dma("sp", "t32s", t32s[:], io["rope"][3], [], ["t32s"])
    items = [(W1[:, :, goff:goff + gcols], [128, 8, gcols]) for (goff, gcols, _) in FM_GROUPS[5:7]]
    items += [(W1[:, :, goff:goff + gcols], [128, 8, gcols]) for (goff, gcols, _) in FM_GROUPS[0:5]]
    items += [(W1[:, :, goff:goff + gcols], [128, 8, gcols]) for (goff, gcols, _, _) in TM_GROUPS]
    items += [(io["Wuq"], [128, 3, 1152]), (io["Wukv"], [128, 768])]
    ws = WStream(K, wsl, ["wsl0", "wsl1"], items)
    wload = lambda src, shape: ws.next()

    pcnt = [0]
    scnt = [0]
    rcnt = [0]

    def rope(psA, psB, ka, kb, cosT, sinT, ck, sk, out, okey, p0, p1, t):
        i = rcnt[0] % 2
        rcnt[0] += 1
        sl = slice(t * 512, (t + 1) * 512)
        K.tt("dve", r1[i][p0:p1, :], psA[p0:p1, :], cosT[p0:p1, sl], ALU.mult, [ka, ck], [f"r1_{i}"])
        K.tt("dve", r2[i][p0:p1, :], psB[p0:p1, :], sinT[p0:p1, sl], ALU.mult, [kb, sk], [f"r2_{i}"])
        K.tt("pool", out, r1[i][p0:p1, :], r2[i][p0:p1, :], ALU.add, [f"r1_{i}", f"r2_{i}"], [okey])

    ag_pending = list(io.get("ag_pieces", []))
    ag_ready = []

    def try_ag(flush=False):
        for item in ag_ready[:]:
            key, src, dst, deps = item
            K.P.dma("pool", "ag" + key[2], lambda e, src=src, dst=dst: e.collective_compute(
                "AllGather", ALU.bypass, replica_groups=RG, ins=[src], outs=[dst[:, :]]), deps, [key], inc=1)
            io["ag_ops"][key] = K.P.lastw[key]
            ag_ready.remove(item)
        for item in ag_pending[:]:
            if all(k in K.P.lastw for k in item[3]):
                ag_pending.remove(item)
                ag_ready.append(item)
        if flush and (ag_ready or ag_pending):
            try_ag(flush=True)

    def fm_groups(groups):
        for (goff, gcols, tiles) in groups:
            w, wk = wload(W1[:, :, goff:goff + gcols], [128, 8, gcols])
            for (kind, loff, M, oname, orow) in tiles:
                roped = kind in ("ropeA", "ropeC", "kr")
                si = scnt[0] % 2
                if kind in ("ropeA", "ropeC"):
                    scnt[0] += 1
                for t in range(4):
                    sl = slice(t * 512, (t + 1) * 512)
                    ba = pcnt[0] % 4
                    pcnt[0] += 1
                    psA = banks[ba]
                    ka = f"bank{ba}"
                    for c in range(8):
                        K.mm(psA[0:M, :], w[:, c, loff:loff + M], hT[:, c, sl], c == 0, c == 7, [wk, ("hT", t)], [ka])
                    if roped:
                        bb = pcnt[0] % 4
                        pcnt[0] += 1
                        psB = banks[bb]
                        kb = f"bank{bb}"
                        so = loff + (128 if kind != "kr" else 96)
                        for c in range(8):
                            K.mm(psB[0:M, :], w[:, c, so:so + M], hT[:, c, sl], c == 0, c == 7, [wk, ("hT", t)], [kb])
                    if kind == "ropeA":
                        rope(psA, psB, ka, kb, t64c, t64s, "t64c", "t64s", stg[si][:, sl], f"stg{si}", 0, 128, t)
                    elif kind == "ropeC":
                        rope(psA, psB, ka, kb, t32c, t32s, "t32c", "t32s", stg[si][:, sl], f"stg{si}", 0, 128, t)
                    elif kind == "kr":
                        rope(psA, psB, ka, kb, t32c, t32s, "t32c", "t32s", krT[64:96, sl], "krT", 64, 96, t)
                    elif kind == "cq":
                        K.copy("act", cqT[:, orow, sl], psA[:, :], [ka], [("cqT", t)])
                    elif kind == "ckv":
                        K.copy("act", ckvT[:, sl], psA[:, :], [ka], [("ckvT", t)])
                if kind in ("ropeA", "ropeC"):
                    K.dma("sp", f"stg{si}", io[oname][orow:orow + 128, :], stg[si][:, :], [f"stg{si}"], [(oname, orow)])
                    try_ag()


    def tm_groups():
        vcnt = 0
        for (goff, gcols, nh, oname) in TM_GROUPS:
            w, wk = wload(W1[:, :, goff:goff + gcols], [128, 8, gcols])
            vi = vcnt % 2
            vcnt += 1
            vk = "vst0"
            for m in range(NBLK):
                bi = 4 + (m % 2)
                ps = banks[bi]
                pk = f"bank{bi}"
                t = m // 4
                for c in range(8):
                    K.mm(ps[:, 0:gcols], hT[:, c, m * 128:(m + 1) * 128], w[:, c, 0:gcols], c == 0, c == 7, [wk, ("hT", t)], [pk])
                K.copy("act" if m % 2 == 0 else "dve", vst[vi][:, 0:nh, m, 0:64],
                       ps[:, 0:gcols].rearrange("p (h d) -> p h d", h=nh), [pk], [vk])
            for h in range(nh):
                K.dma("sp", f"{vk}_{h // (3 if nh == 6 else 2)}", io[oname][h], vst[vi][:, h, :, :], [vk], [(oname, h)])
            try_ag()

        return vcnt

    def mla_norms():
        for t in range(4):
            sl = slice(t * 512, (t + 1) * 512)
            rs = rstd[t % 2]
            rk = f"rstd{t % 2}"
            rms_stats(K, C, [cqT[:, c, sl] for c in range(3)], [("cqT", t)] * 3, 128, "c384", "eps6", banks[7], "bank7", rs[:], rk, sq, sqk)
            for c in range(3):
                K.stt("dve", cqn[:, c, sl], cqT[:, c, sl], vec[:, 24 + c:25 + c], rs[:], ALU.mult, ALU.mult,
                      [("cqT", t), rk, "vec"], [("cqn", t)])
            rs = rstd[(t + 1) % 2]
            rk = f"rstd{(t + 1) % 2}"
            rms_stats(K, C, [ckvT[:, sl]], [("ckvT", t)], 128, "c128", "eps6", banks[6], "bank6", rs[:], rk, sq, sqk)
            K.stt("dve", ckvn[:, sl], ckvT[:, sl], vec[:, 27:28], rs[:], ALU.mult, ALU.mult, [("ckvT", t), rk, "vec"], [("ckvn", t)])

    def mla(vcnt):
        wq, wqk = wload(io["Wuq"], [128, 3, 1152])
        for h in range(6):
            si = scnt[0] % 2
            scnt[0] += 1
            for t in range(4):
                sl = slice(t * 512, (t + 1) * 512)
                ba = pcnt[0] % 4
                pcnt[0] += 1
                bb = pcnt[0] % 4
                pcnt[0] += 1
                psA, psB = banks[ba], banks[bb]
                ka, kb = f"bank{ba}", f"bank{bb}"
                for c in range(3):
                    K.mm(psA[0:96, :], wq[:, c, h * 96:(h + 1) * 96], cqn[:, c, sl], c == 0, c == 2, [wqk, ("cqn", t)], [ka])
                for c in range(3):
                    K.mm(psB[0:96, :], wq[:, c, 576 + h * 96:576 + (h + 1) * 96], cqn[:, c, sl], c == 0, c == 2, [wqk, ("cqn", t)], [kb])
                K.copy("act", stg[si][0:64, sl], psA[0:64, :], [ka], [f"stg{si}"])
                rope(psA, psB, ka, kb, t32c, t32s, "t32c", "t32s", stg[si][64:96, sl], f"stg{si}", 64, 96, t)
            K.dma("sp", f"stg{si}", io["QB"][h], stg[si][0:96, :], [f"stg{si}"], [("QB", h)])

        wkv, wkvk = wload(io["Wukv"], [128, 768])
        for h in range(6):
            si = scnt[0] % 2
            scnt[0] += 1
            for t in range(4):
                sl = slice(t * 512, (t + 1) * 512)
                ba = pcnt[0] % 4
                pcnt[0] += 1
                psA = banks[ba]
                ka = f"bank{ba}"
                K.mm(psA[0:64, :], wkv[:, h * 64:(h + 1) * 64], ckvn[:, sl], True, True, [wkvk, ("ckvn", t)], [ka])
                K.copy("act" if t % 2 == 0 else "dve", stg[si][0:64, sl], psA[0:64, :], [ka], [f"stg{si}"])
            K.dma("sp", f"stg{si}", io["KB"][h, 0:64, :], stg[si][0:64, :], [f"stg{si}"], [("KB", h)])
            K.dma("sp", f"krT_out{h // 2}", io["KB"][h, 64:96, :], krT[64:96, :], ["krT"], [("KBr", h)])
            try_ag()
        vi = vcnt % 2
        vcnt += 1
        vk = "vst0"
        for m in range(NBLK):
            bi = 4 + (m % 2)
            ps = banks[bi]
            pk = f"bank{bi}"
            t = m // 4
            K.mm(ps[:, 0:384], ckvn[:, m * 128:(m + 1) * 128], wkv[:, 384:768], True, True, [wkvk, ("ckvn", t)], [pk])
            K.copy("act" if m % 2 == 0 else "dve", vst[vi][:, 0:6, m, 0:64],
                   ps[:, 0:384].rearrange("p (h d) -> p h d", h=6), [pk], [vk])
        for h in range(6):
            K.dma("sp", f"{vk}_{h // 3}", io["VB"][h], vst[vi][:, h, :, :], [vk], [("VB", h)])
        try_ag()

    fm_groups(FM_GROUPS[5:7])
    mla_norms()
    fm_groups(FM_GROUPS[0:5])
    vc = tm_groups()
    mla(vc)
    try_ag(flush=True)


P1_OUT = dict(QA=[384, TL], KA=[384, TL], QC=[256, TL], KC=[256, TL], QB=[6, 96, TL], KB=[6, 96, TL],
              VA=[6, 128, NBLK, 65], VB=[6, 128, NBLK, 65], VC=[4, 128, NBLK, 65])


def build_p1():
    nc = bass.Bass("TRN2", target_bir_lowering=False)
    with contextlib.ExitStack() as st:
        K = Ctx(nc, st)
        io = {}
        io["xT"] = K.dram("xT", [128, 8, TL], F32, "ExternalInput")
        io["W1"] = K.dram("W1", [128, 8, W1COLS], F32, "ExternalInput")
        io["Wuq"] = K.dram("Wuq", [128, 3, 1152], F32, "ExternalInput")
        io["Wukv"] = K.dram("Wukv", [128, 768], F32, "ExternalInput")
        io["vecs"] = K.dram("vecs", [128, NVEC], F32, "ExternalInput")
        rope = K.dram("rope", [4, 128, TL], F32, "ExternalInput")
        io["rope"] = [rope[i] for i in range(4)]
        for name, shape in P1_OUT.items():
            io[name] = K.dram(name, shape, BF16, "ExternalOutput")
        C = make_consts(K)
        phase1(K, C, io)
        outs = [o for o in K.P.ops["sp"] if o.is_dma and o.dma_sem.startswith(("stg", "vst", "krT_out"))]
        K.P.emit(final_wait_ops=outs)
    return nc


def _swap_cols(cols, blk):
    cols = np.asarray(cols)
    out = cols.copy().reshape(-1, blk)
    half = blk // 2
    out = np.concatenate([out[:, half:], out[:, :half]], axis=1)
    return out.reshape(-1)


def w1_columns():
    cols = []
    for base in (OFF["qa"], OFF["ka"]):
        for t in range(3):
            c = np.arange(base + t * 128, base + (t + 1) * 128)
            cols += [c, _swap_cols(c, 64)]
    for base in (OFF["qc"], OFF["kc"]):
        for t in range(2):
            c = np.arange(base + t * 128, base + (t + 1) * 128)
            cols += [c, _swap_cols(c, 32)]
    cols.append(np.arange(OFF["cq"], OFF["cq"] + 384))
    cols.append(np.arange(OFF["ckv"], OFF["ckv"] + 128))
    filler = np.arange(OFF["ckv"], OFF["ckv"] + 64)
    kr = np.arange(OFF["kr"], OFF["kr"] + 32)
    cols += [filler, kr, filler, _swap_cols(kr, 32)]
    cols.append(np.arange(OFF["va"], OFF["va"] + 384))
    cols.append(np.arange(OFF["vc"], OFF["vc"] + 256))
    cols = np.concatenate(cols)
    assert cols.shape[0] == W1COLS
    return cols


def chunked(w, nchunk):
    n = w.shape[1]
    return np.ascontiguousarray(w.reshape(nchunk, 128, n).transpose(1, 0, 2))


def prep_layer(inp, l):
    f = lambda a: np.asarray(a, dtype=np.float32)
    out = {}
    w_in = f(inp["w_in"][l])
    out["W1"] = chunked(w_in[:, w1_columns()], 8)
    wuq = f(inp["mla_w_uq"][l])
    ncols, scols = [], []
    for h in range(6):
        c = np.arange(h * 96, (h + 1) * 96)
        ncols.append(c)
        scols.append(np.concatenate([c[:64], _swap_cols(c[64:], 32)]))
    out["Wuq"] = chunked(wuq[:, np.concatenate(ncols + scols)], 3)
    wukv = f(inp["mla_w_ukv"][l])
    kc = np.concatenate([np.arange(h * 128, h * 128 + 64) for h in range(6)])
    vc = np.concatenate([np.arange(h * 128 + 64, h * 128 + 128) for h in range(6)])
    out["Wukv"] = np.ascontiguousarray(wukv[:, np.concatenate([kc, vc])])
    vec = np.zeros((128, NVEC), np.float32)
    vec[:, 0:8] = f(inp["ln1_g"][l]).reshape(8, 128).T
    vec[:, 8:16] = f(inp["ln2_g"][l]).reshape(8, 128).T
    vec[:, 16:24] = f(inp["final_g"]).reshape(8, 128).T
    vec[:, 24:27] = f(inp["mla_q_norm_g"][l]).reshape(3, 128).T
    vec[:, 27] = f(inp["mla_kv_norm_g"][l])
    vec[:, 28] = np.tile(f(inp["diff_subln_g"][l]), 2)
    lam_init = 0.8 - 0.6 * math.exp(-0.3 * l)
    vec[:, 29] = lam_init
    vec[:, 30] = 1.0 - lam_init
    vec[:, 32:160] = f(inp["diff_lambda"][l]).reshape(1, 128)
    out["vecs"] = vec
    out["Wo"] = chunked(f(inp["w_o"][l]), 8)
    out["Wup"] = chunked(f(inp["w_up"][l]), 8)
    out["Wdown"] = chunked(f(inp["w_down"][l]), 32)
    return out


def core_positions(j):
    m = np.arange(NBLK)[:, None]
    t = np.arange(128)[None, :]
    return ((4 * m + j) * 128 + t).reshape(-1)


def rope_tables(j):
    pos = core_positions(j).astype(np.float32)
    tabs = []
    for dim in (64, 32):
        half = dim // 2
        inv = (1.0 / (np.float32(10000.0) ** (np.arange(half, dtype=np.float32) / np.float32(half)))).astype(np.float32)
        r = np.arange(128)
        i = r % dim
        fidx = i % half
        ang = (pos[None, :] * inv[fidx][:, None]).astype(np.float32)
        sign = np.where(i < half, -1.0, 1.0).astype(np.float32)[:, None]
        tabs.append(np.cos(ang).astype(np.float32))
        tabs.append((np.sin(ang) * sign).astype(np.float32))
    return np.stack(tabs, 0)


def x_to_core(x, c):
    b, j = c // 4, c % 4
    xs = np.asarray(x[b], dtype=np.float32)[core_positions(j)]
    return chunked(np.ascontiguousarray(xs.T), 8)


def a_cols(ak):
    return max(0, ak) * 128, min(4, ak + 5) * 128


def phase2(K, C, io):
    P = K.P
    banks = K.banks
    vec = K.sb("vec", [128, NVEC], F32)
    K.dma("sp", "vec", vec[:], io["vecs"], [], ["vec"])
    MAt = K.sb("MAt", [128, 32, 512], BF16)
    MBt = K.sb("MBt", [128, 16, 512], BF16)

    kt = [K.sb(f"kt{i}", [128, 4, TL], BF16) for i in range(2)]
    vt = [K.sb(f"vt{i}", [128, 4 * NBLK, 65], BF16) for i in range(2)]
    qa = [K.sb(f"qa{i}", [128, TL], BF16) for i in range(2)]
    qb = [K.sb(f"qb{i}", [128, TL], BF16) for i in range(2)]
    qc = [K.sb(f"qc{i}", [128, 2, TL], BF16) for i in range(2)]
    for i in range(2):
        K.memset("pool", qa[i][:], 0.0, [f"qa{i}"])
        K.memset("pool", qb[i][:], 0.0, [f"qb{i}"])
        K.memset("pool", qc[i][:], 0.0, [f"qc{i}"])
    NPT = 6
    LAG = 3
    NSB = 3
    pt = [K.sb(f"pt{i}", [128, 2, 512], BF16) for i in range(NPT)]
    osb = [K.sb(f"osb{i}", [128, 512], F32) for i in range(4)]
    rz = [K.sb(f"rz{i}", [128, 512], F32) for i in range(4)]
    ones512 = K.sb("ones512", [128, 512], F32)
    K.memset("pool", ones512[:], -1.0, ["ones512"])
    deferred = []
    deferred_a = []
    tA = K.sb("tA", [128, 512], F32)
    tB = K.sb("tB", [128, 512], F32)
    tO = K.sb("tO", [128, 512], F32)
    tS = K.sb("tS", [128, 512], F32)
    ostg = [K.sb(f"ostg{i}", [128, TL], BF16) for i in range(2)]
    lam = K.sb("lam", [128, 40], F32)

    K.tt("dve", lam[:, 0:32], vec[:, 32:64], vec[:, 64:96], ALU.mult, ["vec"], ["lam"])
    P.op("dve", lambda e: e.reduce_sum(out=lam[:, 32:33], in_=lam[:, 0:32], axis=mybir.AxisListType.X), ["lam"], ["lam1"])
    K.tt("dve", lam[:, 0:32], vec[:, 96:128], vec[:, 128:160], ALU.mult, ["vec", "lam1"], ["lam"])
    P.op("dve", lambda e: e.reduce_sum(out=lam[:, 33:34], in_=lam[:, 0:32], axis=mybir.AxisListType.X), ["lam"], ["lam2"])
    K.act(lam[:, 34:36], lam[:, 32:34], AF.Exp, ["lam1", "lam2"], ["lam3"])
    K.tt("dve", lam[:, 36:37], lam[:, 35:36], lam[:, 34:35], ALU.subtract, ["lam3"], ["lam4"])
    K.tt("dve", lam[:, 37:38], lam[:, 36:37], vec[:, 29:30], ALU.subtract, ["lam4", "vec"], ["neglam"])
    K.tt("dve", lam[:, 38:39], vec[:, 28:29], vec[:, 30:31], ALU.mult, ["vec"], ["gc"])
    neglam = lam[:, 37:38]
    gc = lam[:, 38:39]

    jobs = [("A", h) for h in range(6)] + [("C", h) for h in range(4)] + [("B", h) for h in range(6)]
    for i in range(2):
        K.memset("pool", kt[i][96:128, :, :], 0.0, [f"kt{i}"])

    jobslot = {}

    def load_job(ji):
        kind, h = jobs[ji]
        s = ji % 2
        a = io["acc"](kind, h)
        rows = a["rows"]
        if kind == "A":
            qs = h % 2
            K.dma("sp", f"qa{qs}", qa[qs][qs * 64:qs * 64 + 64, :], a["q"], [], [f"qa{qs}"])
            jobslot[ji] = [(qa[qs], f"qa{qs}", None)]
        elif kind == "B":
            qs = h % 2
            K.dma("sp", f"qb{qs}", qb[qs][0:96, :], a["q"], [], [f"qb{qs}"])
            jobslot[ji] = [(qb[qs], f"qb{qs}", None)]
        else:
            qs = h % 2
            for mi in range(2):
                p0 = qs * 64 + mi * 32
                K.dma("sp", f"qc{qs}", qc[qs][p0:p0 + 32, mi, :], a["q"][mi * 32:(mi + 1) * 32, :], [], [f"qc{qs}"])
            jobslot[ji] = [(qc[qs], f"qc{qs}", 0), (qc[qs], f"qc{qs}", 1)]
        for r in range(4):
            K.dma("sp", f"kt{s}", kt[s][0:rows, r, :], a["k"](r), a["kdeps"], [f"kt{s}"])
        for r in range(4):
            K.dma("sp", f"vt{s}", vt[s][:, r * NBLK:(r + 1) * NBLK, :], a["v"](r), a["vdeps"], [f"vt{s}"])

    cnt = dict(s=0, p=0, o=0, f=0)
    out_ops = []
    K.dma("sp", "MAt_hi", MAt[:, 16:32, :], io["MA"][:, 16:32, :], [], ["MAt_hi"])
    load_job(0)
    K.dma("sp", "MAt_lo", MAt[:, 0:16, :], io["MA"][:, 0:16, :], [], ["MAt_lo"])
    K.dma("sp", "MBt", MBt[:], io["MB"], [], ["MBt"])
    for ji, (kind, h) in enumerate(jobs):
        if ji + 1 < len(jobs):
            load_job(ji + 1)
        s = ji % 2
        kk, vk = f"kt{s}", f"vt{s}"
        maps = jobslot[ji]
        if kind == "A":
            scale, row0 = 64 ** -0.5, h * 64
        elif kind == "B":
            scale, row0 = 96 ** -0.5, 384 + h * 64
        else:
            scale, row0 = 32 ** -0.5, 768 + h * 64
        og = ostg[ji % 2]
        ogk = f"ostg{ji % 2}"
        for g in range(4):
            blocks = []
            if kind == "A":
                for ak in (0, -1, -2, -3, -4, 1, 2, 3):
                    mloc = 4 * g + ak
                    if mloc < 0:
                        continue
                    c0, c1 = a_cols(ak)
                    for r in range(4):
                        blocks.append((r, mloc, c0, c1, MAt[:, (ak + 4) * 4 + r, :], "MAt_hi" if ak >= 0 else "MAt_lo"))
            else:
                for mloc in range(4 * g + 4):
                    ak = mloc - 4 * g
                    for r in range(4):
                        if ak < 0:
                            blocks.append((r, mloc, 0, 512, None, None))
                        else:
                            blocks.append((r, mloc, ak * 128, 512, MBt[:, ak * 4 + r, :], "MBt"))
            steps = [(b, mi) for b in blocks for mi in range(len(maps))]
            O = [banks[6 + mi] for mi in range(len(maps))]
            Ok = [f"bank{6 + mi}" for mi in range(len(maps))]
            pend = []
            nst = len(steps)
            first = [True] * len(maps)
            lastidx = [max(i for i, (b, mi) in enumerate(steps) if mi == m_) for m_ in range(len(maps))]
            assert nst % 2 == 0
            npair = nst // 2
            for ip in range(npair + LAG):
                if ip == min(1, npair - 1) and deferred_a:
                    for f in deferred_a:
                        f()
                    deferred_a.clear()
                if ip == min(7, npair - 1) and deferred:
                    for f in deferred:
                        f()
                    deferred.clear()
                if ip < npair:
                    sb_ = cnt["s"] % NSB
                    cnt["s"] += 1
                    pi = cnt["p"] % NPT
                    cnt["p"] += 1
                    SP = K.pairs[sb_]
                    spk, ptk = f"sp{sb_}", f"pt{pi}"
                    c0, c1 = steps[2 * ip][0][2], steps[2 * ip][0][3]
                    assert (steps[2 * ip + 1][0][2], steps[2 * ip + 1][0][3]) == (c0, c1)
                    for hf in range(2):
                        (r, mloc, _, _, mask, mkey), mi = steps[2 * ip + hf]
                        qbuf, qk, qm = maps[mi]
                        qsrc = qbuf[:, g * 512 + c0:g * 512 + c1] if qm is None else qbuf[:, qm, g * 512 + c0:g * 512 + c1]
                        K.mm(SP[:, hf, c0:c1], kt[s][:, r, mloc * 128:(mloc + 1) * 128], qsrc, True, True, [kk, qk], [spk])
                    K.act(pt[pi][:, :, c0:c1], SP[:, :, c0:c1], AF.Exp, [spk], [ptk], scale=float(scale))
                    for hf in range(2):
                        (r, mloc, _, _, mask, mkey), mi = steps[2 * ip + hf]
                        if mask is not None:
                            K.tt("dve", pt[pi][:, hf, c0:c1], pt[pi][:, hf, c0:c1], mask[:, c0:c1], ALU.mult, [ptk, mkey], [ptk])
                    pend.append((ip, c0, c1, pi))
                if ip >= LAG:
                    (ip0, c0, c1, pi) = pend.pop(0)
                    for hf in range(2):
                        i0 = 2 * ip0 + hf
                        (r, mloc, _, _, mask, mkey), mi = steps[i0]
                        K.mm(O[mi][0:65, c0:c1], vt[s][:, r * NBLK + mloc, 0:65], pt[pi][:, hf, c0:c1], first[mi], i0 == lastidx[mi],
                             [vk, f"pt{pi}"], [Ok[mi]])
                        first[mi] = False
            gsl = slice(g * 512, (g + 1) * 512)

            def fbank():
                b = cnt["s"] % NSB
                cnt["s"] += 1
                return K.pairs[b][:, 0, :], f"sp{b}"

            def prec(i):
                K.recip(rz[i][64:65, :], osb[i][64:65, :], [f"osb{i}"], [f"rz{i}"])

            if kind in ("A", "B"):
                fi = cnt["f"] % 4
                cnt["f"] += 1
                K.copy("act", osb[fi][0:65, :], O[0][0:65, :], [Ok[0]], [f"osb{fi}"])

                def part2a(fi=fi):
                    K.recip(rz[fi][64:65, :], osb[fi][64:65, :], [f"osb{fi}"], [f"rz{fi}"])

                def part2(fi=fi, og=og, ogk=ogk, gsl=gsl):
                    fb, fk = fbank()
                    K.mm(fb[0:64, :], C["one"][64:65, 0:64], rz[fi][64:65, :], True, True, [f"rz{fi}", "one"], [fk])
                    K.tt("dve", og[0:64, gsl], osb[fi][0:64, :], fb[0:64, :], ALU.mult, [f"osb{fi}", fk], [ogk])
            else:
                f0 = cnt["f"] % 4
                f1 = (cnt["f"] + 1) % 4
                cnt["f"] += 2
                K.copy("act", osb[f0][0:65, :], O[0][0:65, :], [Ok[0]], [f"osb{f0}"])
                K.copy("act", osb[f1][0:65, :], O[1][0:65, :], [Ok[1]], [f"osb{f1}"])

                def part2a(f0=f0, f1=f1):
                    K.recip(rz[f0][64:65, :], osb[f0][64:65, :], [f"osb{f0}"], [f"rz{f0}"])
                    K.recip(rz[f1][64:65, :], osb[f1][64:65, :], [f"osb{f1}"], [f"rz{f1}"])

                def part2(f0=f0, f1=f1, og=og, ogk=ogk, gsl=gsl):
                    fb, fk = fbank()
                    K.mm(fb[0:64, :], C["one"][64:65, 0:64], rz[f0][64:65, :], True, True, [f"rz{f0}", "one"], [fk])
                    K.tt("dve", tA[0:64, :], osb[f0][0:64, :], fb[0:64, :], ALU.mult, [f"osb{f0}", fk], ["tA"])
                    fb, fk = fbank()
                    K.mm(fb[0:64, :], C["one"][64:65, 0:64], rz[f1][64:65, :], True, True, [f"rz{f1}", "one"], [fk])
                    K.tt("dve", tB[0:64, :], osb[f1][0:64, :], fb[0:64, :], ALU.mult, [f"osb{f1}", fk], ["tB"])
                    K.stt("dve", tO[0:64, :], tB[0:64, :], neglam[0:64, 0:1], tA[0:64, :], ALU.mult, ALU.add, ["tA", "tB", "neglam"], ["tO"])
                    K.tt("pool", tS[0:64, :], tO[0:64, :], tO[0:64, :], ALU.mult, ["tO"], ["tS"])
                    fb, fk = fbank()
                    K.mm(fb[0:64, :], C["c64"][0:64, 0:64], tS[0:64, :], True, True, ["tS", "c64"], [fk])
                    K.act(tS[0:64, :], fb[0:64, :], AF.Sqrt, [fk, "eps5"], ["tS"], bias=C["eps5"][0:64, 0:1], scale=1.0)
                    K.recip(tS[0:64, :], tS[0:64, :], ["tS"], ["tS"])
                    K.stt("dve", og[0:64, gsl], tO[0:64, :], gc[0:64, 0:1], tS[0:64, :], ALU.mult, ALU.mult, ["tO", "tS", "gc"], [ogk])
            deferred_a.append(part2a)
            deferred.append(part2)
        deferred.append(lambda og=og, ogk=ogk, row0=row0: out_ops.append(
            K.dma("sp", ogk, io["mixT"][row0:row0 + 64, :], og[0:64, :], [ogk], [("mixT", row0)])))
    for f in deferred_a + deferred:
        f()
    deferred.clear()
    return out_ops


def simple_acc(io):
    def acc(kind, h):
        if kind == "A":
            return dict(rows=128, q=io["QA"][h * 64:(h + 1) * 64, :], k=lambda r: io["KAg"][r, (h // 2) * 128:(h // 2 + 1) * 128, :],
                        v=lambda r: io["VAg"][r, h], kdeps=[], vdeps=[])
        if kind == "B":
            return dict(rows=96, q=io["QB"][h], k=lambda r: io["KBg"][r, h], v=lambda r: io["VBg"][r, h], kdeps=[], vdeps=[])
        return dict(rows=128, q=io["QC"][h * 64:(h + 1) * 64, :], k=lambda r: io["KCg"][r, (h // 2) * 128:(h // 2 + 1) * 128, :],
                    v=lambda r: io["VCg"][r, h], kdeps=[], vdeps=[])
    return acc


def build_p2():
    nc = bass.Bass("TRN2", target_bir_lowering=False)
    with contextlib.ExitStack() as st:
        K = Ctx(nc, st)
        io = {}
        io["vecs"] = K.dram("vecs", [128, NVEC], F32, "ExternalInput")
        io["MA"] = K.dram("MA", [128, 32, 512], BF16, "ExternalInput")
        io["MB"] = K.dram("MB", [128, 16, 512], BF16, "ExternalInput")
        io["QA"] = K.dram("QA", [384, TL], BF16, "ExternalInput")
        io["QB"] = K.dram("QB", [6, 96, TL], BF16, "ExternalInput")
        io["QC"] = K.dram("QC", [256, TL], BF16, "ExternalInput")
        io["KAg"] = K.dram("KAg", [4, 384, TL], BF16, "ExternalInput")
        io["KBg"] = K.dram("KBg", [4, 6, 96, TL], BF16, "ExternalInput")
        io["KCg"] = K.dram("KCg", [4, 256, TL], BF16, "ExternalInput")
        io["VAg"] = K.dram("VAg", [4, 6, 128, NBLK, 65], BF16, "ExternalInput")
        io["VBg"] = K.dram("VBg", [4, 6, 128, NBLK, 65], BF16, "ExternalInput")
        io["VCg"] = K.dram("VCg", [4, 4, 128, NBLK, 65], BF16, "ExternalInput")
        io["mixT"] = K.dram("mixT", [1024, TL], BF16, "ExternalOutput")
        C = make_consts(K)
        io["acc"] = simple_acc(io)
        outs = phase2(K, C, io)
        K.P.emit(final_wait_ops=outs)
    return nc


def attn_masks(j):
    tk = np.arange(128)[:, None, None]
    aq = np.arange(4)[None, :, None]
    tq = np.arange(128)[None, None, :]
    MA = np.zeros((128, 32, 4, 128), np.float32)
    MB = np.zeros((128, 16, 4, 128), np.float32)
    for ak in range(-4, 4):
        for r in range(4):
            dist = (4 * (aq - ak) + j - r) * 128 + tq - tk
            cnt = ((dist >= 0) & (dist <= 128)).astype(np.float32)
            cnt += ((dist >= 0) & (dist <= 512) & (dist % 4 == 0))
            cnt += ((dist >= 0) & (dist <= 2048) & (dist % 16 == 0))
            MA[:, (ak + 4) * 4 + r] = cnt
            if ak >= 0:
                MB[:, ak * 4 + r] = (dist >= 0)
    return (MA.reshape(128, 32, 512).astype(ml_dtypes.bfloat16), MB.reshape(128, 16, 512).astype(ml_dtypes.bfloat16))


def phase3(K, C, io, last):
    P = K.P
    banks = K.banks
    vec = K.sb("vec", [128, NVEC], F32)
    K.dma("sp", "vec", vec[:], io["vecs"], [], ["vec"])
    xT = K.sb("xT", [128, 8, TL], F32)
    mx = K.sb("mx", [128, 8, TL], BF16)
    aT = [K.sb(f"aT{i}", [128, 4, TL], BF16) for i in range(2)]
    wsl = [K.sb(f"wsl{i}", [128, 4096], BF16) for i in range(2)]
    rl = [K.sb(f"rl{i}", [128, 512], F32) for i in range(2)]
    sq = [K.sb(f"sq{i}", [128, 512], F32) for i in range(2)]
    sqk = ["sq0", "sq1"]
    rstd = [K.sb(f"rstd{i}", [128, 512], F32) for i in range(2)]
    mixv = io["mixT"].rearrange("(c p) t -> p c t", p=128)
    for t in range(4):
        sl = slice(t * 512, (t + 1) * 512)
        K.dma("sp", f"mx{t}", mx[:, :, sl], mixv[:, :, sl], [], [("mx", t)])
        K.dma("sp", f"xT{t}", xT[:, :, sl], io["xT"][:, :, sl], [], [("xT", t)])
    items = [(io["Wo"][:, :, hf * 512:(hf + 1) * 512], [128, 8, 512]) for hf in range(2)]
    for fg in range(8):
        items.append((io["Wup"][:, :, fg * 512:(fg + 1) * 512], [128, 8, 512]))
        items.append((io["Wdown"][:, fg * 4:(fg + 1) * 4, :], [128, 4, 1024]))
    ws = WStream(K, wsl, ["wsl0", "wsl1"], items)
    pc = [0]

    def nbank():
        b = pc[0] % 4
        pc[0] += 1
        return banks[b], f"bank{b}"

    for hf in range(2):
        w, wk = ws.next()
        for dl in range(4):
            d = hf * 4 + dl
            for t in range(4):
                sl = slice(t * 512, (t + 1) * 512)
                ps, pk = nbank()
                for c in range(8):
                    K.mm(ps[:, :], w[:, c, dl * 128:(dl + 1) * 128], mx[:, c, sl], c == 0, c == 7, [wk, ("mx", t)], [pk])
                K.tt("dve", xT[:, d, sl], ps[:, :], xT[:, d, sl], ALU.add, [pk, ("xT", t)], [("xT", t)])
    for t in range(4):
        sl = slice(t * 512, (t + 1) * 512)
        rs, rk = rstd[t % 2], f"rstd{t % 2}"
        rms_stats(K, C, [xT[:, c, sl] for c in range(8)], [("xT", t)] * 8, 128, "c1024", "eps6", banks[7], "bank7", rs[:], rk, sq, sqk)
        for c in range(8):
            K.stt("dve", mx[:, c, sl], xT[:, c, sl], vec[:, 8 + c:9 + c], rs[:], ALU.mult, ALU.mult, [("xT", t), rk, "vec"], [("mx", t)])
    rc = 0
    for fg in range(8):
        wu, wuk = ws.next()
        a = aT[fg % 2]
        ak = f"aT{fg % 2}"
        for fl in range(4):
            for t in range(4):
                sl = slice(t * 512, (t + 1) * 512)
                ps, pk = nbank()
                for c in range(8):
                    K.mm(ps[:, :], wu[:, c, fl * 128:(fl + 1) * 128], mx[:, c, sl], c == 0, c == 7, [wuk, ("mx", t)], [pk])
                ri = rc % 2
                rc += 1
                K.act(rl[ri][:, :], ps[:, :], AF.Relu, [pk], [f"rl{ri}"])
                K.tt("pool", a[:, fl, sl], rl[ri][:, :], rl[ri][:, :], ALU.mult, [f"rl{ri}"], [(ak, t)])
        wd, wdk = ws.next()
        for d in range(8):
            for t in range(4):
                sl = slice(t * 512, (t + 1) * 512)
                ps, pk = nbank()
                for fl in range(4):
                    K.mm(ps[:, :], wd[:, fl, d * 128:(d + 1) * 128], a[:, fl, sl], fl == 0, fl == 3, [wdk, (ak, t)], [pk])
                K.tt("dve", xT[:, d, sl], ps[:, :], xT[:, d, sl], ALU.add, [pk, ("xT", t)], [("xT", t)])
    outs = []
    if last:
        for t in range(4):
            sl = slice(t * 512, (t + 1) * 512)
            rs, rk = rstd[t % 2], f"rstd{t % 2}"
            rms_stats(K, C, [xT[:, c, sl] for c in range(8)], [("xT", t)] * 8, 128, "c1024", "eps6", banks[7], "bank7", rs[:], rk, sq, sqk)
            for c in range(8):
                K.stt("dve", xT[:, c, sl], xT[:, c, sl], vec[:, 16 + c:17 + c], rs[:], ALU.mult, ALU.mult, [("xT", t), rk, "vec"], [("xT", t)])
    for t in range(4):
        sl = slice(t * 512, (t + 1) * 512)
        outs.append(K.dma("sp", "xTo", io["xout"][:, :, sl], xT[:, :, sl], [("xT", t)], [("xout", t)]))
    return outs


def build_p3(last):
    nc = bass.Bass("TRN2", target_bir_lowering=False)
    with contextlib.ExitStack() as st:
        K = Ctx(nc, st)
        io = {}
        io["vecs"] = K.dram("vecs", [128, NVEC], F32, "ExternalInput")
        io["xT"] = K.dram("xT", [128, 8, TL], F32, "ExternalInput")
        io["mixT"] = K.dram("mixT", [1024, TL], BF16, "ExternalInput")
        io["Wo"] = K.dram("Wo", [128, 8, 1024], F32, "ExternalInput")
        io["Wup"] = K.dram("Wup", [128, 8, DFF], F32, "ExternalInput")
        io["Wdown"] = K.dram("Wdown", [128, 32, 1024], F32, "ExternalInput")
        io["xout"] = K.dram("xout", [128, 8, TL], F32, "ExternalOutput")
        C = make_consts(K)
        outs = phase3(K, C, io, last)
        K.P.emit(final_wait_ops=outs)
    return nc


_PROGS = {}


def _prog(name):
    if name not in _PROGS:
        _PROGS[name] = dict(p1=build_p1, p2=build_p2, p3a=lambda: build_p3(False), p3b=lambda: build_p3(True))[name]()
    return _PROGS[name]


def _run(nc, in_maps):
    res = run_bass_kernel_spmd(nc, in_maps, core_ids=list(range(NCORE)))
    return res.results


def kernel_unfused(**inputs):
    x = np.asarray(inputs["x"], dtype=np.float32)
    xs = [x_to_core(x, c) for c in range(NCORE)]
    ropes = [rope_tables(j) for j in range(4)]
    masks = [attn_masks(j) for j in range(4)]
    for l in range(2):
        lw = prep_layer(inputs, l)
        r1 = _run(_prog("p1"), [dict(xT=xs[c], W1=lw["W1"], Wuq=lw["Wuq"], Wukv=lw["Wukv"], vecs=lw["vecs"], rope=ropes[c % 4])
                                for c in range(NCORE)])
        in2 = []
        gath = {}
        for b in range(2):
            for k in ("KA", "KB", "KC", "VA", "VB", "VC"):
                gath[(b, k)] = np.stack([np.asarray(r1[4 * b + j][k]) for j in range(4)], 0)
        for c in range(NCORE):
            b, j = c // 4, c % 4
            in2.append(dict(vecs=lw["vecs"], MA=masks[j][0], MB=masks[j][1],
                            QA=np.asarray(r1[c]["QA"]), QB=np.asarray(r1[c]["QB"]), QC=np.asarray(r1[c]["QC"]),
                            KAg=gath[(b, "KA")], KBg=gath[(b, "KB")], KCg=gath[(b, "KC")],
                            VAg=gath[(b, "VA")], VBg=gath[(b, "VB")], VCg=gath[(b, "VC")]))
        r2 = _run(_prog("p2"), in2)
        r3 = _run(_prog("p3b" if l == 1 else "p3a"),
                  [dict(vecs=lw["vecs"], xT=xs[c], mixT=np.asarray(r2[c]["mixT"]), Wo=lw["Wo"], Wup=lw["Wup"], Wdown=lw["Wdown"])
                   for c in range(NCORE)])
        xs = [np.asarray(r3[c]["xout"]) for c in range(NCORE)]
    out = np.empty((2, SEQ, D), np.float32)
    for c in range(NCORE):
        b, j = c // 4, c % 4
        xt = xs[c].transpose(1, 0, 2).reshape(D, TL)
        out[b, core_positions(j), :] = xt.T
    return out


KROWS = 384 + 576 + 256
RG = [[0, 1, 2, 3], [4, 5, 6, 7]]


def _phase(nc, tag, body, ext_sems=None):
    with nc.cleanup_on_exit():
        with contextlib.ExitStack() as st:
            K = Ctx(nc, st, tag, ext_sems)
            C = make_consts(K)
            outs = body(K, C)
            K.P.emit(final_wait_ops=outs, tag=tag, managed=False)
        nc.all_engine_barrier()


def build_fused():
    nc = bass.Bass("TRN2", target_bir_lowering=False, num_devices=NCORE)
    ext = lambda name, shape, dt: nc.dram_tensor(name, shape, dt, kind="ExternalInput").ap()
    internal = lambda name, shape, dt: nc.dram_tensor(name, shape, dt, kind="Internal").ap()
    xT_in = ext("xT", [128, 8, TL], F32)
    rope = ext("rope", [4, 128, TL], F32)
    MA = ext("MA", [128, 32, 512], BF16)
    MB = ext("MB", [128, 16, 512], BF16)
    xout = nc.dram_tensor("xout", [128, 8, TL], F32, kind="ExternalOutput").ap()
    x1T = internal("x1T", [128, 8, TL], F32)
    agsem = {g: nc.alloc_semaphore(name="agsem" + g) for g in "ABC"}
    ag_val = {g: 0 for g in "ABC"}
    for l in range(2):
        W = dict(W1=ext(f"W1_{l}", [128, 8, W1COLS], F32), Wuq=ext(f"Wuq_{l}", [128, 3, 1152], F32),
                 Wukv=ext(f"Wukv_{l}", [128, 768], F32), vecs=ext(f"vecs_{l}", [128, NVEC], F32),
                 Wo=ext(f"Wo_{l}", [128, 8, 1024], F32), Wup=ext(f"Wup_{l}", [128, 8, DFF], F32),
                 Wdown=ext(f"Wdown_{l}", [128, 32, 1024], F32))
        kloc2 = internal(f"kloc{l}", [KROWS * 8, 256], BF16)
        vloc2 = internal(f"vloc{l}", [16 * 520, 256], BF16)
        kloc = kloc2.rearrange("(r a) b -> r (a b)", a=8)
        QA = internal(f"QA{l}", [384, TL], BF16)
        QB = internal(f"QB{l}", [6, 96, TL], BF16)
        QC = internal(f"QC{l}", [256, TL], BF16)
        mixT = internal(f"mixT{l}", [1024, TL], BF16)
        x_cur = xT_in if l == 0 else x1T
        x_next = x1T if l == 0 else xout
        vl = vloc2.rearrange("(h x) b -> h (x b)", h=16).rearrange("h (p m e) -> h p m e", p=128, m=NBLK)
        io1 = dict(xT=x_cur, W1=W["W1"], Wuq=W["Wuq"], Wukv=W["Wukv"], vecs=W["vecs"], rope=[rope[i] for i in range(4)],
                   QA=QA, QB=QB, QC=QC, KA=kloc[0:384, :], KB=kloc[384:960, :].rearrange("(h d) t -> h d t", h=6),
                   KC=kloc[960:KROWS, :], VA=vl[0:6], VB=vl[6:12], VC=vl[12:16])

        pieces = []
        kp = {}
        for nm, r0, nrows, npc in (("KA", 0, 128, 3), ("KB", 384, 192, 3), ("KC", 960, 128, 2)):
            for p in range(npc):
                key = f"g{nm}{p}"
                g = internal(f"{key}_{l}", [4 * nrows * 8, 256], BF16)
                if nm == "KB":
                    deps = [("KB", 2 * p), ("KBr", 2 * p), ("KB", 2 * p + 1), ("KBr", 2 * p + 1)]
                else:
                    deps = [(nm, p * 128)]
                pieces.append((key, kloc2[(r0 + p * nrows) * 8:(r0 + (p + 1) * nrows) * 8, :], g, deps))
                kp[key] = g.rearrange("(r q a) b -> r q (a b)", r=4, a=8)
        for nm, h0, nh, npc in (("VA", 0, 3, 2), ("VB", 6, 3, 2), ("VC", 12, 2, 2)):
            for p in range(npc):
                key = f"g{nm}{p}"
                g = internal(f"{key}_{l}", [4 * nh * 520, 256], BF16)
                deps = [(nm, p * nh + i) for i in range(nh)]
                pieces.append((key, vloc2[(h0 + p * nh) * 520:(h0 + (p + 1) * nh) * 520, :], g, deps))
                kp[key] = g.rearrange("(r h x) b -> r h (x b)", r=4, h=nh).rearrange("r h (p m e) -> r h p m e", p=128, m=NBLK)
        io1["ag_pieces"] = pieces
        io1["ag_ops"] = {}

        def body1(K, C, io1=io1):
            phase1(K, C, io1)
            return [o for o in K.P.ops["sp"] if o.is_dma and o.dma_sem.startswith(("stg", "vst", "krT_out"))]

        _phase(nc, f"L{l}a_", body1, ext_sems={"ag" + g: (agsem[g], ag_val[g]) for g in "ABC"})
        assert len(io1["ag_ops"]) == len(pieces), (len(io1["ag_ops"]), len(pieces))
        for g in "ABC":
            ag_val[g] = max(op.dma_cnt for key, op in io1["ag_ops"].items() if key[2] == g)
        ag_done = {key: ag_val[key[2]] for key in io1["ag_ops"]}

        def acc(kind, h, kp=kp, QA=QA, QB=QB, QC=QC):
            if kind == "A":
                kk, vk = f"gKA{h // 2}", f"gVA{h // 3}"
                return dict(rows=128, q=QA[h * 64:(h + 1) * 64, :], k=lambda r: kp[kk][r, :, :],
                            v=lambda r: kp[vk][r, h % 3], kdeps=[kk], vdeps=[vk])
            if kind == "B":
                kk, vk = f"gKB{h // 2}", f"gVB{h // 3}"
                return dict(rows=96, q=QB[h], k=lambda r: kp[kk][r, (h % 2) * 96:(h % 2 + 1) * 96, :],
                            v=lambda r: kp[vk][r, h % 3], kdeps=[kk], vdeps=[vk])
            kk, vk = f"gKC{h // 2}", f"gVC{h // 2}"
            return dict(rows=128, q=QC[h * 64:(h + 1) * 64, :], k=lambda r: kp[kk][r, :, :],
                        v=lambda r: kp[vk][r, h % 2], kdeps=[kk], vdeps=[vk])

        io2 = dict(vecs=W["vecs"], MA=MA, MB=MB, mixT=mixT, acc=acc)

        def body2(K, C, io2=io2, ag_done=ag_done):
            for key, val in ag_done.items():
                K.P.external("ag" + key[2], val, [key])
            return phase2(K, C, io2)

        _phase(nc, f"L{l}b_", body2, ext_sems={"ag" + g: (agsem[g], ag_val[g]) for g in "ABC"})
        io3 = dict(vecs=W["vecs"], xT=x_cur, mixT=mixT, Wo=W["Wo"], Wup=W["Wup"], Wdown=W["Wdown"], xout=x_next)
        _phase(nc, f"L{l}c_", lambda K, C, io3=io3, l=l: phase3(K, C, io3, l == 1))
    return nc


def kernel(**inputs):
    x = np.asarray(inputs["x"], dtype=np.float32)
    if "fused" not in _PROGS:
        _PROGS["fused"] = build_fused()
    lws = [prep_layer(inputs, l) for l in range(2)]
    ropes = [rope_tables(j) for j in range(4)]
    masks = [attn_masks(j) for j in range(4)]
    in_maps = []
    for c in range(NCORE):
        j = c % 4
        m = dict(xT=x_to_core(x, c), rope=ropes[j], MA=masks[j][0], MB=masks[j][1])
        for l in range(2):
            for k in ("W1", "Wuq", "Wukv", "vecs", "Wo", "Wup", "Wdown"):
                m[f"{k}_{l}"] = lws[l][k]
        in_maps.append(m)
    res = _run(_PROGS["fused"], in_maps)
    out = np.empty((2, SEQ, D), np.float32)
    for c in range(NCORE):
        b, j = c // 4, c % 4
        xt = np.asarray(res[c]["xout"]).transpose(1, 0, 2).reshape(D, TL)
        out[b, core_positions(j), :] = xt.T
    return out
```

```python
import contextlib
import math
import numpy as np
import ml_dtypes
import concourse.bass as bass
import concourse.mybir as mybir
from concourse.bass_utils import run_bass_kernel_spmd

F32 = mybir.dt.float32
BF16 = mybir.dt.bfloat16
ALU = mybir.AluOpType
AF = mybir.ActivationFunctionType

D = 1024
SEQ = 8192
NCORE = 8
TL = 2048
NBLK = 16
DFF = 4096
OFF = dict(qa=0, ka=384, va=768, cq=1152, ckv=1536, kr=1664, qc=1696, kc=1952, vc=2208)
W1COLS = 3904
NVEC = 160

import os
PROBE_K128 = bool(os.environ.get("PROBE_K128"))
COMPUTE = ("pe", "act", "dve", "pool")
SEM_WRAP = 30000


class Op:
    __slots__ = ("eng", "fn", "deps", "idx", "sig", "dma_sem", "dma_cnt", "is_dma", "sigidx")

    def __init__(self, eng, fn, is_dma=False):
        self.eng = eng
        self.fn = fn
        self.deps = []
        self.sig = False
        self.is_dma = is_dma
        self.dma_sem = None
        self.dma_cnt = 0
        self.sigidx = -1
        self.idx = -1


class Prog:
    def __init__(self, nc, ext_sems=None):
        self.nc = nc
        self.ops = {e: [] for e in ("pe", "act", "dve", "pool", "sp")}
        self.lastw = {}
        self.readers = {}
        self.dma_sems = {}
        self.ext_sems = dict(ext_sems or {})
        for name, (h, base) in self.ext_sems.items():
            self.dma_sems[name] = base

    def external(self, semname, value, writes):
        o = Op("pool", None, is_dma=True)
        o.dma_sem = semname
        o.dma_cnt = value
        for k in writes:
            self.lastw[k] = o
            self.readers[k] = []
        return o

    def _add(self, op, reads, writes):
        deps = []
        for k in reads:
            w = self.lastw.get(k)
            if w is not None:
                deps.append(w)
        for k in writes:
            w = self.lastw.get(k)
            if w is not None:
                deps.append(w)
            deps.extend(self.readers.get(k, ()))
        seen = set()
        for d in deps:
            if d is op or id(d) in seen:
                continue
            seen.add(id(d))
            op.deps.append(d)
        for k in reads:
            lst = self.readers.setdefault(k, [])
            if not op.is_dma:
                for i, r in enumerate(lst):
                    if (not r.is_dma) and r.eng == op.eng:
                        lst[i] = op
                        break
                else:
                    lst.append(op)
            else:
                lst.append(op)
        for k in writes:
            self.lastw[k] = op
            self.readers[k] = []
        op.idx = len(self.ops[op.eng])
        self.ops[op.eng].append(op)
        return op

    def op(self, eng, fn, reads=(), writes=()):
        return self._add(Op(eng, fn), reads, writes)

    def dma(self, eng, semname, fn, reads=(), writes=(), inc=16):
        o = Op(eng, fn, is_dma=True)
        self.dma_sems[semname] = self.dma_sems.get(semname, 0) + inc
        o.dma_sem = semname
        o.dma_cnt = self.dma_sems[semname]
        o.sigidx = inc
        return self._add(o, reads, writes)

    @staticmethod
    def _skip_same(d, o):
        return (not d.is_dma) and d.eng == o.eng and (d.eng == "pe" or o.idx - d.idx > 2)

    def emit(self, final_wait_ops=(), tag="", managed=True):
        nc = self.nc
        for e, lst in self.ops.items():
            for o in lst:
                for d in o.deps:
                    if d.is_dma or self._skip_same(d, o):
                        continue
                    d.sig = True
        for o in final_wait_ops:
            if not o.is_dma:
                o.sig = True
        nsig = {}
        for e in COMPUTE:
            c = 0
            for o in self.ops[e]:
                if o.sig and not o.is_dma:
                    o.sigidx = c
                    c += 1
            nsig[e] = c
        with contextlib.ExitStack() as st:
            if managed:
                mksem = lambda nm: st.enter_context(nc.semaphore(tag + nm))
            else:
                mksem = lambda nm: nc.alloc_semaphore(name=tag + nm)
            csem = {}
            for e in COMPUTE:
                n = max(1, (nsig[e] + SEM_WRAP - 1) // SEM_WRAP)
                csem[e] = [mksem(f"c_{e}_{i}") for i in range(n)]
            dsem = {name: (self.ext_sems[name][0] if name in self.ext_sems else mksem(f"d_{name}")) for name in self.dma_sems}
            block = st.enter_context(nc.Block())
            engobj = {"pe": "tensor", "act": "scalar", "dve": "vector", "pool": "gpsimd", "sp": "sync"}

            def run_engine(e, eng):
                waited = {}
                for o in self.ops[e]:
                    need = {}
                    for d in o.deps:
                        if d.is_dma:
                            key = ("d", d.dma_sem)
                            val = d.dma_cnt
                            sem = dsem[d.dma_sem]
                        else:
                            if self._skip_same(d, o):
                                continue
                            si = d.sigidx // SEM_WRAP
                            key = ("c", d.eng, si)
                            val = d.sigidx % SEM_WRAP + 1
                            sem = csem[d.eng][si]
                        if waited.get(key, 0) >= val:
                            continue
                        if key not in need or need[key][1] < val:
                            need[key] = (sem, val)
                    for key, (sem, val) in need.items():
                        eng.wait_ge(sem, val)
                        waited[key] = val
                    ins = o.fn(eng)
                    if o.is_dma:
                        ins.then_inc(dsem[o.dma_sem], o.sigidx)
                    elif o.sig:
                        ins.then_inc(csem[e][o.sigidx // SEM_WRAP], 1)
                if e == "sp":
                    fin = {}
                    for o in final_wait_ops:
                        if o.is_dma:
                            key, sem, val = ("d", o.dma_sem), dsem[o.dma_sem], o.dma_cnt
                        else:
                            si = o.sigidx // SEM_WRAP
                            key, sem, val = ("c", o.eng, si), csem[o.eng][si], o.sigidx % SEM_WRAP + 1
                        if key not in fin or fin[key][1] < val:
                            fin[key] = (sem, val)
                    for sem, val in fin.values():
                        eng.wait_ge(sem, val)

            for e in ("sp", "pool", "act", "dve", "pe"):
                getattr(block, engobj[e])(lambda eng, e=e: run_engine(e, eng))


class Ctx:
    def __init__(self, nc, st, tag="", ext_sems=None):
        self.nc = nc
        self.st = st
        self.tag = tag
        self.P = Prog(nc, ext_sems)
        self.pairs = [st.enter_context(nc.psum_tensor(f"{tag}pbank{i}", [128, 2, 512], F32)) for i in range(4)]
        self.banks = [self.pairs[i // 2][:, i % 2, :] for i in range(8)]
        self.uid = 0

    def sb(self, name, shape, dt):
        return self.st.enter_context(self.nc.sbuf_tensor(self.tag + "s_" + name, shape, dt))

    def dram(self, name, shape, dt, kind):
        return self.nc.dram_tensor(name, shape, dt, kind=kind).ap()

    def mm(self, out, lhsT, rhs, start, stop, reads, writes, **kw):
        return self.P.op("pe", lambda e: e.matmul(out, lhsT, rhs, start=start, stop=stop, **kw), reads, writes)

    def act(self, out, in_, func, reads, writes, bias=None, scale=None):
        kw = {}
        if bias is not None:
            kw["bias"] = bias
        if scale is not None:
            kw["scale"] = scale
        return self.P.op("act", lambda e: e.activation(out=out, in_=in_, func=func, **kw), reads, writes)

    def tt(self, eng, out, in0, in1, op, reads, writes):
        return self.P.op(eng, lambda e: e.tensor_tensor(out=out, in0=in0, in1=in1, op=op), reads, writes)

    def ts(self, eng, out, in0, s1, op0, reads, writes, s2=None, op1=None):
        if op1 is None:
            return self.P.op(eng, lambda e: e.tensor_scalar(out=out, in0=in0, scalar1=s1, scalar2=None, op0=op0), reads, writes)
        return self.P.op(eng, lambda e: e.tensor_scalar(out=out, in0=in0, scalar1=s1, scalar2=s2, op0=op0, op1=op1), reads, writes)

    def stt(self, eng, out, in0, scalar, in1, op0, op1, reads, writes):
        return self.P.op(eng, lambda e: e.scalar_tensor_tensor(out=out, in0=in0, scalar=scalar, in1=in1, op0=op0, op1=op1), reads, writes)

    def copy(self, eng, out, in_, reads, writes):
        if eng == "act":
            return self.P.op("act", lambda e: e.copy(out=out, in_=in_), reads, writes)
        return self.P.op(eng, lambda e: e.tensor_copy(out=out, in_=in_), reads, writes)

    def recip(self, out, in_, reads, writes):
        return self.P.op("dve", lambda e: e.reciprocal(out=out, in_=in_), reads, writes)

    def memset(self, eng, ap, val, writes):
        return self.P.op(eng, lambda e: e.memset(ap, val), (), writes)

    def dma(self, q, sem, out, in_, reads, writes):
        return self.P.dma(q, sem, lambda e: e.dma_start(out=out, in_=in_), reads, writes)


class WStream:
    def __init__(self, K, slots, keys, items):
        self.K, self.slots, self.keys, self.items = K, slots, keys, items
        self.issued = 0
        self.cur = 0

    def _issue(self):
        i = self.issued
        if i >= len(self.items):
            return
        src, shape = self.items[i]
        n = 1
        for d in shape[1:]:
            n *= d
        ns = len(self.slots)
        dst = self.slots[i % ns][:, 0:n]
        if len(shape) == 3:
            dst = dst.rearrange("p (a b) -> p a b", a=shape[1])
        key = self.keys[i % ns]
        self.K.P.dma("pool", key, lambda e: e.dma_start(out=dst, in_=src), [], [key])
        self.views = getattr(self, "views", {})
        self.views[i] = (dst, key)
        self.issued += 1

    def next(self):
        while self.issued <= min(self.cur + 1, len(self.items) - 1):
            self._issue()
        v = self.views[self.cur]
        self.cur += 1
        return v


def make_consts(K):
    c = {}
    for name, val in (("c1024", 1.0 / 1024), ("c384", 1.0 / 384), ("c128", 1.0 / 128), ("c64", 1.0 / 64), ("one", 1.0)):
        t = K.sb(name, [128, 128], F32)
        K.memset("pool", t[:], val, [name])
        c[name] = t
    for name, val in (("eps6", 1e-6), ("eps5", 1e-5)):
        t = K.sb(name, [128, 1], F32)
        K.memset("pool", t[:], val, [name])
        c[name] = t
    return c


def rms_stats(K, C, srcs, src_keys, rows, cname, epsname, bank, bank_key, rstd, rstd_key, sq, sqkeys):
    n = len(srcs)
    for i, s in enumerate(srcs):
        q = sq[i % 2]
        K.act(q[0:rows, :], s, AF.Square, [src_keys[i]], [sqkeys[i % 2]])
        K.mm(bank[:, :], C[cname][0:rows, :], q[0:rows, :], i == 0, i == n - 1, [sqkeys[i % 2], cname], [bank_key])
    K.act(rstd, bank[:, :], AF.Sqrt, [bank_key, epsname], [rstd_key], bias=C[epsname][:, 0:1], scale=1.0)
    K.recip(rstd, rstd, [rstd_key], [rstd_key])


FM_GROUPS = [
    (0, 512, [("ropeA", 0, 128, "QA", 0), ("ropeA", 256, 128, "QA", 128)]),
    (512, 512, [("ropeA", 0, 128, "QA", 256), ("ropeA", 256, 128, "KA", 0)]),
    (1024, 512, [("ropeA", 0, 128, "KA", 128), ("ropeA", 256, 128, "KA", 256)]),
    (1536, 512, [("ropeC", 0, 128, "QC", 0), ("ropeC", 256, 128, "QC", 128)]),
    (2048, 512, [("ropeC", 0, 128, "KC", 0), ("ropeC", 256, 128, "KC", 128)]),
    (2560, 512, [("cq", 0, 128, None, 0), ("cq", 128, 128, None, 1), ("cq", 256, 128, None, 2), ("ckv", 384, 128, None, 0)]),
    (3072, 192, [("kr", 0, 96, None, 0)]),
]
TM_GROUPS = [(3264, 384, 6, "VA"), (3648, 256, 4, "VC")]


def phase1(K, C, io):
    P = K.P
    banks = K.banks
    xT = io["xT"]
    W1 = io["W1"]
    vec = K.sb("vec", [128, NVEC], F32)
    K.dma("sp", "vec", vec[:], io["vecs"], [], ["vec"])
    xs = [K.sb("xs0", [128, 8, 512], F32)] * 2
    hT = K.sb("hT", [128, 8, TL], BF16)
    rstd = [K.sb(f"rstd{i}", [128, 512], F32) for i in range(2)]
    sq = [K.sb(f"sq{i}", [128, 512], F32) for i in range(2)]
    sqk = ["sq0", "sq1"]
    wsl = [K.sb(f"wsl{i}", [128, 4096], BF16) for i in range(2)]
    t64c = K.sb("t64c", [128, TL], F32)
    t64s = K.sb("t64s", [128, TL], F32)
    t32c = K.sb("t32c", [128, TL], F32)
    t32s = K.sb("t32s", [128, TL], F32)
    NSTG = 6
    stg = [K.sb(f"stg{i}", [128, TL], BF16) for i in range(NSTG)]
    r1 = [K.sb(f"r1_{i}", [128, 512], F32) for i in range(2)]
    r2 = [K.sb(f"r2_{i}", [128, 512], F32) for i in range(2)]
    cqT = K.sb("cqT", [128, 3, TL], F32)
    ckvT = K.sb("ckvT", [128, TL], F32)
    cqn = K.sb("cqn", [128, 3, TL], BF16)
    ckvn = K.sb("ckvn", [128, TL], BF16)
    krT = K.sb("krT", [128, TL], BF16)
    vst = [K.sb("vst0", [128, 6, NBLK, 65], BF16)] * 2
    K.memset("pool", vst[0][:], 1.0, ["vst0"])

    for t in range(4):
        x = xs[t % 2]
        xk = "xs0"
        K.dma("sp", xk, x[:], xT[:, :, t * 512:(t + 1) * 512], [], [xk])
        rs = rstd[t % 2]
        rk = f"rstd{t % 2}"
        rms_stats(K, C, [x[:, c, :] for c in range(8)], [xk] * 8, 128, "c1024", "eps6", banks[7], "bank7", rs[:], rk, sq, sqk)
        for c in range(8):
            K.stt("dve", hT[:, c, t * 512:(t + 1) * 512], x[:, c, :], vec[:, c:c + 1], rs[:], ALU.mult, ALU.mult,
                  [xk, rk, "vec"], [("hT", t)])

    K.dma("sp", "t64c", t64c[:], io["rope"][0], [], ["t64c"])
    K.dma("sp", "t64s", t64s[:], io["rope"][1], [], ["t64s"])
    K.dma("sp", "t32c", t32c[:], io["rope"][2], [], ["t32c"])
    K.dma("sp", "t32s", t32s[:], io["rope"][3], [], ["t32s"])
    items = [(W1[:, :, goff:goff + gcols], [128, 8, gcols]) for (goff, gcols, _) in FM_GROUPS[5:7]]
    items += [(W1[:, :, goff:goff + gcols], [128, 8, gcols]) for (goff, gcols, _) in FM_GROUPS[0:5]]
    items += [(W1[:, :, goff:goff + gcols], [128, 8, gcols]) for (goff, gcols, _, _) in TM_GROUPS]
    items += [(io["Wuq"], [128, 3, 1152]), (io["Wukv"], [128, 768])]
    ws = WStream(K, wsl, ["wsl0", "wsl1"], items)
    wload = lambda src, shape: ws.next()

    pcnt = [0]
    scnt = [0]
    rcnt = [0]

    def rope(psA, psB, ka, kb, cosT, sinT, ck, sk, out, okey, p0, p1, t):
        i = rcnt[0] % 2
        rcnt[0] += 1
        sl = slice(t * 512, (t + 1) * 512)
        K.tt("dve", r1[i][p0:p1, :], psA[p0:p1, :], cosT[p0:p1, sl], ALU.mult, [ka, ck], [f"r1_{i}"])
        K.tt("dve", r2[i][p0:p1, :], psB[p0:p1, :], sinT[p0:p1, sl], ALU.mult, [kb, sk], [f"r2_{i}"])
        K.tt("pool", out, r1[i][p0:p1, :], r2[i][p0:p1, :], ALU.add, [f"r1_{i}", f"r2_{i}"], [okey])

    ag_pending = list(io.get("ag_pieces", []))
    ag_ready = []

    def try_ag(flush=False):
        for item in ag_ready[:]:
            key, src, dst, deps = item
            K.P.dma("pool", "ag" + key[2], lambda e, src=src, dst=dst: e.collective_compute(
                "AllGather", ALU.bypass, replica_groups=RG, ins=[src], outs=[dst[:, :]]), deps, [key], inc=1)
            io["ag_ops"][key] = K.P.lastw[key]
            ag_ready.remove(item)
        for item in ag_pending[:]:
            if all(k in K.P.lastw for k in item[3]):
                ag_pending.remove(item)
                ag_ready.append(item)
        if flush and (ag_ready or ag_pending):
            try_ag(flush=True)

    def fm_groups(groups):
        for (goff, gcols, tiles) in groups:
            w, wk = wload(W1[:, :, goff:goff + gcols], [128, 8, gcols])
            for (kind, loff, M, oname, orow) in tiles:
                roped = kind in ("ropeA", "ropeC", "kr")
                si = scnt[0] % NSTG
                if kind in ("ropeA", "ropeC"):
                    scnt[0] += 1
                for t in range(4):
                    sl = slice(t * 512, (t + 1) * 512)
                    ba = pcnt[0] % 4
                    pcnt[0] += 1
                    psA = banks[ba]
                    ka = f"bank{ba}"
                    for c in range(8):
                        K.mm(psA[0:M, :], w[:, c, loff:loff + M], hT[:, c, sl], c == 0, c == 7, [wk, ("hT", t)], [ka])
                    if roped:
                        bb = pcnt[0] % 4
                        pcnt[0] += 1
                        psB = banks[bb]
                        kb = f"bank{bb}"
                        so = loff + (128 if kind != "kr" else 96)
                        for c in range(8):
                            K.mm(psB[0:M, :], w[:, c, so:so + M], hT[:, c, sl], c == 0, c == 7, [wk, ("hT", t)], [kb])
                    if kind == "ropeA":
                        rope(psA, psB, ka, kb, t64c, t64s, "t64c", "t64s", stg[si][:, sl], f"stg{si}", 0, 128, t)
                    elif kind == "ropeC":
                        rope(psA, psB, ka, kb, t32c, t32s, "t32c", "t32s", stg[si][:, sl], f"stg{si}", 0, 128, t)
                    elif kind == "kr":
                        rope(psA, psB, ka, kb, t32c, t32s, "t32c", "t32s", krT[64:96, sl], "krT", 64, 96, t)
                    elif kind == "cq":
                        K.copy("act", cqT[:, orow, sl], psA[:, :], [ka], [("cqT", t)])
                    elif kind == "ckv":
                        K.copy("act", ckvT[:, sl], psA[:, :], [ka], [("ckvT", t)])
                if kind in ("ropeA", "ropeC"):
                    K.dma("sp", f"stg{si}", io[oname][orow:orow + 128, :], stg[si][:, :], [f"stg{si}"], [(oname, orow)])
                    try_ag()


    def tm_groups():
        vcnt = 0
        for (goff, gcols, nh, oname) in TM_GROUPS:
            w, wk = wload(W1[:, :, goff:goff + gcols], [128, 8, gcols])
            vi = vcnt % 2
            vcnt += 1
            vk = "vst0"
            for m in range(NBLK):
                bi = 4 + (m % 2)
                ps = banks[bi]
                pk = f"bank{bi}"
                t = m // 4
                for c in range(8):
                    K.mm(ps[:, 0:gcols], hT[:, c, m * 128:(m + 1) * 128], w[:, c, 0:gcols], c == 0, c == 7, [wk, ("hT", t)], [pk])
                K.copy("act" if m % 2 == 0 else "dve", vst[vi][:, 0:nh, m, 0:64],
                       ps[:, 0:gcols].rearrange("p (h d) -> p h d", h=nh), [pk], [vk])
            for h in range(nh):
                K.dma("sp", f"{vk}_{h // (3 if nh == 6 else 2)}", io[oname][h], vst[vi][:, h, :, :], [vk], [(oname, h)])
            try_ag()

        return vcnt

    def mla_norms():
        for t in range(4):
            sl = slice(t * 512, (t + 1) * 512)
            rs = rstd[t % 2]
            rk = f"rstd{t % 2}"
            rms_stats(K, C, [cqT[:, c, sl] for c in range(3)], [("cqT", t)] * 3, 128, "c384", "eps6", banks[7], "bank7", rs[:], rk, sq, sqk)
            for c in range(3):
                K.stt("dve", cqn[:, c, sl], cqT[:, c, sl], vec[:, 24 + c:25 + c], rs[:], ALU.mult, ALU.mult,
                      [("cqT", t), rk, "vec"], [("cqn", t)])
            rs = rstd[(t + 1) % 2]
            rk = f"rstd{(t + 1) % 2}"
            rms_stats(K, C, [ckvT[:, sl]], [("ckvT", t)], 128, "c128", "eps6", banks[6], "bank6", rs[:], rk, sq, sqk)
            K.stt("dve", ckvn[:, sl], ckvT[:, sl], vec[:, 27:28], rs[:], ALU.mult, ALU.mult, [("ckvT", t), rk, "vec"], [("ckvn", t)])

    def mla(vcnt):
        wq, wqk = wload(io["Wuq"], [128, 3, 1152])
        for h in range(6):
            si = scnt[0] % NSTG
            scnt[0] += 1
            for t in range(4):
                sl = slice(t * 512, (t + 1) * 512)
                ba = pcnt[0] % 4
                pcnt[0] += 1
                bb = pcnt[0] % 4
                pcnt[0] += 1
                psA, psB = banks[ba], banks[bb]
                ka, kb = f"bank{ba}", f"bank{bb}"
                for c in range(3):
                    K.mm(psA[0:96, :], wq[:, c, h * 96:(h + 1) * 96], cqn[:, c, sl], c == 0, c == 2, [wqk, ("cqn", t)], [ka])
                for c in range(3):
                    K.mm(psB[0:96, :], wq[:, c, 576 + h * 96:576 + (h + 1) * 96], cqn[:, c, sl], c == 0, c == 2, [wqk, ("cqn", t)], [kb])
                K.copy("act", stg[si][0:64, sl], psA[0:64, :], [ka], [f"stg{si}"])
                rope(psA, psB, ka, kb, t32c, t32s, "t32c", "t32s", stg[si][64:96, sl], f"stg{si}", 64, 96, t)
            K.dma("sp", f"stg{si}", io["QB"][h], stg[si][0:96, :], [f"stg{si}"], [("QB", h)])

        wkv, wkvk = wload(io["Wukv"], [128, 768])
        for h in range(6):
            si = scnt[0] % NSTG
            scnt[0] += 1
            for t in range(4):
                sl = slice(t * 512, (t + 1) * 512)
                ba = pcnt[0] % 4
                pcnt[0] += 1
                psA = banks[ba]
                ka = f"bank{ba}"
                K.mm(psA[0:64, :], wkv[:, h * 64:(h + 1) * 64], ckvn[:, sl], True, True, [wkvk, ("ckvn", t)], [ka])
                K.copy("act" if t % 2 == 0 else "dve", stg[si][0:64, sl], psA[0:64, :], [ka], [f"stg{si}"])
            K.dma("sp", f"stg{si}", io["KB"][h, 0:64, :], stg[si][0:64, :], [f"stg{si}"], [("KB", h)])
            K.dma("sp", f"krT_out{h // 2}", io["KB"][h, 64:96, :], krT[64:96, :], ["krT"], [("KBr", h)])
        vi = vcnt % 2
        vcnt += 1
        vk = "vst0"
        for m in range(NBLK):
            bi = 4 + (m % 2)
            ps = banks[bi]
            pk = f"bank{bi}"
            t = m // 4
            K.mm(ps[:, 0:384], ckvn[:, m * 128:(m + 1) * 128], wkv[:, 384:768], True, True, [wkvk, ("ckvn", t)], [pk])
            K.copy("act" if m % 2 == 0 else "dve", vst[vi][:, 0:6, m, 0:64],
                   ps[:, 0:384].rearrange("p (h d) -> p h d", h=6), [pk], [vk])
        for h in range(6):
            K.dma("sp", f"{vk}_{h // 3}", io["VB"][h], vst[vi][:, h, :, :], [vk], [("VB", h)])
        try_ag()

    fm_groups(FM_GROUPS[5:7])
    mla_norms()
    fm_groups(FM_GROUPS[0:5])
    vc = tm_groups()
    mla(vc)
    try_ag(flush=True)


P1_OUT = dict(QA=[384, TL], KA=[384, TL], QC=[256, TL], KC=[256, TL], QB=[6, 96, TL], KB=[6, 96, TL],
              VA=[6, 128, NBLK, 65], VB=[6, 128, NBLK, 65], VC=[4, 128, NBLK, 65])


def build_p1():
    nc = bass.Bass("TRN2", target_bir_lowering=False)
    with contextlib.ExitStack() as st:
        K = Ctx(nc, st)
        io = {}
        io["xT"] = K.dram("xT", [128, 8, TL], F32, "ExternalInput")
        io["W1"] = K.dram("W1", [128, 8, W1COLS], F32, "ExternalInput")
        io["Wuq"] = K.dram("Wuq", [128, 3, 1152], F32, "ExternalInput")
        io["Wukv"] = K.dram("Wukv", [128, 768], F32, "ExternalInput")
        io["vecs"] = K.dram("vecs", [128, NVEC], F32, "ExternalInput")
        rope = K.dram("rope", [4, 128, TL], F32, "ExternalInput")
        io["rope"] = [rope[i] for i in range(4)]
        for name, shape in P1_OUT.items():
            io[name] = K.dram(name, shape, BF16, "ExternalOutput")
        C = make_consts(K)
        phase1(K, C, io)
        outs = [o for o in K.P.ops["sp"] if o.is_dma and o.dma_sem.startswith(("stg", "vst", "krT_out"))]
        K.P.emit(final_wait_ops=outs)
    return nc


def _swap_cols(cols, blk):
    cols = np.asarray(cols)
    out = cols.copy().reshape(-1, blk)
    half = blk // 2
    out = np.concatenate([out[:, half:], out[:, :half]], axis=1)
    return out.reshape(-1)


def w1_columns():
    cols = []
    for base in (OFF["qa"], OFF["ka"]):
        for t in range(3):
            c = np.arange(base + t * 128, base + (t + 1) * 128)
            cols += [c, _swap_cols(c, 64)]
    for base in (OFF["qc"], OFF["kc"]):
        for t in range(2):
            c = np.arange(base + t * 128, base + (t + 1) * 128)
            cols += [c, _swap_cols(c, 32)]
    cols.append(np.arange(OFF["cq"], OFF["cq"] + 384))
    cols.append(np.arange(OFF["ckv"], OFF["ckv"] + 128))
    filler = np.arange(OFF["ckv"], OFF["ckv"] + 64)
    kr = np.arange(OFF["kr"], OFF["kr"] + 32)
    cols += [filler, kr, filler, _swap_cols(kr, 32)]
    cols.append(np.arange(OFF["va"], OFF["va"] + 384))
    cols.append(np.arange(OFF["vc"], OFF["vc"] + 256))
    cols = np.concatenate(cols)
    assert cols.shape[0] == W1COLS
    return cols


def chunked(w, nchunk):
    n = w.shape[1]
    return np.ascontiguousarray(w.reshape(nchunk, 128, n).transpose(1, 0, 2))


def prep_layer(inp, l):
    f = lambda a: np.asarray(a, dtype=np.float32)
    out = {}
    w_in = f(inp["w_in"][l])
    out["W1"] = chunked(w_in[:, w1_columns()], 8)
    wuq = f(inp["mla_w_uq"][l])
    ncols, scols = [], []
    for h in range(6):
        c = np.arange(h * 96, (h + 1) * 96)
        ncols.append(c)
        scols.append(np.concatenate([c[:64], _swap_cols(c[64:], 32)]))
    out["Wuq"] = chunked(wuq[:, np.concatenate(ncols + scols)], 3)
    wukv = f(inp["mla_w_ukv"][l])
    kc = np.concatenate([np.arange(h * 128, h * 128 + 64) for h in range(6)])
    vc = np.concatenate([np.arange(h * 128 + 64, h * 128 + 128) for h in range(6)])
    out["Wukv"] = np.ascontiguousarray(wukv[:, np.concatenate([kc, vc])])
    vec = np.zeros((128, NVEC), np.float32)
    vec[:, 0:8] = f(inp["ln1_g"][l]).reshape(8, 128).T
    vec[:, 8:16] = f(inp["ln2_g"][l]).reshape(8, 128).T
    vec[:, 16:24] = f(inp["final_g"]).reshape(8, 128).T
    vec[:, 24:27] = f(inp["mla_q_norm_g"][l]).reshape(3, 128).T
    vec[:, 27] = f(inp["mla_kv_norm_g"][l])
    vec[:, 28] = np.tile(f(inp["diff_subln_g"][l]), 2)
    lam_init = 0.8 - 0.6 * math.exp(-0.3 * l)
    vec[:, 29] = lam_init
    vec[:, 30] = 1.0 - lam_init
    vec[:, 32:160] = f(inp["diff_lambda"][l]).reshape(1, 128)
    out["vecs"] = vec
    out["Wo"] = chunked(f(inp["w_o"][l]), 8)
    out["Wup"] = chunked(f(inp["w_up"][l]), 8)
    out["Wdown"] = chunked(f(inp["w_down"][l]), 32)
    return out


def core_positions(j):
    m = np.arange(NBLK)[:, None]
    t = np.arange(128)[None, :]
    return ((4 * m + j) * 128 + t).reshape(-1)


def rope_tables(j):
    pos = core_positions(j).astype(np.float32)
    tabs = []
    for dim in (64, 32):
        half = dim // 2
        inv = (1.0 / (np.float32(10000.0) ** (np.arange(half, dtype=np.float32) / np.float32(half)))).astype(np.float32)
        r = np.arange(128)
        i = r % dim
        fidx = i % half
        ang = (pos[None, :] * inv[fidx][:, None]).astype(np.float32)
        sign = np.where(i < half, -1.0, 1.0).astype(np.float32)[:, None]
        tabs.append(np.cos(ang).astype(np.float32))
        tabs.append((np.sin(ang) * sign).astype(np.float32))
    return np.stack(tabs, 0)


def x_to_core(x, c):
    b, j = c // 4, c % 4
    xs = np.asarray(x[b], dtype=np.float32)[core_positions(j)]
    return chunked(np.ascontiguousarray(xs.T), 8)


def a_cols(ak):
    return max(0, ak) * 128, min(4, ak + 5) * 128


def phase2(K, C, io):
    P = K.P
    banks = K.banks
    vec = K.sb("vec", [128, NVEC], F32)
    K.dma("sp", "vec", vec[:], io["vecs"], [], ["vec"])
    MAt = K.sb("MAt", [128, 32, 512], BF16)
    MBt = K.sb("MBt", [128, 16, 512], BF16)

    kt = [K.sb(f"kt{i}", [128, 4, TL], BF16) for i in range(2)]
    vt = [K.sb(f"vt{i}", [128, 4 * NBLK, 65], BF16) for i in range(2)]
    qa = [K.sb(f"qa{i}", [128, TL], BF16) for i in range(2)]
    qb = [K.sb(f"qb{i}", [128, TL], BF16) for i in range(2)]
    qc = [K.sb(f"qc{i}", [128, 2, TL], BF16) for i in range(2)]
    for qq, nm in ((qa, "qa"), (qc, "qc"), (qb, "qb")):
        for i in range(2):
            K.memset("pool", qq[i][:], 0.0, [f"{nm}{i}"])
    NPT = 6
    LAG = 3
    NSB = 3
    pt = [K.sb(f"pt{i}", [128, 2, 512], BF16) for i in range(NPT)]
    osb = [K.sb(f"osb{i}", [128, 512], F32) for i in range(4)]
    rz = [K.sb(f"rz{i}", [128, 512], F32) for i in range(4)]
    ones512 = K.sb("ones512", [128, 512], F32)
    K.memset("pool", ones512[:], -1.0, ["ones512"])
    deferred = []
    deferred_a = []
    tA = K.sb("tA", [128, 512], F32)
    tB = K.sb("tB", [128, 512], F32)
    tO = K.sb("tO", [128, 512], F32)
    tS = K.sb("tS", [128, 512], F32)
    ostg = [K.sb(f"ostg{i}", [128, TL], BF16) for i in range(2)]
    lam = K.sb("lam", [128, 40], F32)

    K.tt("dve", lam[:, 0:32], vec[:, 32:64], vec[:, 64:96], ALU.mult, ["vec"], ["lam"])
    P.op("dve", lambda e: e.reduce_sum(out=lam[:, 32:33], in_=lam[:, 0:32], axis=mybir.AxisListType.X), ["lam"], ["lam1"])
    K.tt("dve", lam[:, 0:32], vec[:, 96:128], vec[:, 128:160], ALU.mult, ["vec", "lam1"], ["lam"])
    P.op("dve", lambda e: e.reduce_sum(out=lam[:, 33:34], in_=lam[:, 0:32], axis=mybir.AxisListType.X), ["lam"], ["lam2"])
    K.act(lam[:, 34:36], lam[:, 32:34], AF.Exp, ["lam1", "lam2"], ["lam3"])
    K.tt("dve", lam[:, 36:37], lam[:, 35:36], lam[:, 34:35], ALU.subtract, ["lam3"], ["lam4"])
    K.tt("dve", lam[:, 37:38], lam[:, 36:37], vec[:, 29:30], ALU.subtract, ["lam4", "vec"], ["neglam"])
    K.tt("dve", lam[:, 38:39], vec[:, 28:29], vec[:, 30:31], ALU.mult, ["vec"], ["gc"])
    neglam = lam[:, 37:38]
    gc = lam[:, 38:39]

    jobs = [("A", h) for h in range(6)] + [("C", h) for h in range(4)] + [("B", h) for h in range(6)]

    jobslot = {}

    def load_job(ji):
        kind, h = jobs[ji]
        s = ji % 2
        a = io["acc"](kind, h)
        rows = a["rows"]
        if kind == "A":
            qs = h % 2
            K.dma("sp", f"qa{qs}", qa[qs][qs * 64:qs * 64 + 64, :], a["q"], [], [f"qa{qs}"])
            jobslot[ji] = [(qa[qs], f"qa{qs}", None)]
        elif kind == "B":
            qs = h % 2
            K.dma("sp", f"qb{qs}", qb[qs][0:96, :], a["q"], [], [f"qb{qs}"])
            jobslot[ji] = [(qb[qs], f"qb{qs}", None)]
        else:
            qs = h % 2
            for mi in range(2):
                p0 = qs * 64 + mi * 32
                K.dma("sp", f"qc{qs}", qc[qs][p0:p0 + 32, mi, :], a["q"][mi * 32:(mi + 1) * 32, :], [], [f"qc{qs}"])
            jobslot[ji] = [(qc[qs], f"qc{qs}", 0), (qc[qs], f"qc{qs}", 1)]
        for r in range(4):
            K.dma("sp", f"kt{s}", kt[s][0:rows, r, :], a["k"](r), a["kdeps"], [f"kt{s}"])
        for r in range(4):
            K.dma("sp", f"vt{s}", vt[s][:, r * NBLK:(r + 1) * NBLK, :], a["v"](r), a["vdeps"], [f"vt{s}"])

    cnt = dict(s=0, p=0, o=0, f=0)
    out_ops = []
    K.dma("sp", "MAt_hi", MAt[:, 16:32, :], io["MA"][:, 16:32, :], [], ["MAt_hi"])
    load_job(0)
    K.dma("sp", "MAt_lo", MAt[:, 0:16, :], io["MA"][:, 0:16, :], [], ["MAt_lo"])
    K.dma("sp", "MBt", MBt[:], io["MB"], [], ["MBt"])
    for ji, (kind, h) in enumerate(jobs):
        if ji + 1 < len(jobs):
            load_job(ji + 1)
        s = ji % 2
        kk, vk = f"kt{s}", f"vt{s}"
        maps = jobslot[ji]
        if kind == "A":
            scale, row0 = 64 ** -0.5, h * 64
        elif kind == "B":
            scale, row0 = 96 ** -0.5, 384 + h * 64
        else:
            scale, row0 = 32 ** -0.5, 768 + h * 64
        og = ostg[ji % 2]
        ogk = f"ostg{ji % 2}"
        for g in range(4):
            blocks = []
            if kind == "A":
                for ak in (0, -1, -2, -3, -4, 1, 2, 3):
                    mloc = 4 * g + ak
                    if mloc < 0:
                        continue
                    c0, c1 = a_cols(ak)
                    for r in range(4):
                        blocks.append((r, mloc, c0, c1, MAt[:, (ak + 4) * 4 + r, :], "MAt_hi" if ak >= 0 else "MAt_lo"))
            else:
                for mloc in range(4 * g + 4):
                    ak = mloc - 4 * g
                    for r in range(4):
                        if ak < 0:
                            blocks.append((r, mloc, 0, 512, None, None))
                        else:
                            blocks.append((r, mloc, ak * 128, 512, MBt[:, ak * 4 + r, :], "MBt"))
            steps = [(b, mi) for b in blocks for mi in range(len(maps))]
            O = [banks[6 + mi] for mi in range(len(maps))]
            Ok = [f"bank{6 + mi}" for mi in range(len(maps))]
            pend = []
            nst = len(steps)
            first = [True] * len(maps)
            lastidx = [max(i for i, (b, mi) in enumerate(steps) if mi == m_) for m_ in range(len(maps))]
            assert nst % 2 == 0
            npair = nst // 2
            for ip in range(npair + LAG):
                if ip == min(1, npair - 1) and deferred_a:
                    for f in deferred_a:
                        f()
                    deferred_a.clear()
                if ip == min(7, npair - 1) and deferred:
                    for f in deferred:
                        f()
                    deferred.clear()
                if ip < npair:
                    sb_ = cnt["s"] % NSB
                    cnt["s"] += 1
                    pi = cnt["p"] % NPT
                    cnt["p"] += 1
                    SP = K.pairs[sb_]
                    spk, ptk = f"sp{sb_}", f"pt{pi}"
                    c0, c1 = steps[2 * ip][0][2], steps[2 * ip][0][3]
                    assert (steps[2 * ip + 1][0][2], steps[2 * ip + 1][0][3]) == (c0, c1)
                    for hf in range(2):
                        (r, mloc, _, _, mask, mkey), mi = steps[2 * ip + hf]
                        qbuf, qk, qm = maps[mi]
                        qsrc = qbuf[:, g * 512 + c0:g * 512 + c1] if qm is None else qbuf[:, qm, g * 512 + c0:g * 512 + c1]
                        K.mm(SP[:, hf, c0:c1], kt[s][:, r, mloc * 128:(mloc + 1) * 128], qsrc, True, True, [kk, qk], [spk])
                    K.act(pt[pi][:, :, c0:c1], SP[:, :, c0:c1], AF.Exp, [spk], [ptk], scale=float(scale))
                    for hf in range(2):
                        (r, mloc, _, _, mask, mkey), mi = steps[2 * ip + hf]
                        if mask is not None:
                            K.tt("dve", pt[pi][:, hf, c0:c1], pt[pi][:, hf, c0:c1], mask[:, c0:c1], ALU.mult, [ptk, mkey], [ptk])
                    pend.append((ip, c0, c1, pi))
                if ip >= LAG:
                    (ip0, c0, c1, pi) = pend.pop(0)
                    for hf in range(2):
                        i0 = 2 * ip0 + hf
                        (r, mloc, _, _, mask, mkey), mi = steps[i0]
                        K.mm(O[mi][0:65, c0:c1], vt[s][:, r * NBLK + mloc, 0:65], pt[pi][:, hf, c0:c1], first[mi], i0 == lastidx[mi],
                             [vk, f"pt{pi}"], [Ok[mi]])
                        first[mi] = False
            gsl = slice(g * 512, (g + 1) * 512)

            def fbank():
                b = cnt["s"] % NSB
                cnt["s"] += 1
                return K.pairs[b][:, 0, :], f"sp{b}"

            def prec(i):
                K.recip(rz[i][64:65, :], osb[i][64:65, :], [f"osb{i}"], [f"rz{i}"])

            if kind in ("A", "B"):
                fi = cnt["f"] % 4
                cnt["f"] += 1
                K.copy("act", osb[fi][0:65, :], O[0][0:65, :], [Ok[0]], [f"osb{fi}"])

                def part2a(fi=fi):
                    K.recip(rz[fi][64:65, :], osb[fi][64:65, :], [f"osb{fi}"], [f"rz{fi}"])

                def part2(fi=fi, og=og, ogk=ogk, gsl=gsl):
                    fb, fk = fbank()
                    K.mm(fb[0:64, :], C["one"][64:65, 0:64], rz[fi][64:65, :], True, True, [f"rz{fi}", "one"], [fk])
                    K.tt("dve", og[0:64, gsl], osb[fi][0:64, :], fb[0:64, :], ALU.mult, [f"osb{fi}", fk], [ogk])
            else:
                f0 = cnt["f"] % 4
                f1 = (cnt["f"] + 1) % 4
                cnt["f"] += 2
                K.copy("act", osb[f0][0:65, :], O[0][0:65, :], [Ok[0]], [f"osb{f0}"])
                K.copy("act", osb[f1][0:65, :], O[1][0:65, :], [Ok[1]], [f"osb{f1}"])

                def part2a(f0=f0, f1=f1):
                    K.recip(rz[f0][64:65, :], osb[f0][64:65, :], [f"osb{f0}"], [f"rz{f0}"])
                    K.recip(rz[f1][64:65, :], osb[f1][64:65, :], [f"osb{f1}"], [f"rz{f1}"])

                def part2(f0=f0, f1=f1, og=og, ogk=ogk, gsl=gsl):
                    fb, fk = fbank()
                    K.mm(fb[0:64, :], C["one"][64:65, 0:64], rz[f0][64:65, :], True, True, [f"rz{f0}", "one"], [fk])
                    K.tt("dve", tA[0:64, :], osb[f0][0:64, :], fb[0:64, :], ALU.mult, [f"osb{f0}", fk], ["tA"])
                    fb, fk = fbank()
                    K.mm(fb[0:64, :], C["one"][64:65, 0:64], rz[f1][64:65, :], True, True, [f"rz{f1}", "one"], [fk])
                    K.tt("dve", tB[0:64, :], osb[f1][0:64, :], fb[0:64, :], ALU.mult, [f"osb{f1}", fk], ["tB"])
                    K.stt("dve", tO[0:64, :], tB[0:64, :], neglam[0:64, 0:1], tA[0:64, :], ALU.mult, ALU.add, ["tA", "tB", "neglam"], ["tO"])
                    K.tt("pool", tS[0:64, :], tO[0:64, :], tO[0:64, :], ALU.mult, ["tO"], ["tS"])
                    fb, fk = fbank()
                    K.mm(fb[0:64, :], C["c64"][0:64, 0:64], tS[0:64, :], True, True, ["tS", "c64"], [fk])
                    K.act(tS[0:64, :], fb[0:64, :], AF.Sqrt, [fk, "eps5"], ["tS"], bias=C["eps5"][0:64, 0:1], scale=1.0)
                    K.recip(tS[0:64, :], tS[0:64, :], ["tS"], ["tS"])
                    K.stt("dve", og[0:64, gsl], tO[0:64, :], gc[0:64, 0:1], tS[0:64, :], ALU.mult, ALU.mult, ["tO", "tS", "gc"], [ogk])
            deferred_a.append(part2a)
            deferred.append(part2)
        deferred.append(lambda og=og, ogk=ogk, row0=row0: out_ops.append(
            K.dma("sp", ogk, io["mixT"][row0:row0 + 64, :], og[0:64, :], [ogk], [("mixT", row0)])))
    for f in deferred_a + deferred:
        f()
    deferred.clear()
    return out_ops


def simple_acc(io):
    def acc(kind, h):
        if kind == "A":
            return dict(rows=128, q=io["QA"][h * 64:(h + 1) * 64, :], k=lambda r: io["KAg"][r, (h // 2) * 128:(h // 2 + 1) * 128, :],
                        v=lambda r: io["VAg"][r, h], kdeps=[], vdeps=[])
        if kind == "B":
            return dict(rows=96, q=io["QB"][h], k=lambda r: io["KBg"][r, h], v=lambda r: io["VBg"][r, h], kdeps=[], vdeps=[])
        return dict(rows=128, q=io["QC"][h * 64:(h + 1) * 64, :], k=lambda r: io["KCg"][r, (h // 2) * 128:(h // 2 + 1) * 128, :],
                    v=lambda r: io["VCg"][r, h], kdeps=[], vdeps=[])
    return acc


def build_p2():
    nc = bass.Bass("TRN2", target_bir_lowering=False)
    with contextlib.ExitStack() as st:
        K = Ctx(nc, st)
        io = {}
        io["vecs"] = K.dram("vecs", [128, NVEC], F32, "ExternalInput")
        io["MA"] = K.dram("MA", [128, 32, 512], BF16, "ExternalInput")
        io["MB"] = K.dram("MB", [128, 16, 512], BF16, "ExternalInput")
        io["QA"] = K.dram("QA", [384, TL], BF16, "ExternalInput")
        io["QB"] = K.dram("QB", [6, 96, TL], BF16, "ExternalInput")
        io["QC"] = K.dram("QC", [256, TL], BF16, "ExternalInput")
        io["KAg"] = K.dram("KAg", [4, 384, TL], BF16, "ExternalInput")
        io["KBg"] = K.dram("KBg", [4, 6, 96, TL], BF16, "ExternalInput")
        io["KCg"] = K.dram("KCg", [4, 256, TL], BF16, "ExternalInput")
        io["VAg"] = K.dram("VAg", [4, 6, 128, NBLK, 65], BF16, "ExternalInput")
        io["VBg"] = K.dram("VBg", [4, 6, 128, NBLK, 65], BF16, "ExternalInput")
        io["VCg"] = K.dram("VCg", [4, 4, 128, NBLK, 65], BF16, "ExternalInput")
        io["mixT"] = K.dram("mixT", [1024, TL], BF16, "ExternalOutput")
        C = make_consts(K)
        io["acc"] = simple_acc(io)
        outs = phase2(K, C, io)
        K.P.emit(final_wait_ops=outs)
    return nc


def attn_masks(j):
    tk = np.arange(128)[:, None, None]
    aq = np.arange(4)[None, :, None]
    tq = np.arange(128)[None, None, :]
    MA = np.zeros((128, 32, 4, 128), np.float32)
    MB = np.zeros((128, 16, 4, 128), np.float32)
    for ak in range(-4, 4):
        for r in range(4):
            dist = (4 * (aq - ak) + j - r) * 128 + tq - tk
            cnt = ((dist >= 0) & (dist <= 128)).astype(np.float32)
            cnt += ((dist >= 0) & (dist <= 512) & (dist % 4 == 0))
            cnt += ((dist >= 0) & (dist <= 2048) & (dist % 16 == 0))
            MA[:, (ak + 4) * 4 + r] = cnt
            if ak >= 0:
                MB[:, ak * 4 + r] = (dist >= 0)
    return (MA.reshape(128, 32, 512).astype(ml_dtypes.bfloat16), MB.reshape(128, 16, 512).astype(ml_dtypes.bfloat16))


def phase3(K, C, io, last):
    P = K.P
    banks = K.banks
    vec = K.sb("vec", [128, NVEC], F32)
    K.dma("sp", "vec", vec[:], io["vecs"], [], ["vec"])
    xT = K.sb("xT", [128, 8, TL], F32)
    mx = K.sb("mx", [128, 8, TL], BF16)
    aT = [K.sb(f"aT{i}", [128, 4, TL], BF16) for i in range(2)]
    wsl = [K.sb(f"wsl{i}", [128, 4096], BF16) for i in range(2)]
    rl = [K.sb(f"rl{i}", [128, 512], F32) for i in range(2)]
    sq = [K.sb(f"sq{i}", [128, 512], F32) for i in range(2)]
    sqk = ["sq0", "sq1"]
    rstd = [K.sb(f"rstd{i}", [128, 512], F32) for i in range(2)]
    mixv = io["mixT"].rearrange("(c p) t -> p c t", p=128)
    for t in range(4):
        sl = slice(t * 512, (t + 1) * 512)
        K.dma("sp", f"mx{t}", mx[:, :, sl], mixv[:, :, sl], [], [("mx", t)])
        K.dma("sp", f"xT{t}", xT[:, :, sl], io["xT"][:, :, sl], [], [("xT", t)])
    items = [(io["Wo"][:, :, hf * 512:(hf + 1) * 512], [128, 8, 512]) for hf in range(2)]
    for fg in range(8):
        items.append((io["Wup"][:, :, fg * 512:(fg + 1) * 512], [128, 8, 512]))
        items.append((io["Wdown"][:, fg * 4:(fg + 1) * 4, :], [128, 4, 1024]))
    ws = WStream(K, wsl, ["wsl0", "wsl1"], items)
    pc = [0]

    def nbank():
        b = pc[0] % 4
        pc[0] += 1
        return banks[b], f"bank{b}"

    for hf in range(2):
        w, wk = ws.next()
        for dl in range(4):
            d = hf * 4 + dl
            for t in range(4):
                sl = slice(t * 512, (t + 1) * 512)
                ps, pk = nbank()
                for c in range(8):
                    K.mm(ps[:, :], w[:, c, dl * 128:(dl + 1) * 128], mx[:, c, sl], c == 0, c == 7, [wk, ("mx", t)], [pk])
                K.tt("dve", xT[:, d, sl], ps[:, :], xT[:, d, sl], ALU.add, [pk, ("xT", t)], [("xT", t)])
    for t in range(4):
        sl = slice(t * 512, (t + 1) * 512)
        rs, rk = rstd[t % 2], f"rstd{t % 2}"
        rms_stats(K, C, [xT[:, c, sl] for c in range(8)], [("xT", t)] * 8, 128, "c1024", "eps6", banks[7], "bank7", rs[:], rk, sq, sqk)
        for c in range(8):
            K.stt("dve", mx[:, c, sl], xT[:, c, sl], vec[:, 8 + c:9 + c], rs[:], ALU.mult, ALU.mult, [("xT", t), rk, "vec"], [("mx", t)])
    rc = 0
    for fg in range(8):
        wu, wuk = ws.next()
        a = aT[fg % 2]
        ak = f"aT{fg % 2}"
        for fl in range(4):
            for t in range(4):
                sl = slice(t * 512, (t + 1) * 512)
                ps, pk = nbank()
                for c in range(8):
                    K.mm(ps[:, :], wu[:, c, fl * 128:(fl + 1) * 128], mx[:, c, sl], c == 0, c == 7, [wuk, ("mx", t)], [pk])
                ri = rc % 2
                rc += 1
                K.act(rl[ri][:, :], ps[:, :], AF.Relu, [pk], [f"rl{ri}"])
                K.tt("pool", a[:, fl, sl], rl[ri][:, :], rl[ri][:, :], ALU.mult, [f"rl{ri}"], [(ak, t)])
        wd, wdk = ws.next()
        for d in range(8):
            for t in range(4):
                sl = slice(t * 512, (t + 1) * 512)
                ps, pk = nbank()
                for fl in range(4):
                    K.mm(ps[:, :], wd[:, fl, d * 128:(d + 1) * 128], a[:, fl, sl], fl == 0, fl == 3, [wdk, (ak, t)], [pk])
                K.tt("dve", xT[:, d, sl], ps[:, :], xT[:, d, sl], ALU.add, [pk, ("xT", t)], [("xT", t)])
    outs = []
    if last:
        for t in range(4):
            sl = slice(t * 512, (t + 1) * 512)
            rs, rk = rstd[t % 2], f"rstd{t % 2}"
            rms_stats(K, C, [xT[:, c, sl] for c in range(8)], [("xT", t)] * 8, 128, "c1024", "eps6", banks[7], "bank7", rs[:], rk, sq, sqk)
            for c in range(8):
                K.stt("dve", xT[:, c, sl], xT[:, c, sl], vec[:, 16 + c:17 + c], rs[:], ALU.mult, ALU.mult, [("xT", t), rk, "vec"], [("xT", t)])
    for t in range(4):
        sl = slice(t * 512, (t + 1) * 512)
        outs.append(K.dma("sp", "xTo", io["xout"][:, :, sl], xT[:, :, sl], [("xT", t)], [("xout", t)]))
    return outs


def build_p3(last):
    nc = bass.Bass("TRN2", target_bir_lowering=False)
    with contextlib.ExitStack() as st:
        K = Ctx(nc, st)
        io = {}
        io["vecs"] = K.dram("vecs", [128, NVEC], F32, "ExternalInput")
        io["xT"] = K.dram("xT", [128, 8, TL], F32, "ExternalInput")
        io["mixT"] = K.dram("mixT", [1024, TL], BF16, "ExternalInput")
        io["Wo"] = K.dram("Wo", [128, 8, 1024], F32, "ExternalInput")
        io["Wup"] = K.dram("Wup", [128, 8, DFF], F32, "ExternalInput")
        io["Wdown"] = K.dram("Wdown", [128, 32, 1024], F32, "ExternalInput")
        io["xout"] = K.dram("xout", [128, 8, TL], F32, "ExternalOutput")
        C = make_consts(K)
        outs = phase3(K, C, io, last)
        K.P.emit(final_wait_ops=outs)
    return nc


_PROGS = {}


def _prog(name):
    if name not in _PROGS:
        _PROGS[name] = dict(p1=build_p1, p2=build_p2, p3a=lambda: build_p3(False), p3b=lambda: build_p3(True))[name]()
    return _PROGS[name]


def _run(nc, in_maps):
    res = run_bass_kernel_spmd(nc, in_maps, core_ids=list(range(NCORE)))
    return res.results


def kernel_unfused(**inputs):
    x = np.asarray(inputs["x"], dtype=np.float32)
    xs = [x_to_core(x, c) for c in range(NCORE)]
    ropes = [rope_tables(j) for j in range(4)]
    masks = [attn_masks(j) for j in range(4)]
    for l in range(2):
        lw = prep_layer(inputs, l)
        r1 = _run(_prog("p1"), [dict(xT=xs[c], W1=lw["W1"], Wuq=lw["Wuq"], Wukv=lw["Wukv"], vecs=lw["vecs"], rope=ropes[c % 4])
                                for c in range(NCORE)])
        in2 = []
        gath = {}
        for b in range(2):
            for k in ("KA", "KB", "KC", "VA", "VB", "VC"):
                gath[(b, k)] = np.stack([np.asarray(r1[4 * b + j][k]) for j in range(4)], 0)
        for c in range(NCORE):
            b, j = c // 4, c % 4
            in2.append(dict(vecs=lw["vecs"], MA=masks[j][0], MB=masks[j][1],
                            QA=np.asarray(r1[c]["QA"]), QB=np.asarray(r1[c]["QB"]), QC=np.asarray(r1[c]["QC"]),
                            KAg=gath[(b, "KA")], KBg=gath[(b, "KB")], KCg=gath[(b, "KC")],
                            VAg=gath[(b, "VA")], VBg=gath[(b, "VB")], VCg=gath[(b, "VC")]))
        r2 = _run(_prog("p2"), in2)
        r3 = _run(_prog("p3b" if l == 1 else "p3a"),
                  [dict(vecs=lw["vecs"], xT=xs[c], mixT=np.asarray(r2[c]["mixT"]), Wo=lw["Wo"], Wup=lw["Wup"], Wdown=lw["Wdown"])
                   for c in range(NCORE)])
        xs = [np.asarray(r3[c]["xout"]) for c in range(NCORE)]
    out = np.empty((2, SEQ, D), np.float32)
    for c in range(NCORE):
        b, j = c // 4, c % 4
        xt = xs[c].transpose(1, 0, 2).reshape(D, TL)
        out[b, core_positions(j), :] = xt.T
    return out


KROWS = 384 + 576 + 256
RG = [[0, 1, 2, 3], [4, 5, 6, 7]]


def _phase(nc, tag, body, ext_sems=None):
    with nc.cleanup_on_exit():
        with contextlib.ExitStack() as st:
            K = Ctx(nc, st, tag, ext_sems)
            C = make_consts(K)
            outs = body(K, C)
            K.P.emit(final_wait_ops=outs, tag=tag, managed=False)
        nc.all_engine_barrier()


def build_fused():
    nc = bass.Bass("TRN2", target_bir_lowering=False, num_devices=NCORE)
    ext = lambda name, shape, dt: nc.dram_tensor(name, shape, dt, kind="ExternalInput").ap()
    internal = lambda name, shape, dt: nc.dram_tensor(name, shape, dt, kind="Internal").ap()
    xT_in = ext("xT", [128, 8, TL], F32)
    rope = ext("rope", [4, 128, TL], F32)
    MA = ext("MA", [128, 32, 512], BF16)
    MB = ext("MB", [128, 16, 512], BF16)
    xout = nc.dram_tensor("xout", [128, 8, TL], F32, kind="ExternalOutput").ap()
    x1T = internal("x1T", [128, 8, TL], F32)
    agsem = {g: nc.alloc_semaphore(name="agsem" + g) for g in "ABC"}
    ag_val = {g: 0 for g in "ABC"}
    for l in range(2):
        W = dict(W1=ext(f"W1_{l}", [128, 8, W1COLS], F32), Wuq=ext(f"Wuq_{l}", [128, 3, 1152], F32),
                 Wukv=ext(f"Wukv_{l}", [128, 768], F32), vecs=ext(f"vecs_{l}", [128, NVEC], F32),
                 Wo=ext(f"Wo_{l}", [128, 8, 1024], F32), Wup=ext(f"Wup_{l}", [128, 8, DFF], F32),
                 Wdown=ext(f"Wdown_{l}", [128, 32, 1024], F32))
        kloc2 = internal(f"kloc{l}", [KROWS * 8, 256], BF16)
        vloc2 = internal(f"vloc{l}", [16 * 520, 256], BF16)
        kloc = kloc2.rearrange("(r a) b -> r (a b)", a=8)
        QA = internal(f"QA{l}", [384, TL], BF16)
        QB = internal(f"QB{l}", [6, 96, TL], BF16)
        QC = internal(f"QC{l}", [256, TL], BF16)
        mixT = internal(f"mixT{l}", [1024, TL], BF16)
        x_cur = xT_in if l == 0 else x1T
        x_next = x1T if l == 0 else xout
        vl = vloc2.rearrange("(h x) b -> h (x b)", h=16).rearrange("h (p m e) -> h p m e", p=128, m=NBLK)
        io1 = dict(xT=x_cur, W1=W["W1"], Wuq=W["Wuq"], Wukv=W["Wukv"], vecs=W["vecs"], rope=[rope[i] for i in range(4)],
                   QA=QA, QB=QB, QC=QC, KA=kloc[0:384, :], KB=kloc[384:960, :].rearrange("(h d) t -> h d t", h=6),
                   KC=kloc[960:KROWS, :], VA=vl[0:6], VB=vl[6:12], VC=vl[12:16])

        pieces = []
        kp = {}
        for nm, r0, nrows, npc in (("KA", 0, 128, 3), ("KB", 384, 192, 3), ("KC", 960, 128, 2)):
            for p in range(npc):
                key = f"g{nm}{p}"
                g = internal(f"{key}_{l}", [4 * nrows * 8, 256], BF16)
                if nm == "KB":
                    deps = [("KB", 2 * p), ("KBr", 2 * p), ("KB", 2 * p + 1), ("KBr", 2 * p + 1)]
                else:
                    deps = [(nm, p * 128)]
                pieces.append((key, kloc2[(r0 + p * nrows) * 8:(r0 + (p + 1) * nrows) * 8, :], g, deps))
                kp[key] = g.rearrange("(r q a) b -> r q (a b)", r=4, a=8)
        for nm, h0, nh, npc in (("VA", 0, 3, 2), ("VB", 6, 3, 2), ("VC", 12, 2, 2)):
            for p in range(npc):
                key = f"g{nm}{p}"
                g = internal(f"{key}_{l}", [4 * nh * 520, 256], BF16)
                deps = [(nm, p * nh + i) for i in range(nh)]
                pieces.append((key, vloc2[(h0 + p * nh) * 520:(h0 + (p + 1) * nh) * 520, :], g, deps))
                kp[key] = g.rearrange("(r h x) b -> r h (x b)", r=4, h=nh).rearrange("r h (p m e) -> r h p m e", p=128, m=NBLK)
        io1["ag_pieces"] = pieces
        io1["ag_ops"] = {}

        def body1(K, C, io1=io1):
            phase1(K, C, io1)
            return [o for o in K.P.ops["sp"] if o.is_dma and o.dma_sem.startswith(("stg", "vst", "krT_out"))]

        _phase(nc, f"L{l}a_", body1, ext_sems={"ag" + g: (agsem[g], ag_val[g]) for g in "ABC"})
        assert len(io1["ag_ops"]) == len(pieces), (len(io1["ag_ops"]), len(pieces))
        for g in "ABC":
            ag_val[g] = max(op.dma_cnt for key, op in io1["ag_ops"].items() if key[2] == g)
        ag_done = {key: ag_val[key[2]] for key in io1["ag_ops"]}

        def acc(kind, h, kp=kp, QA=QA, QB=QB, QC=QC):
            if kind == "A":
                kk, vk = f"gKA{h // 2}", f"gVA{h // 3}"
                return dict(rows=128, q=QA[h * 64:(h + 1) * 64, :], k=lambda r: kp[kk][r, :, :],
                            v=lambda r: kp[vk][r, h % 3], kdeps=[kk], vdeps=[vk])
            if kind == "B":
                kk, vk = f"gKB{h // 2}", f"gVB{h // 3}"
                return dict(rows=96, q=QB[h], k=lambda r: kp[kk][r, (h % 2) * 96:(h % 2 + 1) * 96, :],
                            v=lambda r: kp[vk][r, h % 3], kdeps=[kk], vdeps=[vk])
            kk, vk = f"gKC{h // 2}", f"gVC{h // 2}"
            return dict(rows=128, q=QC[h * 64:(h + 1) * 64, :], k=lambda r: kp[kk][r, :, :],
                        v=lambda r: kp[vk][r, h % 2], kdeps=[kk], vdeps=[vk])

        io2 = dict(vecs=W["vecs"], MA=MA, MB=MB, mixT=mixT, acc=acc)

        def body2(K, C, io2=io2, ag_done=ag_done):
            for key, val in ag_done.items():
                K.P.external("ag" + key[2], val, [key])
            return phase2(K, C, io2)

        _phase(nc, f"L{l}b_", body2, ext_sems={"ag" + g: (agsem[g], ag_val[g]) for g in "ABC"})
        io3 = dict(vecs=W["vecs"], xT=x_cur, mixT=mixT, Wo=W["Wo"], Wup=W["Wup"], Wdown=W["Wdown"], xout=x_next)
        _phase(nc, f"L{l}c_", lambda K, C, io3=io3, l=l: phase3(K, C, io3, l == 1))
    return nc


def kernel(**inputs):
    x = np.asarray(inputs["x"], dtype=np.float32)
    if "fused" not in _PROGS:
        _PROGS["fused"] = build_fused()
    lws = [prep_layer(inputs, l) for l in range(2)]
    ropes = [rope_tables(j) for j in range(4)]
    masks = [attn_masks(j) for j in range(4)]
    in_maps = []
    for c in range(NCORE):
        j = c % 4
        m = dict(xT=x_to_core(x, c), rope=ropes[j], MA=masks[j][0], MB=masks[j][1])
        for l in range(2):
            for k in ("W1", "Wuq", "Wukv", "vecs", "Wo", "Wup", "Wdown"):
                m[f"{k}_{l}"] = lws[l][k]
        in_maps.append(m)
    res = _run(_PROGS["fused"], in_maps)
    out = np.empty((2, SEQ, D), np.float32)
    for c in range(NCORE):
        b, j = c // 4, c % 4
        xt = np.asarray(res[c]["xout"]).transpose(1, 0, 2).reshape(D, TL)
        out[b, core_positions(j), :] = xt.T
    return out
```

```python
import contextlib
import math
import numpy as np
import ml_dtypes
import concourse.bass as bass
import concourse.mybir as mybir
from concourse.bass_utils import run_bass_kernel_spmd

F32 = mybir.dt.float32
BF16 = mybir.dt.bfloat16
ALU = mybir.AluOpType
AF = mybir.ActivationFunctionType

D = 1024
SEQ = 8192
NCORE = 8
TL = 2048
NBLK = 16
DFF = 4096
OFF = dict(qa=0, ka=384, va=768, cq=1152, ckv=1536, kr=1664, qc=1696, kc=1952, vc=2208)
W1COLS = 3904
NVEC = 160

import os
PROBE_K128 = bool(os.environ.get("PROBE_K128"))
COMPUTE = ("pe", "act", "dve", "pool")
SEM_WRAP = 30000


class Op:
    __slots__ = ("eng", "fn", "deps", "idx", "sig", "dma_sem", "dma_cnt", "is_dma", "sigidx")

    def __init__(self, eng, fn, is_dma=False):
        self.eng = eng
        self.fn = fn
        self.deps = []
        self.sig = False
        self.is_dma = is_dma
        self.dma_sem = None
        self.dma_cnt = 0
        self.sigidx = -1
        self.idx = -1


class Prog:
    def __init__(self, nc, ext_sems=None):
        self.nc = nc
        self.ops = {e: [] for e in ("pe", "act", "dve", "pool", "sp")}
        self.lastw = {}
        self.readers = {}
        self.dma_sems = {}
        self.ext_sems = dict(ext_sems or {})
        for name, (h, base) in self.ext_sems.items():
            self.dma_sems[name] = base

    def external(self, semname, value, writes):
        o = Op("pool", None, is_dma=True)
        o.dma_sem = semname
        o.dma_cnt = value
        for k in writes:
            self.lastw[k] = o
            self.readers[k] = []
        return o

    def _add(self, op, reads, writes):
        deps = []
        for k in reads:
            w = self.lastw.get(k)
            if w is not None:
                deps.append(w)
        for k in writes:
            w = self.lastw.get(k)
            if w is not None:
                deps.append(w)
            deps.extend(self.readers.get(k, ()))
        seen = set()
        for d in deps:
            if d is op or id(d) in seen:
                continue
            seen.add(id(d))
            op.deps.append(d)
        for k in reads:
            lst = self.readers.setdefault(k, [])
            if not op.is_dma:
                for i, r in enumerate(lst):
                    if (not r.is_dma) and r.eng == op.eng:
                        lst[i] = op
                        break
                else:
                    lst.append(op)
            else:
                lst.append(op)
        for k in writes:
            self.lastw[k] = op
            self.readers[k] = []
        op.idx = len(self.ops[op.eng])
        self.ops[op.eng].append(op)
        return op

    def op(self, eng, fn, reads=(), writes=()):
        return self._add(Op(eng, fn), reads, writes)

    def dma(self, eng, semname, fn, reads=(), writes=(), inc=16):
        o = Op(eng, fn, is_dma=True)
        self.dma_sems[semname] = self.dma_sems.get(semname, 0) + inc
        o.dma_sem = semname
        o.dma_cnt = self.dma_sems[semname]
        o.sigidx = inc
        return self._add(o, reads, writes)

    @staticmethod
    def _skip_same(d, o):
        return (not d.is_dma) and d.eng == o.eng and (d.eng == "pe" or o.idx - d.idx > 2)

    def emit(self, final_wait_ops=(), tag="", managed=True):
        nc = self.nc
        for e, lst in self.ops.items():
            for o in lst:
                for d in o.deps:
                    if d.is_dma or self._skip_same(d, o):
                        continue
                    d.sig = True
        for o in final_wait_ops:
            if not o.is_dma:
                o.sig = True
        nsig = {}
        for e in COMPUTE:
            c = 0
            for o in self.ops[e]:
                if o.sig and not o.is_dma:
                    o.sigidx = c
                    c += 1
            nsig[e] = c
        with contextlib.ExitStack() as st:
            if managed:
                mksem = lambda nm: st.enter_context(nc.semaphore(tag + nm))
            else:
                mksem = lambda nm: nc.alloc_semaphore(name=tag + nm)
            csem = {}
            for e in COMPUTE:
                n = max(1, (nsig[e] + SEM_WRAP - 1) // SEM_WRAP)
                csem[e] = [mksem(f"c_{e}_{i}") for i in range(n)]
            dsem = {name: (self.ext_sems[name][0] if name in self.ext_sems else mksem(f"d_{name}")) for name in self.dma_sems}
            block = st.enter_context(nc.Block())
            engobj = {"pe": "tensor", "act": "scalar", "dve": "vector", "pool": "gpsimd", "sp": "sync"}

            def run_engine(e, eng):
                waited = {}
                for o in self.ops[e]:
                    need = {}
                    for d in o.deps:
                        if d.is_dma:
                            key = ("d", d.dma_sem)
                            val = d.dma_cnt
                            sem = dsem[d.dma_sem]
                        else:
                            if self._skip_same(d, o):
                                continue
                            si = d.sigidx // SEM_WRAP
                            key = ("c", d.eng, si)
                            val = d.sigidx % SEM_WRAP + 1
                            sem = csem[d.eng][si]
                        if waited.get(key, 0) >= val:
                            continue
                        if key not in need or need[key][1] < val:
                            need[key] = (sem, val)
                    for key, (sem, val) in need.items():
                        eng.wait_ge(sem, val)
                        waited[key] = val
                    ins = o.fn(eng)
                    if o.is_dma:
                        ins.then_inc(dsem[o.dma_sem], o.sigidx)
                    elif o.sig:
                        ins.then_inc(csem[e][o.sigidx // SEM_WRAP], 1)
                if e == "sp":
                    fin = {}
                    for o in final_wait_ops:
                        if o.is_dma:
                            key, sem, val = ("d", o.dma_sem), dsem[o.dma_sem], o.dma_cnt
                        else:
                            si = o.sigidx // SEM_WRAP
                            key, sem, val = ("c", o.eng, si), csem[o.eng][si], o.sigidx % SEM_WRAP + 1
                        if key not in fin or fin[key][1] < val:
                            fin[key] = (sem, val)
                    for sem, val in fin.values():
                        eng.wait_ge(sem, val)

            for e in ("sp", "pool", "act", "dve", "pe"):
                getattr(block, engobj[e])(lambda eng, e=e: run_engine(e, eng))


class Ctx:
    def __init__(self, nc, st, tag="", ext_sems=None):
        self.nc = nc
        self.st = st
        self.tag = tag
        self.P = Prog(nc, ext_sems)
        self.pairs = [st.enter_context(nc.psum_tensor(f"{tag}pbank{i}", [128, 2, 512], F32)) for i in range(4)]
        self.banks = [self.pairs[i // 2][:, i % 2, :] for i in range(8)]
        self.uid = 0

    def sb(self, name, shape, dt):
        return self.st.enter_context(self.nc.sbuf_tensor(self.tag + "s_" + name, shape, dt))

    def dram(self, name, shape, dt, kind):
        return self.nc.dram_tensor(name, shape, dt, kind=kind).ap()

    def mm(self, out, lhsT, rhs, start, stop, reads, writes, **kw):
        return self.P.op("pe", lambda e: e.matmul(out, lhsT, rhs, start=start, stop=stop, **kw), reads, writes)

    def act(self, out, in_, func, reads, writes, bias=None, scale=None):
        kw = {}
        if bias is not None:
            kw["bias"] = bias
        if scale is not None:
            kw["scale"] = scale
        return self.P.op("act", lambda e: e.activation(out=out, in_=in_, func=func, **kw), reads, writes)

    def tt(self, eng, out, in0, in1, op, reads, writes):
        return self.P.op(eng, lambda e: e.tensor_tensor(out=out, in0=in0, in1=in1, op=op), reads, writes)

    def ts(self, eng, out, in0, s1, op0, reads, writes, s2=None, op1=None):
        if op1 is None:
            return self.P.op(eng, lambda e: e.tensor_scalar(out=out, in0=in0, scalar1=s1, scalar2=None, op0=op0), reads, writes)
        return self.P.op(eng, lambda e: e.tensor_scalar(out=out, in0=in0, scalar1=s1, scalar2=s2, op0=op0, op1=op1), reads, writes)

    def stt(self, eng, out, in0, scalar, in1, op0, op1, reads, writes):
        return self.P.op(eng, lambda e: e.scalar_tensor_tensor(out=out, in0=in0, scalar=scalar, in1=in1, op0=op0, op1=op1), reads, writes)

    def copy(self, eng, out, in_, reads, writes):
        if eng == "act":
            return self.P.op("act", lambda e: e.copy(out=out, in_=in_), reads, writes)
        return self.P.op(eng, lambda e: e.tensor_copy(out=out, in_=in_), reads, writes)

    def recip(self, out, in_, reads, writes):
        return self.P.op("dve", lambda e: e.reciprocal(out=out, in_=in_), reads, writes)

    def memset(self, eng, ap, val, writes):
        return self.P.op(eng, lambda e: e.memset(ap, val), (), writes)

    def dma(self, q, sem, out, in_, reads, writes):
        return self.P.dma(q, sem, lambda e: e.dma_start(out=out, in_=in_), reads, writes)


class WStream:
    def __init__(self, K, slots, keys, items):
        self.K, self.slots, self.keys, self.items = K, slots, keys, items
        self.issued = 0
        self.cur = 0

    def _issue(self):
        i = self.issued
        if i >= len(self.items):
            return
        src, shape = self.items[i]
        n = 1
        for d in shape[1:]:
            n *= d
        ns = len(self.slots)
        dst = self.slots[i % ns][:, 0:n]
        if len(shape) == 3:
            dst = dst.rearrange("p (a b) -> p a b", a=shape[1])
        key = self.keys[i % ns]
        self.K.P.dma("pool", key, lambda e: e.dma_start(out=dst, in_=src), [], [key])
        self.views = getattr(self, "views", {})
        self.views[i] = (dst, key)
        self.issued += 1

    def next(self):
        while self.issued <= min(self.cur + 1, len(self.items) - 1):
            self._issue()
        v = self.views[self.cur]
        self.cur += 1
        return v


def make_consts(K):
    c = {}
    for name, val in (("c1024", 1.0 / 1024), ("c384", 1.0 / 384), ("c128", 1.0 / 128), ("c64", 1.0 / 64), ("one", 1.0)):
        t = K.sb(name, [128, 128], F32)
        K.memset("pool", t[:], val, [name])
        c[name] = t
    for name, val in (("eps6", 1e-6), ("eps5", 1e-5)):
        t = K.sb(name, [128, 1], F32)
        K.memset("pool", t[:], val, [name])
        c[name] = t
    return c


def rms_stats(K, C, srcs, src_keys, rows, cname, epsname, bank, bank_key, rstd, rstd_key, sq, sqkeys):
    n = len(srcs)
    for i, s in enumerate(srcs):
        q = sq[i % 2]
        K.act(q[0:rows, :], s, AF.Square, [src_keys[i]], [sqkeys[i % 2]])
        K.mm(bank[:, :], C[cname][0:rows, :], q[0:rows, :], i == 0, i == n - 1, [sqkeys[i % 2], cname], [bank_key])
    K.act(rstd, bank[:, :], AF.Sqrt, [bank_key, epsname], [rstd_key], bias=C[epsname][:, 0:1], scale=1.0)
    K.recip(rstd, rstd, [rstd_key], [rstd_key])


FM_GROUPS = [
    (0, 512, [("ropeA", 0, 128, "QA", 0), ("ropeA", 256, 128, "QA", 128)]),
    (512, 512, [("ropeA", 0, 128, "QA", 256), ("ropeA", 256, 128, "KA", 0)]),
    (1024, 512, [("ropeA", 0, 128, "KA", 128), ("ropeA", 256, 128, "KA", 256)]),
    (1536, 512, [("ropeC", 0, 128, "QC", 0), ("ropeC", 256, 128, "QC", 128)]),
    (2048, 512, [("ropeC", 0, 128, "KC", 0), ("ropeC", 256, 128, "KC", 128)]),
    (2560, 512, [("cq", 0, 128, None, 0), ("cq", 128, 128, None, 1), ("cq", 256, 128, None, 2), ("ckv", 384, 128, None, 0)]),
    (3072, 192, [("kr", 0, 96, None, 0)]),
]
TM_GROUPS = [(3264, 384, 6, "VA"), (3648, 256, 4, "VC")]


def phase1(K, C, io):
    P = K.P
    banks = K.banks
    xT = io["xT"]
    W1 = io["W1"]
    vec = K.sb("vec", [128, NVEC], F32)
    K.dma("sp", "vec", vec[:], io["vecs"], [], ["vec"])
    xs = [K.sb("xs0", [128, 8, 512], F32)] * 2
    hT = K.sb("hT", [128, 8, TL], BF16)
    rstd = [K.sb(f"rstd{i}", [128, 512], F32) for i in range(2)]
    sq = [K.sb(f"sq{i}", [128, 512], F32) for i in range(2)]
    sqk = ["sq0", "sq1"]
    wsl = [K.sb(f"wsl{i}", [128, 4096], BF16) for i in range(2)]
    t64c = K.sb("t64c", [128, TL], F32)
    t64s = K.sb("t64s", [128, TL], F32)
    t32c = K.sb("t32c", [128, TL], F32)
    t32s = K.sb("t32s", [128, TL], F32)
    NSTG = 6
    stg = [K.sb(f"stg{i}", [128, TL], BF16) for i in range(NSTG)]
    r1 = [K.sb(f"r1_{i}", [128, 512], F32) for i in range(2)]
    r2 = [K.sb(f"r2_{i}", [128, 512], F32) for i in range(2)]
    cqT = K.sb("cqT", [128, 3, TL], F32)
    ckvT = K.sb("ckvT", [128, TL], F32)
    cqn = K.sb("cqn", [128, 3, TL], BF16)
    ckvn = K.sb("ckvn", [128, TL], BF16)
    krT = K.sb("krT", [128, TL], BF16)
    vst = [K.sb("vst0", [128, 6, NBLK, 65], BF16)] * 2
    K.memset("pool", vst[0][:], 1.0, ["vst0"])

    for t in range(4):
        x = xs[t % 2]
        xk = "xs0"
        xka, xkb = xk + "a", xk + "b"
        K.dma("sp", xka, x[:, 0:4, :], xT[:, 0:4, t * 512:(t + 1) * 512], [], [xka])
        K.dma("act", xkb, x[:, 4:8, :], xT[:, 4:8, t * 512:(t + 1) * 512], [], [xkb])
        xkeys = [xka] * 4 + [xkb] * 4
        rs = rstd[t % 2]
        rk = f"rstd{t % 2}"
        rms_stats(K, C, [x[:, c, :] for c in range(8)], xkeys, 128, "c1024", "eps6", banks[7], "bank7", rs[:], rk, sq, sqk)
        for c in range(8):
            K.stt("dve", hT[:, c, t * 512:(t + 1) * 512], x[:, c, :], vec[:, c:c + 1], rs[:], ALU.mult, ALU.mult,
                  [xkeys[c], rk, "vec"], [("hT", t)])

    K.dma("sp", "t64c", t64c[:], io["rope"][0], [], ["t64c"])
    K.dma("sp", "t64s", t64s[:], io["rope"][1], [], ["t64s"])
    K.dma("sp", "t32c", t32c[:], io["rope"][2], [], ["t32c"])
    K.dma("sp", "t32s", t32s[:], io["rope"][3], [], ["t32s"])
    items = [(W1[:, :, goff:goff + gcols], [128, 8, gcols]) for (goff, gcols, _) in FM_GROUPS[5:7]]
    items += [(W1[:, :, goff:goff + gcols], [128, 8, gcols]) for (goff, gcols, _) in FM_GROUPS[0:5]]
    items += [(W1[:, :, goff:goff + gcols], [128, 8, gcols]) for (goff, gcols, _, _) in TM_GROUPS]
    items += [(io["Wuq"], [128, 3, 1152]), (io["Wukv"], [128, 768])]
    ws = WStream(K, wsl, ["wsl0", "wsl1"], items)
    wload = lambda src, shape: ws.next()

    pcnt = [0]
    scnt = [0]
    rcnt = [0]

    def rope(psA, psB, ka, kb, cosT, sinT, ck, sk, out, okey, p0, p1, t):
        i = rcnt[0] % 2
        rcnt[0] += 1
        sl = slice(t * 512, (t + 1) * 512)
        K.tt("dve", r1[i][p0:p1, :], psA[p0:p1, :], cosT[p0:p1, sl], ALU.mult, [ka, ck], [f"r1_{i}"])
        K.tt("dve", r2[i][p0:p1, :], psB[p0:p1, :], sinT[p0:p1, sl], ALU.mult, [kb, sk], [f"r2_{i}"])
        K.tt("pool", out, r1[i][p0:p1, :], r2[i][p0:p1, :], ALU.add, [f"r1_{i}", f"r2_{i}"], [okey])

    ag_pending = [p for p in io.get("ag_pieces", []) if p[0][2] != "B"]
    ag_ready = []

    def try_ag(flush=False):
        for item in ag_ready[:]:
            key, src, dst, deps = item
            K.P.dma("pool", "ag" + key[2], lambda e, src=src, dst=dst: e.collective_compute(
                "AllGather", ALU.bypass, replica_groups=RG, ins=[src], outs=[dst[:, :]]), deps, [key], inc=1)
            io["ag_ops"][key] = K.P.lastw[key]
            ag_ready.remove(item)
        for item in ag_pending[:]:
            if all(k in K.P.lastw for k in item[3]):
                ag_pending.remove(item)
                ag_ready.append(item)
        if flush and (ag_ready or ag_pending):
            try_ag(flush=True)

    def fm_groups(groups):
        for (goff, gcols, tiles) in groups:
            w, wk = wload(W1[:, :, goff:goff + gcols], [128, 8, gcols])
            for (kind, loff, M, oname, orow) in tiles:
                roped = kind in ("ropeA", "ropeC", "kr")
                si = scnt[0] % NSTG
                if kind in ("ropeA", "ropeC"):
                    scnt[0] += 1
                for t in range(4):
                    sl = slice(t * 512, (t + 1) * 512)
                    ba = pcnt[0] % 4
                    pcnt[0] += 1
                    psA = banks[ba]
                    ka = f"bank{ba}"
                    for c in range(8):
                        K.mm(psA[0:M, :], w[:, c, loff:loff + M], hT[:, c, sl], c == 0, c == 7, [wk, ("hT", t)], [ka])
                    if roped:
                        bb = pcnt[0] % 4
                        pcnt[0] += 1
                        psB = banks[bb]
                        kb = f"bank{bb}"
                        so = loff + (128 if kind != "kr" else 96)
                        for c in range(8):
                            K.mm(psB[0:M, :], w[:, c, so:so + M], hT[:, c, sl], c == 0, c == 7, [wk, ("hT", t)], [kb])
                    if kind == "ropeA":
                        rope(psA, psB, ka, kb, t64c, t64s, "t64c", "t64s", stg[si][:, sl], f"stg{si}", 0, 128, t)
                    elif kind == "ropeC":
                        rope(psA, psB, ka, kb, t32c, t32s, "t32c", "t32s", stg[si][:, sl], f"stg{si}", 0, 128, t)
                    elif kind == "kr":
                        rope(psA, psB, ka, kb, t32c, t32s, "t32c", "t32s", krT[64:96, sl], "krT", 64, 96, t)
                    elif kind == "cq":
                        K.copy("act", cqT[:, orow, sl], psA[:, :], [ka], [("cqT", t)])
                    elif kind == "ckv":
                        K.copy("act", ckvT[:, sl], psA[:, :], [ka], [("ckvT", t)])
                if kind in ("ropeA", "ropeC"):
                    K.dma("sp", f"stg{si}", io[oname][orow:orow + 128, :], stg[si][:, :], [f"stg{si}"], [(oname, orow)])
                    try_ag()


    def tm_groups():
        vcnt = 0
        for (goff, gcols, nh, oname) in TM_GROUPS:
            w, wk = wload(W1[:, :, goff:goff + gcols], [128, 8, gcols])
            vi = vcnt % 2
            vcnt += 1
            vk = "vst0"
            for m in range(NBLK):
                bi = 4 + (m % 2)
                ps = banks[bi]
                pk = f"bank{bi}"
                t = m // 4
                for c in range(8):
                    K.mm(ps[:, 0:gcols], hT[:, c, m * 128:(m + 1) * 128], w[:, c, 0:gcols], c == 0, c == 7, [wk, ("hT", t)], [pk])
                K.copy("act" if m % 2 == 0 else "dve", vst[vi][:, 0:nh, m, 0:64],
                       ps[:, 0:gcols].rearrange("p (h d) -> p h d", h=nh), [pk], [vk])
            for h in range(nh):
                K.dma("sp", f"{vk}_{h // (3 if nh == 6 else 2)}", io[oname][h], vst[vi][:, h, :, :], [vk], [(oname, h)])
            try_ag()

        return vcnt

    def mla_norms():
        for t in range(4):
            sl = slice(t * 512, (t + 1) * 512)
            rs = rstd[t % 2]
            rk = f"rstd{t % 2}"
            rms_stats(K, C, [cqT[:, c, sl] for c in range(3)], [("cqT", t)] * 3, 128, "c384", "eps6", banks[7], "bank7", rs[:], rk, sq, sqk)
            for c in range(3):
                K.stt("dve", cqn[:, c, sl], cqT[:, c, sl], vec[:, 24 + c:25 + c], rs[:], ALU.mult, ALU.mult,
                      [("cqT", t), rk, "vec"], [("cqn", t)])
            rs = rstd[(t + 1) % 2]
            rk = f"rstd{(t + 1) % 2}"
            rms_stats(K, C, [ckvT[:, sl]], [("ckvT", t)], 128, "c128", "eps6", banks[6], "bank6", rs[:], rk, sq, sqk)
            K.stt("dve", ckvn[:, sl], ckvT[:, sl], vec[:, 27:28], rs[:], ALU.mult, ALU.mult, [("ckvT", t), rk, "vec"], [("ckvn", t)])

    def mla(vcnt):
        wq, wqk = wload(io["Wuq"], [128, 3, 1152])
        for h in range(6):
            si = scnt[0] % NSTG
            scnt[0] += 1
            for t in range(4):
                sl = slice(t * 512, (t + 1) * 512)
                ba = pcnt[0] % 4
                pcnt[0] += 1
                bb = pcnt[0] % 4
                pcnt[0] += 1
                psA, psB = banks[ba], banks[bb]
                ka, kb = f"bank{ba}", f"bank{bb}"
                for c in range(3):
                    K.mm(psA[0:96, :], wq[:, c, h * 96:(h + 1) * 96], cqn[:, c, sl], c == 0, c == 2, [wqk, ("cqn", t)], [ka])
                for c in range(3):
                    K.mm(psB[0:96, :], wq[:, c, 576 + h * 96:576 + (h + 1) * 96], cqn[:, c, sl], c == 0, c == 2, [wqk, ("cqn", t)], [kb])
                K.copy("act", stg[si][0:64, sl], psA[0:64, :], [ka], [f"stg{si}"])
                rope(psA, psB, ka, kb, t32c, t32s, "t32c", "t32s", stg[si][64:96, sl], f"stg{si}", 64, 96, t)
            K.dma("sp", f"stg{si}", io["QB"][h], stg[si][0:96, :], [f"stg{si}"], [("QB", h)])

        wkv, wkvk = wload(io["Wukv"], [128, 768])
        for h in range(6):
            si = scnt[0] % NSTG
            scnt[0] += 1
            for t in range(4):
                sl = slice(t * 512, (t + 1) * 512)
                ba = pcnt[0] % 4
                pcnt[0] += 1
                psA = banks[ba]
                ka = f"bank{ba}"
                K.mm(psA[0:64, :], wkv[:, h * 64:(h + 1) * 64], ckvn[:, sl], True, True, [wkvk, ("ckvn", t)], [ka])
                K.copy("act" if t % 2 == 0 else "dve", stg[si][0:64, sl], psA[0:64, :], [ka], [f"stg{si}"])
            K.dma("sp", f"stg{si}", io["KB"][h, 0:64, :], stg[si][0:64, :], [f"stg{si}"], [("KB", h)])
            K.dma("sp", f"krT_out{h // 2}", io["KB"][h, 64:96, :], krT[64:96, :], ["krT"], [("KBr", h)])
        vi = vcnt % 2
        vcnt += 1
        vk = "vst0"
        for m in range(NBLK):
            bi = 4 + (m % 2)
            ps = banks[bi]
            pk = f"bank{bi}"
            t = m // 4
            K.mm(ps[:, 0:384], ckvn[:, m * 128:(m + 1) * 128], wkv[:, 384:768], True, True, [wkvk, ("ckvn", t)], [pk])
            K.copy("act" if m % 2 == 0 else "dve", vst[vi][:, 0:6, m, 0:64],
                   ps[:, 0:384].rearrange("p (h d) -> p h d", h=6), [pk], [vk])
        for h in range(6):
            K.dma("sp", f"{vk}_{h // 3}", io["VB"][h], vst[vi][:, h, :, :], [vk], [("VB", h)])
        try_ag()

    fm_groups(FM_GROUPS[5:7])
    mla_norms()
    fm_groups(FM_GROUPS[0:5])
    vc = tm_groups()
    mla(vc)
    try_ag(flush=True)


P1_OUT = dict(QA=[384, TL], KA=[384, TL], QC=[256, TL], KC=[256, TL], QB=[6, 96, TL], KB=[6, 96, TL],
              VA=[6, 128, NBLK, 65], VB=[6, 128, NBLK, 65], VC=[4, 128, NBLK, 65])


def build_p1():
    nc = bass.Bass("TRN2", target_bir_lowering=False)
    with contextlib.ExitStack() as st:
        K = Ctx(nc, st)
        io = {}
        io["xT"] = K.dram("xT", [128, 8, TL], F32, "ExternalInput")
        io["W1"] = K.dram("W1", [128, 8, W1COLS], F32, "ExternalInput")
        io["Wuq"] = K.dram("Wuq", [128, 3, 1152], F32, "ExternalInput")
        io["Wukv"] = K.dram("Wukv", [128, 768], F32, "ExternalInput")
        io["vecs"] = K.dram("vecs", [128, NVEC], F32, "ExternalInput")
        rope = K.dram("rope", [4, 128, TL], F32, "ExternalInput")
        io["rope"] = [rope[i] for i in range(4)]
        for name, shape in P1_OUT.items():
            io[name] = K.dram(name, shape, BF16, "ExternalOutput")
        C = make_consts(K)
        phase1(K, C, io)
        outs = [o for o in K.P.ops["sp"] if o.is_dma and o.dma_sem.startswith(("stg", "vst", "krT_out"))]
        K.P.emit(final_wait_ops=outs)
    return nc


def _swap_cols(cols, blk):
    cols = np.asarray(cols)
    out = cols.copy().reshape(-1, blk)
    half = blk // 2
    out = np.concatenate([out[:, half:], out[:, :half]], axis=1)
    return out.reshape(-1)


def w1_columns():
    cols = []
    for base in (OFF["qa"], OFF["ka"]):
        for t in range(3):
            c = np.arange(base + t * 128, base + (t + 1) * 128)
            cols += [c, _swap_cols(c, 64)]
    for base in (OFF["qc"], OFF["kc"]):
        for t in range(2):
            c = np.arange(base + t * 128, base + (t + 1) * 128)
            cols += [c, _swap_cols(c, 32)]
    cols.append(np.arange(OFF["cq"], OFF["cq"] + 384))
    cols.append(np.arange(OFF["ckv"], OFF["ckv"] + 128))
    filler = np.arange(OFF["ckv"], OFF["ckv"] + 64)
    kr = np.arange(OFF["kr"], OFF["kr"] + 32)
    cols += [filler, kr, filler, _swap_cols(kr, 32)]
    cols.append(np.arange(OFF["va"], OFF["va"] + 384))
    cols.append(np.arange(OFF["vc"], OFF["vc"] + 256))
    cols = np.concatenate(cols)
    assert cols.shape[0] == W1COLS
    return cols


def chunked(w, nchunk):
    n = w.shape[1]
    return np.ascontiguousarray(w.reshape(nchunk, 128, n).transpose(1, 0, 2))


def prep_layer(inp, l):
    f = lambda a: np.asarray(a, dtype=np.float32)
    out = {}
    w_in = f(inp["w_in"][l])
    out["W1"] = chunked(w_in[:, w1_columns()], 8)
    wuq = f(inp["mla_w_uq"][l])
    ncols, scols = [], []
    for h in range(6):
        c = np.arange(h * 96, (h + 1) * 96)
        ncols.append(c)
        scols.append(np.concatenate([c[:64], _swap_cols(c[64:], 32)]))
    out["Wuq"] = chunked(wuq[:, np.concatenate(ncols + scols)], 3)
    wukv = f(inp["mla_w_ukv"][l])
    kc = np.concatenate([np.arange(h * 128, h * 128 + 64) for h in range(6)])
    vc = np.concatenate([np.arange(h * 128 + 64, h * 128 + 128) for h in range(6)])
    out["Wukv"] = np.ascontiguousarray(wukv[:, np.concatenate([kc, vc])])
    vec = np.zeros((128, NVEC), np.float32)
    vec[:, 0:8] = f(inp["ln1_g"][l]).reshape(8, 128).T
    vec[:, 8:16] = f(inp["ln2_g"][l]).reshape(8, 128).T
    vec[:, 16:24] = f(inp["final_g"]).reshape(8, 128).T
    vec[:, 24:27] = f(inp["mla_q_norm_g"][l]).reshape(3, 128).T
    vec[:, 27] = f(inp["mla_kv_norm_g"][l])
    vec[:, 28] = np.tile(f(inp["diff_subln_g"][l]), 2)
    lam_init = 0.8 - 0.6 * math.exp(-0.3 * l)
    vec[:, 29] = lam_init
    vec[:, 30] = 1.0 - lam_init
    vec[:, 32:160] = f(inp["diff_lambda"][l]).reshape(1, 128)
    out["vecs"] = vec
    out["Wo"] = chunked(f(inp["w_o"][l]), 8)
    out["Wup"] = chunked(f(inp["w_up"][l]), 8)
    out["Wdown"] = chunked(f(inp["w_down"][l]), 32)
    return out


def core_positions(j):
    m = np.arange(NBLK)[:, None]
    t = np.arange(128)[None, :]
    return ((4 * m + j) * 128 + t).reshape(-1)


def rope_tables(j):
    pos = core_positions(j).astype(np.float32)
    tabs = []
    for dim in (64, 32):
        half = dim // 2
        inv = (1.0 / (np.float32(10000.0) ** (np.arange(half, dtype=np.float32) / np.float32(half)))).astype(np.float32)
        r = np.arange(128)
        i = r % dim
        fidx = i % half
        ang = (pos[None, :] * inv[fidx][:, None]).astype(np.float32)
        sign = np.where(i < half, -1.0, 1.0).astype(np.float32)[:, None]
        tabs.append(np.cos(ang).astype(np.float32))
        tabs.append((np.sin(ang) * sign).astype(np.float32))
    return np.stack(tabs, 0)


def x_to_core(x, c):
    b, j = c // 4, c % 4
    xs = np.asarray(x[b], dtype=np.float32)[core_positions(j)]
    return chunked(np.ascontiguousarray(xs.T), 8)


def a_cols(ak):
    return max(0, ak) * 128, min(4, ak + 5) * 128


def phase2(K, C, io):
    P = K.P
    banks = K.banks
    vec = K.sb("vec", [128, NVEC], F32)
    K.dma("sp", "vec", vec[:], io["vecs"], [], ["vec"])
    MAt = K.sb("MAt", [128, 32, 512], BF16)
    MBt = K.sb("MBt", [128, 16, 512], BF16)

    kt = [K.sb(f"kt{i}", [128, 4, TL], BF16) for i in range(2)]
    vt = [K.sb(f"vt{i}", [128, 4 * NBLK, 65], BF16) for i in range(2)]
    qa = [K.sb(f"qa{i}", [128, TL], BF16) for i in range(2)]
    qb = [K.sb(f"qb{i}", [128, TL], BF16) for i in range(2)]
    qc = [K.sb(f"qc{i}", [128, 2, TL], BF16) for i in range(2)]
    for qq, nm in ((qa, "qa"), (qc, "qc"), (qb, "qb")):
        for i in range(2):
            K.memset("pool", qq[i][:], 0.0, [f"{nm}{i}"])
    NPT = 6
    LAG = 3
    NSB = 3
    pt = [K.sb(f"pt{i}", [128, 2, 512], BF16) for i in range(NPT)]
    osb = [K.sb(f"osb{i}", [128, 512], F32) for i in range(4)]
    rz = [K.sb(f"rz{i}", [128, 512], F32) for i in range(4)]
    ones512 = K.sb("ones512", [128, 512], F32)
    K.memset("pool", ones512[:], -1.0, ["ones512"])
    deferred = []
    deferred_a = []
    tA = K.sb("tA", [128, 512], F32)
    tB = K.sb("tB", [128, 512], F32)
    tO = K.sb("tO", [128, 512], F32)
    tS = K.sb("tS", [128, 512], F32)
    ostg = [K.sb(f"ostg{i}", [128, TL], BF16) for i in range(2)]
    lam = K.sb("lam", [128, 40], F32)

    K.tt("dve", lam[:, 0:32], vec[:, 32:64], vec[:, 64:96], ALU.mult, ["vec"], ["lam"])
    P.op("dve", lambda e: e.reduce_sum(out=lam[:, 32:33], in_=lam[:, 0:32], axis=mybir.AxisListType.X), ["lam"], ["lam1"])
    K.tt("dve", lam[:, 0:32], vec[:, 96:128], vec[:, 128:160], ALU.mult, ["vec", "lam1"], ["lam"])
    P.op("dve", lambda e: e.reduce_sum(out=lam[:, 33:34], in_=lam[:, 0:32], axis=mybir.AxisListType.X), ["lam"], ["lam2"])
    K.act(lam[:, 34:36], lam[:, 32:34], AF.Exp, ["lam1", "lam2"], ["lam3"])
    K.tt("dve", lam[:, 36:37], lam[:, 35:36], lam[:, 34:35], ALU.subtract, ["lam3"], ["lam4"])
    K.tt("dve", lam[:, 37:38], lam[:, 36:37], vec[:, 29:30], ALU.subtract, ["lam4", "vec"], ["neglam"])
    K.tt("dve", lam[:, 38:39], vec[:, 28:29], vec[:, 30:31], ALU.mult, ["vec"], ["gc"])
    neglam = lam[:, 37:38]
    gc = lam[:, 38:39]

    jobs = [("A", h) for h in range(6)] + [("C", h) for h in range(4)] + [("B", h) for h in range(6)]

    jobslot = {}

    def load_job(ji):
        kind, h = jobs[ji]
        s = ji % 2
        a = io["acc"](kind, h)
        rows = a["rows"]
        if kind == "A":
            qs = h % 2
            K.dma("sp", f"qa{qs}", qa[qs][qs * 64:qs * 64 + 64, :], a["q"], [], [f"qa{qs}"])
            jobslot[ji] = [(qa[qs], f"qa{qs}", None)]
        elif kind == "B":
            qs = h % 2
            K.dma("sp", f"qb{qs}", qb[qs][0:96, :], a["q"], [], [f"qb{qs}"])
            jobslot[ji] = [(qb[qs], f"qb{qs}", None)]
        else:
            qs = h % 2
            for mi in range(2):
                p0 = qs * 64 + mi * 32
                K.dma("sp", f"qc{qs}", qc[qs][p0:p0 + 32, mi, :], a["q"][mi * 32:(mi + 1) * 32, :], [], [f"qc{qs}"])
            jobslot[ji] = [(qc[qs], f"qc{qs}", 0), (qc[qs], f"qc{qs}", 1)]
        for r in range(4):
            K.dma("sp", f"kt{s}", kt[s][0:rows, r, :], a["k"](r), a["kdeps"], [f"kt{s}"])
        for r in range(4):
            K.dma("sp", f"vt{s}", vt[s][:, r * NBLK:(r + 1) * NBLK, :], a["v"](r), a["vdeps"], [f"vt{s}"])

    cnt = dict(s=0, p=0, o=0, f=0)
    out_ops = []
    K.dma("sp", "MAt_hi", MAt[:, 16:32, :], io["MA"][:, 16:32, :], [], ["MAt_hi"])
    load_job(0)
    K.dma("sp", "MAt_lo", MAt[:, 0:16, :], io["MA"][:, 0:16, :], [], ["MAt_lo"])
    K.dma("sp", "MBt", MBt[:], io["MB"], [], ["MBt"])
    for ji, (kind, h) in enumerate(jobs):
        if ji + 1 < len(jobs):
            load_job(ji + 1)
        s = ji % 2
        kk, vk = f"kt{s}", f"vt{s}"
        maps = jobslot[ji]
        if kind == "A":
            scale, row0 = 64 ** -0.5, h * 64
        elif kind == "B":
            scale, row0 = 96 ** -0.5, 384 + h * 64
        else:
            scale, row0 = 32 ** -0.5, 768 + h * 64
        og = ostg[ji % 2]
        ogk = f"ostg{ji % 2}"
        for g in range(4):
            blocks = []
            if kind == "A":
                for ak in (0, -1, -2, -3, -4, 1, 2, 3):
                    mloc = 4 * g + ak
                    if mloc < 0:
                        continue
                    c0, c1 = a_cols(ak)
                    for r in range(4):
                        blocks.append((r, mloc, c0, c1, MAt[:, (ak + 4) * 4 + r, :], "MAt_hi" if ak >= 0 else "MAt_lo"))
            else:
                for mloc in range(4 * g + 4):
                    ak = mloc - 4 * g
                    for r in range(4):
                        if ak < 0:
                            blocks.append((r, mloc, 0, 512, None, None))
                        else:
                            blocks.append((r, mloc, ak * 128, 512, MBt[:, ak * 4 + r, :], "MBt"))
            steps = [(b, mi) for b in blocks for mi in range(len(maps))]
            O = [banks[6 + mi] for mi in range(len(maps))]
            Ok = [f"bank{6 + mi}" for mi in range(len(maps))]
            pend = []
            nst = len(steps)
            first = [True] * len(maps)
            lastidx = [max(i for i, (b, mi) in enumerate(steps) if mi == m_) for m_ in range(len(maps))]
            assert nst % 2 == 0
            npair = nst // 2
            for ip in range(npair + LAG):
                if ip == min(1, npair - 1) and deferred_a:
                    for f in deferred_a:
                        f()
                    deferred_a.clear()
                if ip == min(7, npair - 1) and deferred:
                    for f in deferred:
                        f()
                    deferred.clear()
                if ip < npair:
                    sb_ = cnt["s"] % NSB
                    cnt["s"] += 1
                    pi = cnt["p"] % NPT
                    cnt["p"] += 1
                    SP = K.pairs[sb_]
                    spk, ptk = f"sp{sb_}", f"pt{pi}"
                    c0, c1 = steps[2 * ip][0][2], steps[2 * ip][0][3]
                    assert (steps[2 * ip + 1][0][2], steps[2 * ip + 1][0][3]) == (c0, c1)
                    for hf in range(2):
                        (r, mloc, _, _, mask, mkey), mi = steps[2 * ip + hf]
                        qbuf, qk, qm = maps[mi]
                        qsrc = qbuf[:, g * 512 + c0:g * 512 + c1] if qm is None else qbuf[:, qm, g * 512 + c0:g * 512 + c1]
                        K.mm(SP[:, hf, c0:c1], kt[s][:, r, mloc * 128:(mloc + 1) * 128], qsrc, True, True, [kk, qk], [spk])
                    K.act(pt[pi][:, :, c0:c1], SP[:, :, c0:c1], AF.Exp, [spk], [ptk], scale=float(scale))
                    for hf in range(2):
                        (r, mloc, _, _, mask, mkey), mi = steps[2 * ip + hf]
                        if mask is not None:
                            K.tt("dve", pt[pi][:, hf, c0:c1], pt[pi][:, hf, c0:c1], mask[:, c0:c1], ALU.mult, [ptk, mkey], [ptk])
                    pend.append((ip, c0, c1, pi))
                if ip >= LAG:
                    (ip0, c0, c1, pi) = pend.pop(0)
                    for hf in range(2):
                        i0 = 2 * ip0 + hf
                        (r, mloc, _, _, mask, mkey), mi = steps[i0]
                        K.mm(O[mi][0:65, c0:c1], vt[s][:, r * NBLK + mloc, 0:65], pt[pi][:, hf, c0:c1], first[mi], i0 == lastidx[mi],
                             [vk, f"pt{pi}"], [Ok[mi]])
                        first[mi] = False
            gsl = slice(g * 512, (g + 1) * 512)

            def fbank():
                b = cnt["s"] % NSB
                cnt["s"] += 1
                return K.pairs[b][:, 0, :], f"sp{b}"

            def prec(i):
                K.recip(rz[i][64:65, :], osb[i][64:65, :], [f"osb{i}"], [f"rz{i}"])

            if kind in ("A", "B"):
                fi = cnt["f"] % 4
                cnt["f"] += 1
                K.copy("act", osb[fi][0:65, :], O[0][0:65, :], [Ok[0]], [f"osb{fi}"])

                def part2a(fi=fi):
                    K.recip(rz[fi][64:65, :], osb[fi][64:65, :], [f"osb{fi}"], [f"rz{fi}"])

                def part2(fi=fi, og=og, ogk=ogk, gsl=gsl):
                    fb, fk = fbank()
                    K.mm(fb[0:64, :], C["one"][64:65, 0:64], rz[fi][64:65, :], True, True, [f"rz{fi}", "one"], [fk])
                    K.tt("dve", og[0:64, gsl], osb[fi][0:64, :], fb[0:64, :], ALU.mult, [f"osb{fi}", fk], [ogk])
            else:
                f0 = cnt["f"] % 4
                f1 = (cnt["f"] + 1) % 4
                cnt["f"] += 2
                K.copy("act", osb[f0][0:65, :], O[0][0:65, :], [Ok[0]], [f"osb{f0}"])
                K.copy("act", osb[f1][0:65, :], O[1][0:65, :], [Ok[1]], [f"osb{f1}"])

                def part2a(f0=f0, f1=f1):
                    K.recip(rz[f0][64:65, :], osb[f0][64:65, :], [f"osb{f0}"], [f"rz{f0}"])
                    K.recip(rz[f1][64:65, :], osb[f1][64:65, :], [f"osb{f1}"], [f"rz{f1}"])

                def part2(f0=f0, f1=f1, og=og, ogk=ogk, gsl=gsl):
                    fb, fk = fbank()
                    K.mm(fb[0:64, :], C["one"][64:65, 0:64], rz[f0][64:65, :], True, True, [f"rz{f0}", "one"], [fk])
                    K.tt("dve", tA[0:64, :], osb[f0][0:64, :], fb[0:64, :], ALU.mult, [f"osb{f0}", fk], ["tA"])
                    fb, fk = fbank()
                    K.mm(fb[0:64, :], C["one"][64:65, 0:64], rz[f1][64:65, :], True, True, [f"rz{f1}", "one"], [fk])
                    K.tt("dve", tB[0:64, :], osb[f1][0:64, :], fb[0:64, :], ALU.mult, [f"osb{f1}", fk], ["tB"])
                    K.stt("dve", tO[0:64, :], tB[0:64, :], neglam[0:64, 0:1], tA[0:64, :], ALU.mult, ALU.add, ["tA", "tB", "neglam"], ["tO"])
                    K.tt("pool", tS[0:64, :], tO[0:64, :], tO[0:64, :], ALU.mult, ["tO"], ["tS"])
                    fb, fk = fbank()
                    K.mm(fb[0:64, :], C["c64"][0:64, 0:64], tS[0:64, :], True, True, ["tS", "c64"], [fk])
                    K.act(tS[0:64, :], fb[0:64, :], AF.Sqrt, [fk, "eps5"], ["tS"], bias=C["eps5"][0:64, 0:1], scale=1.0)
                    K.recip(tS[0:64, :], tS[0:64, :], ["tS"], ["tS"])
                    K.stt("dve", og[0:64, gsl], tO[0:64, :], gc[0:64, 0:1], tS[0:64, :], ALU.mult, ALU.mult, ["tO", "tS", "gc"], [ogk])
            deferred_a.append(part2a)
            deferred.append(part2)
        deferred.append(lambda og=og, ogk=ogk, row0=row0: out_ops.append(
            K.dma("sp", ogk, io["mixT"][row0:row0 + 64, :], og[0:64, :], [ogk], [("mixT", row0)])))
    for f in deferred_a + deferred:
        f()
    deferred.clear()
    return out_ops


def simple_acc(io):
    def acc(kind, h):
        if kind == "A":
            return dict(rows=128, q=io["QA"][h * 64:(h + 1) * 64, :], k=lambda r: io["KAg"][r, (h // 2) * 128:(h // 2 + 1) * 128, :],
                        v=lambda r: io["VAg"][r, h], kdeps=[], vdeps=[])
        if kind == "B":
            return dict(rows=96, q=io["QB"][h], k=lambda r: io["KBg"][r, h], v=lambda r: io["VBg"][r, h], kdeps=[], vdeps=[])
        return dict(rows=128, q=io["QC"][h * 64:(h + 1) * 64, :], k=lambda r: io["KCg"][r, (h // 2) * 128:(h // 2 + 1) * 128, :],
                    v=lambda r: io["VCg"][r, h], kdeps=[], vdeps=[])
    return acc


def build_p2():
    nc = bass.Bass("TRN2", target_bir_lowering=False)
    with contextlib.ExitStack() as st:
        K = Ctx(nc, st)
        io = {}
        io["vecs"] = K.dram("vecs", [128, NVEC], F32, "ExternalInput")
        io["MA"] = K.dram("MA", [128, 32, 512], BF16, "ExternalInput")
        io["MB"] = K.dram("MB", [128, 16, 512], BF16, "ExternalInput")
        io["QA"] = K.dram("QA", [384, TL], BF16, "ExternalInput")
        io["QB"] = K.dram("QB", [6, 96, TL], BF16, "ExternalInput")
        io["QC"] = K.dram("QC", [256, TL], BF16, "ExternalInput")
        io["KAg"] = K.dram("KAg", [4, 384, TL], BF16, "ExternalInput")
        io["KBg"] = K.dram("KBg", [4, 6, 96, TL], BF16, "ExternalInput")
        io["KCg"] = K.dram("KCg", [4, 256, TL], BF16, "ExternalInput")
        io["VAg"] = K.dram("VAg", [4, 6, 128, NBLK, 65], BF16, "ExternalInput")
        io["VBg"] = K.dram("VBg", [4, 6, 128, NBLK, 65], BF16, "ExternalInput")
        io["VCg"] = K.dram("VCg", [4, 4, 128, NBLK, 65], BF16, "ExternalInput")
        io["mixT"] = K.dram("mixT", [1024, TL], BF16, "ExternalOutput")
        C = make_consts(K)
        io["acc"] = simple_acc(io)
        outs = phase2(K, C, io)
        K.P.emit(final_wait_ops=outs)
    return nc


def attn_masks(j):
    tk = np.arange(128)[:, None, None]
    aq = np.arange(4)[None, :, None]
    tq = np.arange(128)[None, None, :]
    MA = np.zeros((128, 32, 4, 128), np.float32)
    MB = np.zeros((128, 16, 4, 128), np.float32)
    for ak in range(-4, 4):
        for r in range(4):
            dist = (4 * (aq - ak) + j - r) * 128 + tq - tk
            cnt = ((dist >= 0) & (dist <= 128)).astype(np.float32)
            cnt += ((dist >= 0) & (dist <= 512) & (dist % 4 == 0))
            cnt += ((dist >= 0) & (dist <= 2048) & (dist % 16 == 0))
            MA[:, (ak + 4) * 4 + r] = cnt
            if ak >= 0:
                MB[:, ak * 4 + r] = (dist >= 0)
    return (MA.reshape(128, 32, 512).astype(ml_dtypes.bfloat16), MB.reshape(128, 16, 512).astype(ml_dtypes.bfloat16))


def phase3(K, C, io, last):
    P = K.P
    banks = K.banks
    vec = K.sb("vec", [128, NVEC], F32)
    K.dma("sp", "vec", vec[:], io["vecs"], [], ["vec"])
    xT = K.sb("xT", [128, 8, TL], F32)
    mx = K.sb("mx", [128, 8, TL], BF16)
    aT = [K.sb(f"aT{i}", [128, 4, TL], BF16) for i in range(2)]
    wsl = [K.sb(f"wsl{i}", [128, 4096], BF16) for i in range(2)]
    rl = [K.sb(f"rl{i}", [128, 512], F32) for i in range(2)]
    sq = [K.sb(f"sq{i}", [128, 512], F32) for i in range(2)]
    sqk = ["sq0", "sq1"]
    rstd = [K.sb(f"rstd{i}", [128, 512], F32) for i in range(2)]
    mixv = io["mixT"].rearrange("(c p) t -> p c t", p=128)
    for t in range(4):
        sl = slice(t * 512, (t + 1) * 512)
        K.dma("sp", f"mx{t}", mx[:, :, sl], mixv[:, :, sl], [], [("mx", t)])
        K.dma("sp", f"xT{t}", xT[:, :, sl], io["xT"][:, :, sl], [], [("xT", t)])
    items = [(io["Wo"][:, :, hf * 512:(hf + 1) * 512], [128, 8, 512]) for hf in range(2)]
    for fg in range(8):
        items.append((io["Wup"][:, :, fg * 512:(fg + 1) * 512], [128, 8, 512]))
        items.append((io["Wdown"][:, fg * 4:(fg + 1) * 4, :], [128, 4, 1024]))
    ws = WStream(K, wsl, ["wsl0", "wsl1"], items)
    pc = [0]

    def nbank():
        b = pc[0] % 4
        pc[0] += 1
        return banks[b], f"bank{b}"

    for hf in range(2):
        w, wk = ws.next()
        for dl in range(4):
            d = hf * 4 + dl
            for t in range(4):
                sl = slice(t * 512, (t + 1) * 512)
                ps, pk = nbank()
                for c in range(8):
                    K.mm(ps[:, :], w[:, c, dl * 128:(dl + 1) * 128], mx[:, c, sl], c == 0, c == 7, [wk, ("mx", t)], [pk])
                K.tt("dve", xT[:, d, sl], ps[:, :], xT[:, d, sl], ALU.add, [pk, ("xT", t)], [("xT", t)])
    for t in range(4):
        sl = slice(t * 512, (t + 1) * 512)
        rs, rk = rstd[t % 2], f"rstd{t % 2}"
        rms_stats(K, C, [xT[:, c, sl] for c in range(8)], [("xT", t)] * 8, 128, "c1024", "eps6", banks[7], "bank7", rs[:], rk, sq, sqk)
        for c in range(8):
            K.stt("dve", mx[:, c, sl], xT[:, c, sl], vec[:, 8 + c:9 + c], rs[:], ALU.mult, ALU.mult, [("xT", t), rk, "vec"], [("mx", t)])
    rc = 0
    for fg in range(8):
        wu, wuk = ws.next()
        a = aT[fg % 2]
        ak = f"aT{fg % 2}"
        for fl in range(4):
            for t in range(4):
                sl = slice(t * 512, (t + 1) * 512)
                ps, pk = nbank()
                for c in range(8):
                    K.mm(ps[:, :], wu[:, c, fl * 128:(fl + 1) * 128], mx[:, c, sl], c == 0, c == 7, [wuk, ("mx", t)], [pk])
                ri = rc % 2
                rc += 1
                K.act(rl[ri][:, :], ps[:, :], AF.Relu, [pk], [f"rl{ri}"])
                K.tt("pool", a[:, fl, sl], rl[ri][:, :], rl[ri][:, :], ALU.mult, [f"rl{ri}"], [(ak, t)])
        wd, wdk = ws.next()
        for d in range(8):
            for t in range(4):
                sl = slice(t * 512, (t + 1) * 512)
                ps, pk = nbank()
                for fl in range(4):
                    K.mm(ps[:, :], wd[:, fl, d * 128:(d + 1) * 128], a[:, fl, sl], fl == 0, fl == 3, [wdk, (ak, t)], [pk])
                K.tt("dve", xT[:, d, sl], ps[:, :], xT[:, d, sl], ALU.add, [pk, ("xT", t)], [("xT", t)])
    outs = []
    if last:
        for t in range(4):
            sl = slice(t * 512, (t + 1) * 512)
            rs, rk = rstd[t % 2], f"rstd{t % 2}"
            rms_stats(K, C, [xT[:, c, sl] for c in range(8)], [("xT", t)] * 8, 128, "c1024", "eps6", banks[7], "bank7", rs[:], rk, sq, sqk)
            for c in range(8):
                K.stt("dve", xT[:, c, sl], xT[:, c, sl], vec[:, 16 + c:17 + c], rs[:], ALU.mult, ALU.mult, [("xT", t), rk, "vec"], [("xT", t)])
    for t in range(4):
        sl = slice(t * 512, (t + 1) * 512)
        outs.append(K.dma("sp", "xTo", io["xout"][:, :, sl], xT[:, :, sl], [("xT", t)], [("xout", t)]))
    return outs


def build_p3(last):
    nc = bass.Bass("TRN2", target_bir_lowering=False)
    with contextlib.ExitStack() as st:
        K = Ctx(nc, st)
        io = {}
        io["vecs"] = K.dram("vecs", [128, NVEC], F32, "ExternalInput")
        io["xT"] = K.dram("xT", [128, 8, TL], F32, "ExternalInput")
        io["mixT"] = K.dram("mixT", [1024, TL], BF16, "ExternalInput")
        io["Wo"] = K.dram("Wo", [128, 8, 1024], F32, "ExternalInput")
        io["Wup"] = K.dram("Wup", [128, 8, DFF], F32, "ExternalInput")
        io["Wdown"] = K.dram("Wdown", [128, 32, 1024], F32, "ExternalInput")
        io["xout"] = K.dram("xout", [128, 8, TL], F32, "ExternalOutput")
        C = make_consts(K)
        outs = phase3(K, C, io, last)
        K.P.emit(final_wait_ops=outs)
    return nc


_PROGS = {}


def _prog(name):
    if name not in _PROGS:
        _PROGS[name] = dict(p1=build_p1, p2=build_p2, p3a=lambda: build_p3(False), p3b=lambda: build_p3(True))[name]()
    return _PROGS[name]


def _run(nc, in_maps):
    res = run_bass_kernel_spmd(nc, in_maps, core_ids=list(range(NCORE)))
    return res.results


def kernel_unfused(**inputs):
    x = np.asarray(inputs["x"], dtype=np.float32)
    xs = [x_to_core(x, c) for c in range(NCORE)]
    ropes = [rope_tables(j) for j in range(4)]
    masks = [attn_masks(j) for j in range(4)]
    for l in range(2):
        lw = prep_layer(inputs, l)
        r1 = _run(_prog("p1"), [dict(xT=xs[c], W1=lw["W1"], Wuq=lw["Wuq"], Wukv=lw["Wukv"], vecs=lw["vecs"], rope=ropes[c % 4])
                                for c in range(NCORE)])
        in2 = []
        gath = {}
        for b in range(2):
            for k in ("KA", "KB", "KC", "VA", "VB", "VC"):
                gath[(b, k)] = np.stack([np.asarray(r1[4 * b + j][k]) for j in range(4)], 0)
        for c in range(NCORE):
            b, j = c // 4, c % 4
            in2.append(dict(vecs=lw["vecs"], MA=masks[j][0], MB=masks[j][1],
                            QA=np.asarray(r1[c]["QA"]), QB=np.asarray(r1[c]["QB"]), QC=np.asarray(r1[c]["QC"]),
                            KAg=gath[(b, "KA")], KBg=gath[(b, "KB")], KCg=gath[(b, "KC")],
                            VAg=gath[(b, "VA")], VBg=gath[(b, "VB")], VCg=gath[(b, "VC")]))
        r2 = _run(_prog("p2"), in2)
        r3 = _run(_prog("p3b" if l == 1 else "p3a"),
                  [dict(vecs=lw["vecs"], xT=xs[c], mixT=np.asarray(r2[c]["mixT"]), Wo=lw["Wo"], Wup=lw["Wup"], Wdown=lw["Wdown"])
                   for c in range(NCORE)])
        xs = [np.asarray(r3[c]["xout"]) for c in range(NCORE)]
    out = np.empty((2, SEQ, D), np.float32)
    for c in range(NCORE):
        b, j = c // 4, c % 4
        xt = xs[c].transpose(1, 0, 2).reshape(D, TL)
        out[b, core_positions(j), :] = xt.T
    return out


KROWS = 384 + 576 + 256
RG = [[0, 1, 2, 3], [4, 5, 6, 7]]


def _phase(nc, tag, body, ext_sems=None):
    with nc.cleanup_on_exit():
        with contextlib.ExitStack() as st:
            K = Ctx(nc, st, tag, ext_sems)
            C = make_consts(K)
            outs = body(K, C)
            K.P.emit(final_wait_ops=outs, tag=tag, managed=False)
        nc.all_engine_barrier()


def build_fused():
    nc = bass.Bass("TRN2", target_bir_lowering=False, num_devices=NCORE)
    ext = lambda name, shape, dt: nc.dram_tensor(name, shape, dt, kind="ExternalInput").ap()
    internal = lambda name, shape, dt: nc.dram_tensor(name, shape, dt, kind="Internal").ap()
    xT_in = ext("xT", [128, 8, TL], F32)
    rope = ext("rope", [4, 128, TL], F32)
    MA = ext("MA", [128, 32, 512], BF16)
    MB = ext("MB", [128, 16, 512], BF16)
    xout = nc.dram_tensor("xout", [128, 8, TL], F32, kind="ExternalOutput").ap()
    x1T = internal("x1T", [128, 8, TL], F32)
    agsem = {g: nc.alloc_semaphore(name="agsem" + g) for g in "ABC"}
    ag_val = {g: 0 for g in "ABC"}
    for l in range(2):
        W = dict(W1=ext(f"W1_{l}", [128, 8, W1COLS], F32), Wuq=ext(f"Wuq_{l}", [128, 3, 1152], F32),
                 Wukv=ext(f"Wukv_{l}", [128, 768], F32), vecs=ext(f"vecs_{l}", [128, NVEC], F32),
                 Wo=ext(f"Wo_{l}", [128, 8, 1024], F32), Wup=ext(f"Wup_{l}", [128, 8, DFF], F32),
                 Wdown=ext(f"Wdown_{l}", [128, 32, 1024], F32))
        kloc2 = internal(f"kloc{l}", [KROWS * 8, 256], BF16)
        vloc2 = internal(f"vloc{l}", [16 * 520, 256], BF16)
        kloc = kloc2.rearrange("(r a) b -> r (a b)", a=8)
        QA = internal(f"QA{l}", [384, TL], BF16)
        QB = internal(f"QB{l}", [6, 96, TL], BF16)
        QC = internal(f"QC{l}", [256, TL], BF16)
        mixT = internal(f"mixT{l}", [1024, TL], BF16)
        x_cur = xT_in if l == 0 else x1T
        x_next = x1T if l == 0 else xout
        vl = vloc2.rearrange("(h x) b -> h (x b)", h=16).rearrange("h (p m e) -> h p m e", p=128, m=NBLK)
        io1 = dict(xT=x_cur, W1=W["W1"], Wuq=W["Wuq"], Wukv=W["Wukv"], vecs=W["vecs"], rope=[rope[i] for i in range(4)],
                   QA=QA, QB=QB, QC=QC, KA=kloc[0:384, :], KB=kloc[384:960, :].rearrange("(h d) t -> h d t", h=6),
                   KC=kloc[960:KROWS, :], VA=vl[0:6], VB=vl[6:12], VC=vl[12:16])

        pieces = []
        kp = {}
        for nm, r0, nrows, npc in (("KA", 0, 128, 3), ("KB", 384, 192, 3), ("KC", 960, 128, 2)):
            for p in range(npc):
                key = f"g{nm}{p}"
                g = internal(f"{key}_{l}", [4 * nrows * 8, 256], BF16)
                if nm == "KB":
                    deps = [("KB", 2 * p), ("KBr", 2 * p), ("KB", 2 * p + 1), ("KBr", 2 * p + 1)]
                else:
                    deps = [(nm, p * 128)]
                pieces.append((key, kloc2[(r0 + p * nrows) * 8:(r0 + (p + 1) * nrows) * 8, :], g, deps))
                kp[key] = g.rearrange("(r q a) b -> r q (a b)", r=4, a=8)
        for nm, h0, nh, npc in (("VA", 0, 3, 2), ("VB", 6, 3, 2), ("VC", 12, 2, 2)):
            for p in range(npc):
                key = f"g{nm}{p}"
                g = internal(f"{key}_{l}", [4 * nh * 520, 256], BF16)
                deps = [(nm, p * nh + i) for i in range(nh)]
                pieces.append((key, vloc2[(h0 + p * nh) * 520:(h0 + (p + 1) * nh) * 520, :], g, deps))
                kp[key] = g.rearrange("(r h x) b -> r h (x b)", r=4, h=nh).rearrange("r h (p m e) -> r h p m e", p=128, m=NBLK)
        io1["ag_pieces"] = pieces
        io1["ag_ops"] = {}

        def body1(K, C, io1=io1):
            phase1(K, C, io1)
            return [o for o in K.P.ops["sp"] if o.is_dma and o.dma_sem.startswith(("stg", "vst", "krT_out"))]

        _phase(nc, f"L{l}a_", body1, ext_sems={"ag" + g: (agsem[g], ag_val[g]) for g in "ABC"})
        bpieces = [p for p in pieces if p[0][2] == "B"]
        assert len(io1["ag_ops"]) + len(bpieces) == len(pieces), (len(io1["ag_ops"]), len(pieces))
        for g in "AC":
            ag_val[g] = max(op.dma_cnt for key, op in io1["ag_ops"].items() if key[2] == g)
        ag_done = {key: ag_val[key[2]] for key in io1["ag_ops"]}

        def acc(kind, h, kp=kp, QA=QA, QB=QB, QC=QC):
            if kind == "A":
                kk, vk = f"gKA{h // 2}", f"gVA{h // 3}"
                return dict(rows=128, q=QA[h * 64:(h + 1) * 64, :], k=lambda r: kp[kk][r, :, :],
                            v=lambda r: kp[vk][r, h % 3], kdeps=[kk], vdeps=[vk])
            if kind == "B":
                kk, vk = f"gKB{h // 2}", f"gVB{h // 3}"
                return dict(rows=96, q=QB[h], k=lambda r: kp[kk][r, (h % 2) * 96:(h % 2 + 1) * 96, :],
                            v=lambda r: kp[vk][r, h % 3], kdeps=[kk], vdeps=[vk])
            kk, vk = f"gKC{h // 2}", f"gVC{h // 2}"
            return dict(rows=128, q=QC[h * 64:(h + 1) * 64, :], k=lambda r: kp[kk][r, :, :],
                        v=lambda r: kp[vk][r, h % 2], kdeps=[kk], vdeps=[vk])

        io2 = dict(vecs=W["vecs"], MA=MA, MB=MB, mixT=mixT, acc=acc)

        def body2(K, C, io2=io2, ag_done=ag_done, bpieces=bpieces):
            for key, val in ag_done.items():
                K.P.external("ag" + key[2], val, [key])
            for (key, src, dst, deps) in bpieces:
                K.P.dma("pool", "agB", lambda e, src=src, dst=dst: e.collective_compute(
                    "AllGather", ALU.bypass, replica_groups=RG, ins=[src], outs=[dst[:, :]]), [], [key], inc=1)
            for (key, _, _, _) in bpieces:
                K.P.lastw[key] = K.P.lastw[bpieces[-1][0]]
            outs = phase2(K, C, io2)
            ag_val["B"] = K.P.dma_sems["agB"]
            return outs

        _phase(nc, f"L{l}b_", body2, ext_sems={"ag" + g: (agsem[g], ag_val[g]) for g in "ABC"})
        io3 = dict(vecs=W["vecs"], xT=x_cur, mixT=mixT, Wo=W["Wo"], Wup=W["Wup"], Wdown=W["Wdown"], xout=x_next)
        _phase(nc, f"L{l}c_", lambda K, C, io3=io3, l=l: phase3(K, C, io3, l == 1))
    return nc


def kernel(**inputs):
    x = np.asarray(inputs["x"], dtype=np.float32)
    if "fused" not in _PROGS:
        _PROGS["fused"] = build_fused()
    lws = [prep_layer(inputs, l) for l in range(2)]
    ropes = [rope_tables(j) for j in range(4)]
    masks = [attn_masks(j) for j in range(4)]
    in_maps = []
    for c in range(NCORE):
        j = c % 4
        m = dict(xT=x_to_core(x, c), rope=ropes[j], MA=masks[j][0], MB=masks[j][1])
        for l in range(2):
            for k in ("W1", "Wuq", "Wukv", "vecs", "Wo", "Wup", "Wdown"):
                m[f"{k}_{l}"] = lws[l][k]
        in_maps.append(m)
    res = _run(_PROGS["fused"], in_maps)
    out = np.empty((2, SEQ, D), np.float32)
    for c in range(NCORE):
        b, j = c // 4, c % 4
        xt = np.asarray(res[c]["xout"]).transpose(1, 0, 2).reshape(D, TL)
        out[b, core_positions(j), :] = xt.T
    return out
```

```python
import contextlib
import math
import numpy as np
import ml_dtypes
import concourse.bass as bass
import concourse.mybir as mybir
from concourse.bass_utils import run_bass_kernel_spmd

F32 = mybir.dt.float32
BF16 = mybir.dt.bfloat16
ALU = mybir.AluOpType
AF = mybir.ActivationFunctionType

D = 1024
SEQ = 8192
NCORE = 8
TL = 2048
NBLK = 16
DFF = 4096
OFF = dict(qa=0, ka=384, va=768, cq=1152, ckv=1536, kr=1664, qc=1696, kc=1952, vc=2208)
W1COLS = 3904
NVEC = 160

import os
PROBE_K128 = bool(os.environ.get("PROBE_K128"))
COMPUTE = ("pe", "act", "dve", "pool")
SEM_WRAP = 30000


class Op:
    __slots__ = ("eng", "fn", "deps", "idx", "sig", "dma_sem", "dma_cnt", "is_dma", "sigidx")

    def __init__(self, eng, fn, is_dma=False):
        self.eng = eng
        self.fn = fn
        self.deps = []
        self.sig = False
        self.is_dma = is_dma
        self.dma_sem = None
        self.dma_cnt = 0
        self.sigidx = -1
        self.idx = -1


class Prog:
    def __init__(self, nc, ext_sems=None):
        self.nc = nc
        self.ops = {e: [] for e in ("pe", "act", "dve", "pool", "sp")}
        self.lastw = {}
        self.readers = {}
        self.dma_sems = {}
        self.ext_sems = dict(ext_sems or {})
        for name, (h, base) in self.ext_sems.items():
            self.dma_sems[name] = base

    def external(self, semname, value, writes):
        o = Op("pool", None, is_dma=True)
        o.dma_sem = semname
        o.dma_cnt = value
        for k in writes:
            self.lastw[k] = o
            self.readers[k] = []
        return o

    def _add(self, op, reads, writes):
        deps = []
        for k in reads:
            w = self.lastw.get(k)
            if w is not None:
                deps.append(w)
        for k in writes:
            w = self.lastw.get(k)
            if w is not None:
                deps.append(w)
            deps.extend(self.readers.get(k, ()))
        seen = set()
        for d in deps:
            if d is op or id(d) in seen:
                continue
            seen.add(id(d))
            op.deps.append(d)
        for k in reads:
            lst = self.readers.setdefault(k, [])
            if not op.is_dma:
                for i, r in enumerate(lst):
                    if (not r.is_dma) and r.eng == op.eng:
                        lst[i] = op
                        break
                else:
                    lst.append(op)
            else:
                lst.append(op)
        for k in writes:
            self.lastw[k] = op
            self.readers[k] = []
        op.idx = len(self.ops[op.eng])
        self.ops[op.eng].append(op)
        return op

    def op(self, eng, fn, reads=(), writes=()):
        return self._add(Op(eng, fn), reads, writes)

    def dma(self, eng, semname, fn, reads=(), writes=(), inc=16):
        o = Op(eng, fn, is_dma=True)
        self.dma_sems[semname] = self.dma_sems.get(semname, 0) + inc
        o.dma_sem = semname
        o.dma_cnt = self.dma_sems[semname]
        o.sigidx = inc
        return self._add(o, reads, writes)

    @staticmethod
    def _skip_same(d, o):
        return (not d.is_dma) and d.eng == o.eng and (d.eng == "pe" or o.idx - d.idx > 2)

    def emit(self, final_wait_ops=(), tag="", managed=True):
        nc = self.nc
        for e, lst in self.ops.items():
            for o in lst:
                for d in o.deps:
                    if d.is_dma or self._skip_same(d, o):
                        continue
                    d.sig = True
        for o in final_wait_ops:
            if not o.is_dma:
                o.sig = True
        nsig = {}
        for e in COMPUTE:
            c = 0
            for o in self.ops[e]:
                if o.sig and not o.is_dma:
                    o.sigidx = c
                    c += 1
            nsig[e] = c
        with contextlib.ExitStack() as st:
            if managed:
                mksem = lambda nm: st.enter_context(nc.semaphore(tag + nm))
            else:
                mksem = lambda nm: nc.alloc_semaphore(name=tag + nm)
            csem = {}
            for e in COMPUTE:
                n = max(1, (nsig[e] + SEM_WRAP - 1) // SEM_WRAP)
                csem[e] = [mksem(f"c_{e}_{i}") for i in range(n)]
            dsem = {name: (self.ext_sems[name][0] if name in self.ext_sems else mksem(f"d_{name}")) for name in self.dma_sems}
            block = st.enter_context(nc.Block())
            engobj = {"pe": "tensor", "act": "scalar", "dve": "vector", "pool": "gpsimd", "sp": "sync"}

            def run_engine(e, eng):
                waited = {}
                for o in self.ops[e]:
                    need = {}
                    for d in o.deps:
                        if d.is_dma:
                            key = ("d", d.dma_sem)
                            val = d.dma_cnt
                            sem = dsem[d.dma_sem]
                        else:
                            if self._skip_same(d, o):
                                continue
                            si = d.sigidx // SEM_WRAP
                            key = ("c", d.eng, si)
                            val = d.sigidx % SEM_WRAP + 1
                            sem = csem[d.eng][si]
                        if waited.get(key, 0) >= val:
                            continue
                        if key not in need or need[key][1] < val:
                            need[key] = (sem, val)
                    for key, (sem, val) in need.items():
                        eng.wait_ge(sem, val)
                        waited[key] = val
                    ins = o.fn(eng)
                    if o.is_dma:
                        ins.then_inc(dsem[o.dma_sem], o.sigidx)
                    elif o.sig:
                        ins.then_inc(csem[e][o.sigidx // SEM_WRAP], 1)
                if e == "sp":
                    fin = {}
                    for o in final_wait_ops:
                        if o.is_dma:
                            key, sem, val = ("d", o.dma_sem), dsem[o.dma_sem], o.dma_cnt
                        else:
                            si = o.sigidx // SEM_WRAP
                            key, sem, val = ("c", o.eng, si), csem[o.eng][si], o.sigidx % SEM_WRAP + 1
                        if key not in fin or fin[key][1] < val:
                            fin[key] = (sem, val)
                    for sem, val in fin.values():
                        eng.wait_ge(sem, val)

            for e in ("sp", "pool", "act", "dve", "pe"):
                getattr(block, engobj[e])(lambda eng, e=e: run_engine(e, eng))


class Ctx:
    def __init__(self, nc, st, tag="", ext_sems=None):
        self.nc = nc
        self.st = st
        self.tag = tag
        self.P = Prog(nc, ext_sems)
        self.pairs = [st.enter_context(nc.psum_tensor(f"{tag}pbank{i}", [128, 2, 512], F32)) for i in range(4)]
        self.banks = [self.pairs[i // 2][:, i % 2, :] for i in range(8)]
        self.uid = 0

    def sb(self, name, shape, dt):
        return self.st.enter_context(self.nc.sbuf_tensor(self.tag + "s_" + name, shape, dt))

    def dram(self, name, shape, dt, kind):
        return self.nc.dram_tensor(name, shape, dt, kind=kind).ap()

    def mm(self, out, lhsT, rhs, start, stop, reads, writes, **kw):
        return self.P.op("pe", lambda e: e.matmul(out, lhsT, rhs, start=start, stop=stop, **kw), reads, writes)

    def act(self, out, in_, func, reads, writes, bias=None, scale=None):
        kw = {}
        if bias is not None:
            kw["bias"] = bias
        if scale is not None:
            kw["scale"] = scale
        return self.P.op("act", lambda e: e.activation(out=out, in_=in_, func=func, **kw), reads, writes)

    def tt(self, eng, out, in0, in1, op, reads, writes):
        return self.P.op(eng, lambda e: e.tensor_tensor(out=out, in0=in0, in1=in1, op=op), reads, writes)

    def ts(self, eng, out, in0, s1, op0, reads, writes, s2=None, op1=None):
        if op1 is None:
            return self.P.op(eng, lambda e: e.tensor_scalar(out=out, in0=in0, scalar1=s1, scalar2=None, op0=op0), reads, writes)
        return self.P.op(eng, lambda e: e.tensor_scalar(out=out, in0=in0, scalar1=s1, scalar2=s2, op0=op0, op1=op1), reads, writes)

    def stt(self, eng, out, in0, scalar, in1, op0, op1, reads, writes):
        return self.P.op(eng, lambda e: e.scalar_tensor_tensor(out=out, in0=in0, scalar=scalar, in1=in1, op0=op0, op1=op1), reads, writes)

    def copy(self, eng, out, in_, reads, writes):
        if eng == "act":
            return self.P.op("act", lambda e: e.copy(out=out, in_=in_), reads, writes)
        return self.P.op(eng, lambda e: e.tensor_copy(out=out, in_=in_), reads, writes)

    def recip(self, out, in_, reads, writes):
        return self.P.op("dve", lambda e: e.reciprocal(out=out, in_=in_), reads, writes)

    def memset(self, eng, ap, val, writes):
        return self.P.op(eng, lambda e: e.memset(ap, val), (), writes)

    def dma(self, q, sem, out, in_, reads, writes):
        return self.P.dma(q, sem, lambda e: e.dma_start(out=out, in_=in_), reads, writes)


class WStream:
    def __init__(self, K, slots, keys, items):
        self.K, self.slots, self.keys, self.items = K, slots, keys, items
        self.issued = 0
        self.cur = 0

    def _issue(self):
        i = self.issued
        if i >= len(self.items):
            return
        src, shape = self.items[i]
        n = 1
        for d in shape[1:]:
            n *= d
        ns = len(self.slots)
        dst = self.slots[i % ns][:, 0:n]
        if len(shape) == 3:
            dst = dst.rearrange("p (a b) -> p a b", a=shape[1])
        key = self.keys[i % ns]
        self.K.P.dma("pool", key, lambda e: e.dma_start(out=dst, in_=src), [], [key])
        self.views = getattr(self, "views", {})
        self.views[i] = (dst, key)
        self.issued += 1

    def next(self):
        while self.issued <= min(self.cur + 1, len(self.items) - 1):
            self._issue()
        v = self.views[self.cur]
        self.cur += 1
        return v


def make_consts(K):
    c = {}
    for name, val in (("c1024", 1.0 / 1024), ("c384", 1.0 / 384), ("c128", 1.0 / 128), ("c64", 1.0 / 64), ("one", 1.0)):
        t = K.sb(name, [128, 128], F32)
        K.memset("pool", t[:], val, [name])
        c[name] = t
    for name, val in (("eps6", 1e-6), ("eps5", 1e-5)):
        t = K.sb(name, [128, 1], F32)
        K.memset("pool", t[:], val, [name])
        c[name] = t
    return c


def rms_stats(K, C, srcs, src_keys, rows, cname, epsname, bank, bank_key, rstd, rstd_key, sq, sqkeys):
    n = len(srcs)
    for i, s in enumerate(srcs):
        q = sq[i % 2]
        K.act(q[0:rows, :], s, AF.Square, [src_keys[i]], [sqkeys[i % 2]])
        K.mm(bank[:, :], C[cname][0:rows, :], q[0:rows, :], i == 0, i == n - 1, [sqkeys[i % 2], cname], [bank_key])
    K.act(rstd, bank[:, :], AF.Sqrt, [bank_key, epsname], [rstd_key], bias=C[epsname][:, 0:1], scale=1.0)
    K.recip(rstd, rstd, [rstd_key], [rstd_key])


FM_GROUPS = [
    (0, 512, [("ropeA", 0, 128, "QA", 0), ("ropeA", 256, 128, "QA", 128)]),
    (512, 512, [("ropeA", 0, 128, "QA", 256), ("ropeA", 256, 128, "KA", 0)]),
    (1024, 512, [("ropeA", 0, 128, "KA", 128), ("ropeA", 256, 128, "KA", 256)]),
    (1536, 512, [("ropeC", 0, 128, "QC", 0), ("ropeC", 256, 128, "QC", 128)]),
    (2048, 512, [("ropeC", 0, 128, "KC", 0), ("ropeC", 256, 128, "KC", 128)]),
    (2560, 512, [("cq", 0, 128, None, 0), ("cq", 128, 128, None, 1), ("cq", 256, 128, None, 2), ("ckv", 384, 128, None, 0)]),
    (3072, 192, [("kr", 0, 96, None, 0)]),
]
TM_GROUPS = [(3264, 384, 6, "VA"), (3648, 256, 4, "VC")]


def phase1(K, C, io):
    P = K.P
    banks = K.banks
    xT = io["xT"]
    W1 = io["W1"]
    vec = K.sb("vec", [128, NVEC], F32)
    K.dma("sp", "vec", vec[:], io["vecs"], [], ["vec"])
    xs = [K.sb("xs0", [128, 8, 512], F32)] * 2
    hT = K.sb("hT", [128, 8, TL], BF16)
    rstd = [K.sb(f"rstd{i}", [128, 512], F32) for i in range(2)]
    sq = [K.sb(f"sq{i}", [128, 512], F32) for i in range(2)]
    sqk = ["sq0", "sq1"]
    wsl = [K.sb(f"wsl{i}", [128, 4096], BF16) for i in range(2)]
    t64c = K.sb("t64c", [128, TL], F32)
    t64s = K.sb("t64s", [128, TL], F32)
    t32c = K.sb("t32c", [128, TL], F32)
    t32s = K.sb("t32s", [128, TL], F32)
    NSTG = 6
    stg = [K.sb(f"stg{i}", [128, TL], BF16) for i in range(NSTG)]
    r1 = [K.sb(f"r1_{i}", [128, 512], F32) for i in range(2)]
    r2 = [K.sb(f"r2_{i}", [128, 512], F32) for i in range(2)]
    cqT = K.sb("cqT", [128, 3, TL], F32)
    ckvT = K.sb("ckvT", [128, TL], F32)
    cqn = K.sb("cqn", [128, 3, TL], BF16)
    ckvn = K.sb("ckvn", [128, TL], BF16)
    krT = K.sb("krT", [128, TL], BF16)
    vst = [K.sb("vst0", [128, 6, NBLK, 65], BF16)] * 2
    K.memset("pool", vst[0][:], 1.0, ["vst0"])

    for t in range(4):
        x = xs[t % 2]
        xk = "xs0"
        K.dma("sp", xk, x[:], xT[:, :, t * 512:(t + 1) * 512], [], [xk])
        rs = rstd[t % 2]
        rk = f"rstd{t % 2}"
        rms_stats(K, C, [x[:, c, :] for c in range(8)], [xk] * 8, 128, "c1024", "eps6", banks[7], "bank7", rs[:], rk, sq, sqk)
        for c in range(8):
            K.stt("dve", hT[:, c, t * 512:(t + 1) * 512], x[:, c, :], vec[:, c:c + 1], rs[:], ALU.mult, ALU.mult,
                  [xk, rk, "vec"], [("hT", t)])

    K.dma("sp", "t64c", t64c[:], io["rope"][0], [], ["t64c"])
    K.dma("sp", "t64s", t64s[:], io["rope"][1], [], ["t64s"])
    K.dma("sp", "t32c", t32c[:], io["rope"][2], [], ["t32c"])
    K.dma("sp", "t32s", t32s[:], io["rope"][3], [], ["t32s"])
    fmi = lambda gs: [(W1[:, :, goff:goff + gcols], [128, 8, gcols]) for (goff, gcols, _) in gs]
    tmi = lambda gs: [(W1[:, :, goff:goff + gcols], [128, 8, gcols]) for (goff, gcols, _, _) in gs]
    items = fmi(FM_GROUPS[5:7]) + fmi(FM_GROUPS[0:3]) + tmi(TM_GROUPS[0:1]) + fmi(FM_GROUPS[3:5]) + tmi(TM_GROUPS[1:2])
    items += [(io["Wuq"], [128, 3, 1152]), (io["Wukv"], [128, 768])]
    ws = WStream(K, wsl, ["wsl0", "wsl1"], items)
    wload = lambda src, shape: ws.next()

    pcnt = [0]
    scnt = [0]
    rcnt = [0]

    def rope(psA, psB, ka, kb, cosT, sinT, ck, sk, out, okey, p0, p1, t):
        i = rcnt[0] % 2
        rcnt[0] += 1
        sl = slice(t * 512, (t + 1) * 512)
        K.tt("dve", r1[i][p0:p1, :], psA[p0:p1, :], cosT[p0:p1, sl], ALU.mult, [ka, ck], [f"r1_{i}"])
        K.tt("dve", r2[i][p0:p1, :], psB[p0:p1, :], sinT[p0:p1, sl], ALU.mult, [kb, sk], [f"r2_{i}"])
        K.tt("pool", out, r1[i][p0:p1, :], r2[i][p0:p1, :], ALU.add, [f"r1_{i}", f"r2_{i}"], [okey])

    ag_pending = list(io.get("ag_pieces", []))
    ag_ready = []

    def try_ag(flush=False):
        for item in ag_ready[:]:
            key, src, dst, deps = item
            K.P.dma("pool", "ag" + key[2], lambda e, src=src, dst=dst: e.collective_compute(
                "AllGather", ALU.bypass, replica_groups=RG, ins=[src], outs=[dst[:, :]]), deps, [key], inc=1)
            io["ag_ops"][key] = K.P.lastw[key]
            ag_ready.remove(item)
        for item in ag_pending[:]:
            if all(k in K.P.lastw for k in item[3]):
                ag_pending.remove(item)
                ag_ready.append(item)
        if flush and (ag_ready or ag_pending):
            try_ag(flush=True)

    def fm_groups(groups):
        for (goff, gcols, tiles) in groups:
            w, wk = wload(W1[:, :, goff:goff + gcols], [128, 8, gcols])
            for (kind, loff, M, oname, orow) in tiles:
                roped = kind in ("ropeA", "ropeC", "kr")
                si = scnt[0] % NSTG
                if kind in ("ropeA", "ropeC"):
                    scnt[0] += 1
                for t in range(4):
                    sl = slice(t * 512, (t + 1) * 512)
                    ba = pcnt[0] % 4
                    pcnt[0] += 1
                    psA = banks[ba]
                    ka = f"bank{ba}"
                    for c in range(8):
                        K.mm(psA[0:M, :], w[:, c, loff:loff + M], hT[:, c, sl], c == 0, c == 7, [wk, ("hT", t)], [ka])
                    if roped:
                        bb = pcnt[0] % 4
                        pcnt[0] += 1
                        psB = banks[bb]
                        kb = f"bank{bb}"
                        so = loff + (128 if kind != "kr" else 96)
                        for c in range(8):
                            K.mm(psB[0:M, :], w[:, c, so:so + M], hT[:, c, sl], c == 0, c == 7, [wk, ("hT", t)], [kb])
                    if kind == "ropeA":
                        rope(psA, psB, ka, kb, t64c, t64s, "t64c", "t64s", stg[si][:, sl], f"stg{si}", 0, 128, t)
                    elif kind == "ropeC":
                        rope(psA, psB, ka, kb, t32c, t32s, "t32c", "t32s", stg[si][:, sl], f"stg{si}", 0, 128, t)
                    elif kind == "kr":
                        rope(psA, psB, ka, kb, t32c, t32s, "t32c", "t32s", krT[64:96, sl], "krT", 64, 96, t)
                    elif kind == "cq":
                        K.copy("act", cqT[:, orow, sl], psA[:, :], [ka], [("cqT", t)])
                    elif kind == "ckv":
                        K.copy("act", ckvT[:, sl], psA[:, :], [ka], [("ckvT", t)])
                if kind in ("ropeA", "ropeC"):
                    K.dma("sp", f"stg{si}", io[oname][orow:orow + 128, :], stg[si][:, :], [f"stg{si}"], [(oname, orow)])
                    try_ag()


    def tm_groups(groups, vcnt):
        for (goff, gcols, nh, oname) in groups:
            w, wk = wload(W1[:, :, goff:goff + gcols], [128, 8, gcols])
            vi = vcnt % 2
            vcnt += 1
            vk = "vst0"
            for m in range(NBLK):
                bi = 4 + (m % 2)
                ps = banks[bi]
                pk = f"bank{bi}"
                t = m // 4
                for c in range(8):
                    K.mm(ps[:, 0:gcols], hT[:, c, m * 128:(m + 1) * 128], w[:, c, 0:gcols], c == 0, c == 7, [wk, ("hT", t)], [pk])
                K.copy("act" if m % 2 == 0 else "dve", vst[vi][:, 0:nh, m, 0:64],
                       ps[:, 0:gcols].rearrange("p (h d) -> p h d", h=nh), [pk], [vk])
            for h in range(nh):
                K.dma("sp", f"{vk}_{h // (3 if nh == 6 else 2)}", io[oname][h], vst[vi][:, h, :, :], [vk], [(oname, h)])
            try_ag()

        return vcnt

    def mla_norms():
        for t in range(4):
            sl = slice(t * 512, (t + 1) * 512)
            rs = rstd[t % 2]
            rk = f"rstd{t % 2}"
            rms_stats(K, C, [cqT[:, c, sl] for c in range(3)], [("cqT", t)] * 3, 128, "c384", "eps6", banks[7], "bank7", rs[:], rk, sq, sqk)
            for c in range(3):
                K.stt("dve", cqn[:, c, sl], cqT[:, c, sl], vec[:, 24 + c:25 + c], rs[:], ALU.mult, ALU.mult,
                      [("cqT", t), rk, "vec"], [("cqn", t)])
            rs = rstd[(t + 1) % 2]
            rk = f"rstd{(t + 1) % 2}"
            rms_stats(K, C, [ckvT[:, sl]], [("ckvT", t)], 128, "c128", "eps6", banks[6], "bank6", rs[:], rk, sq, sqk)
            K.stt("dve", ckvn[:, sl], ckvT[:, sl], vec[:, 27:28], rs[:], ALU.mult, ALU.mult, [("ckvT", t), rk, "vec"], [("ckvn", t)])

    def mla(vcnt):
        wq, wqk = wload(io["Wuq"], [128, 3, 1152])
        for h in range(6):
            si = scnt[0] % NSTG
            scnt[0] += 1
            for t in range(4):
                sl = slice(t * 512, (t + 1) * 512)
                ba = pcnt[0] % 4
                pcnt[0] += 1
                bb = pcnt[0] % 4
                pcnt[0] += 1
                psA, psB = banks[ba], banks[bb]
                ka, kb = f"bank{ba}", f"bank{bb}"
                for c in range(3):
                    K.mm(psA[0:96, :], wq[:, c, h * 96:(h + 1) * 96], cqn[:, c, sl], c == 0, c == 2, [wqk, ("cqn", t)], [ka])
                for c in range(3):
                    K.mm(psB[0:96, :], wq[:, c, 576 + h * 96:576 + (h + 1) * 96], cqn[:, c, sl], c == 0, c == 2, [wqk, ("cqn", t)], [kb])
                K.copy("act", stg[si][0:64, sl], psA[0:64, :], [ka], [f"stg{si}"])
                rope(psA, psB, ka, kb, t32c, t32s, "t32c", "t32s", stg[si][64:96, sl], f"stg{si}", 64, 96, t)
            K.dma("sp", f"stg{si}", io["QB"][h], stg[si][0:96, :], [f"stg{si}"], [("QB", h)])

        wkv, wkvk = wload(io["Wukv"], [128, 768])
        for h in range(6):
            si = scnt[0] % NSTG
            scnt[0] += 1
            for t in range(4):
                sl = slice(t * 512, (t + 1) * 512)
                ba = pcnt[0] % 4
                pcnt[0] += 1
                psA = banks[ba]
                ka = f"bank{ba}"
                K.mm(psA[0:64, :], wkv[:, h * 64:(h + 1) * 64], ckvn[:, sl], True, True, [wkvk, ("ckvn", t)], [ka])
                K.copy("act" if t % 2 == 0 else "dve", stg[si][0:64, sl], psA[0:64, :], [ka], [f"stg{si}"])
            K.dma("sp", f"stg{si}", io["KB"][h, 0:64, :], stg[si][0:64, :], [f"stg{si}"], [("KB", h)])
            K.dma("sp", f"krT_out{h // 2}", io["KB"][h, 64:96, :], krT[64:96, :], ["krT"], [("KBr", h)])
        vi = vcnt % 2
        vcnt += 1
        vk = "vst0"
        for m in range(NBLK):
            bi = 4 + (m % 2)
            ps = banks[bi]
            pk = f"bank{bi}"
            t = m // 4
            K.mm(ps[:, 0:384], ckvn[:, m * 128:(m + 1) * 128], wkv[:, 384:768], True, True, [wkvk, ("ckvn", t)], [pk])
            K.copy("act" if m % 2 == 0 else "dve", vst[vi][:, 0:6, m, 0:64],
                   ps[:, 0:384].rearrange("p (h d) -> p h d", h=6), [pk], [vk])
        for h in range(6):
            K.dma("sp", f"{vk}_{h // 3}", io["VB"][h], vst[vi][:, h, :, :], [vk], [("VB", h)])
        try_ag()

    fm_groups(FM_GROUPS[5:7])
    mla_norms()
    fm_groups(FM_GROUPS[0:3])
    vc = tm_groups(TM_GROUPS[0:1], 0)
    fm_groups(FM_GROUPS[3:5])
    vc = tm_groups(TM_GROUPS[1:2], vc)
    mla(vc)
    try_ag(flush=True)


P1_OUT = dict(QA=[384, TL], KA=[384, TL], QC=[256, TL], KC=[256, TL], QB=[6, 96, TL], KB=[6, 96, TL],
              VA=[6, 128, NBLK, 65], VB=[6, 128, NBLK, 65], VC=[4, 128, NBLK, 65])


def build_p1():
    nc = bass.Bass("TRN2", target_bir_lowering=False)
    with contextlib.ExitStack() as st:
        K = Ctx(nc, st)
        io = {}
        io["xT"] = K.dram("xT", [128, 8, TL], F32, "ExternalInput")
        io["W1"] = K.dram("W1", [128, 8, W1COLS], F32, "ExternalInput")
        io["Wuq"] = K.dram("Wuq", [128, 3, 1152], F32, "ExternalInput")
        io["Wukv"] = K.dram("Wukv", [128, 768], F32, "ExternalInput")
        io["vecs"] = K.dram("vecs", [128, NVEC], F32, "ExternalInput")
        rope = K.dram("rope", [4, 128, TL], F32, "ExternalInput")
        io["rope"] = [rope[i] for i in range(4)]
        for name, shape in P1_OUT.items():
            io[name] = K.dram(name, shape, BF16, "ExternalOutput")
        C = make_consts(K)
        phase1(K, C, io)
        outs = [o for o in K.P.ops["sp"] if o.is_dma and o.dma_sem.startswith(("stg", "vst", "krT_out"))]
        K.P.emit(final_wait_ops=outs)
    return nc


def _swap_cols(cols, blk):
    cols = np.asarray(cols)
    out = cols.copy().reshape(-1, blk)
    half = blk // 2
    out = np.concatenate([out[:, half:], out[:, :half]], axis=1)
    return out.reshape(-1)


def w1_columns():
    cols = []
    for base in (OFF["qa"], OFF["ka"]):
        for t in range(3):
            c = np.arange(base + t * 128, base + (t + 1) * 128)
            cols += [c, _swap_cols(c, 64)]
    for base in (OFF["qc"], OFF["kc"]):
        for t in range(2):
            c = np.arange(base + t * 128, base + (t + 1) * 128)
            cols += [c, _swap_cols(c, 32)]
    cols.append(np.arange(OFF["cq"], OFF["cq"] + 384))
    cols.append(np.arange(OFF["ckv"], OFF["ckv"] + 128))
    filler = np.arange(OFF["ckv"], OFF["ckv"] + 64)
    kr = np.arange(OFF["kr"], OFF["kr"] + 32)
    cols += [filler, kr, filler, _swap_cols(kr, 32)]
    cols.append(np.arange(OFF["va"], OFF["va"] + 384))
    cols.append(np.arange(OFF["vc"], OFF["vc"] + 256))
    cols = np.concatenate(cols)
    assert cols.shape[0] == W1COLS
    return cols


def chunked(w, nchunk):
    n = w.shape[1]
    return np.ascontiguousarray(w.reshape(nchunk, 128, n).transpose(1, 0, 2))


def prep_layer(inp, l):
    f = lambda a: np.asarray(a, dtype=np.float32)
    out = {}
    w_in = f(inp["w_in"][l])
    out["W1"] = chunked(w_in[:, w1_columns()], 8)
    wuq = f(inp["mla_w_uq"][l])
    ncols, scols = [], []
    for h in range(6):
        c = np.arange(h * 96, (h + 1) * 96)
        ncols.append(c)
        scols.append(np.concatenate([c[:64], _swap_cols(c[64:], 32)]))
    out["Wuq"] = chunked(wuq[:, np.concatenate(ncols + scols)], 3)
    wukv = f(inp["mla_w_ukv"][l])
    kc = np.concatenate([np.arange(h * 128, h * 128 + 64) for h in range(6)])
    vc = np.concatenate([np.arange(h * 128 + 64, h * 128 + 128) for h in range(6)])
    out["Wukv"] = np.ascontiguousarray(wukv[:, np.concatenate([kc, vc])])
    vec = np.zeros((128, NVEC), np.float32)
    vec[:, 0:8] = f(inp["ln1_g"][l]).reshape(8, 128).T
    vec[:, 8:16] = f(inp["ln2_g"][l]).reshape(8, 128).T
    vec[:, 16:24] = f(inp["final_g"]).reshape(8, 128).T
    vec[:, 24:27] = f(inp["mla_q_norm_g"][l]).reshape(3, 128).T
    vec[:, 27] = f(inp["mla_kv_norm_g"][l])
    vec[:, 28] = np.tile(f(inp["diff_subln_g"][l]), 2)
    lam_init = 0.8 - 0.6 * math.exp(-0.3 * l)
    vec[:, 29] = lam_init
    vec[:, 30] = 1.0 - lam_init
    vec[:, 32:160] = f(inp["diff_lambda"][l]).reshape(1, 128)
    out["vecs"] = vec
    out["Wo"] = chunked(f(inp["w_o"][l]), 8)
    out["Wup"] = chunked(f(inp["w_up"][l]), 8)
    out["Wdown"] = chunked(f(inp["w_down"][l]), 32)
    return out


def core_positions(j):
    m = np.arange(NBLK)[:, None]
    t = np.arange(128)[None, :]
    return ((4 * m + j) * 128 + t).reshape(-1)


def rope_tables(j):
    pos = core_positions(j).astype(np.float32)
    tabs = []
    for dim in (64, 32):
        half = dim // 2
        inv = (1.0 / (np.float32(10000.0) ** (np.arange(half, dtype=np.float32) / np.float32(half)))).astype(np.float32)
        r = np.arange(128)
        i = r % dim
        fidx = i % half
        ang = (pos[None, :] * inv[fidx][:, None]).astype(np.float32)
        sign = np.where(i < half, -1.0, 1.0).astype(np.float32)[:, None]
        tabs.append(np.cos(ang).astype(np.float32))
        tabs.append((np.sin(ang) * sign).astype(np.float32))
    return np.stack(tabs, 0)


def x_to_core(x, c):
    b, j = c // 4, c % 4
    xs = np.asarray(x[b], dtype=np.float32)[core_positions(j)]
    return chunked(np.ascontiguousarray(xs.T), 8)


def a_cols(ak):
    return max(0, ak) * 128, min(4, ak + 5) * 128


def phase2(K, C, io):
    P = K.P
    banks = K.banks
    vec = K.sb("vec", [128, NVEC], F32)
    K.dma("sp", "vec", vec[:], io["vecs"], [], ["vec"])
    MAt = K.sb("MAt", [128, 32, 512], BF16)
    MBt = K.sb("MBt", [128, 16, 512], BF16)

    kt = [K.sb(f"kt{i}", [128, 4, TL], BF16) for i in range(2)]
    vt = [K.sb(f"vt{i}", [128, 4 * NBLK, 65], BF16) for i in range(2)]
    qa = [K.sb(f"qa{i}", [128, TL], BF16) for i in range(2)]
    qb = [K.sb(f"qb{i}", [128, TL], BF16) for i in range(2)]
    qc = [K.sb(f"qc{i}", [128, 2, TL], BF16) for i in range(2)]
    for qq, nm in ((qa, "qa"), (qc, "qc"), (qb, "qb")):
        for i in range(2):
            K.memset("pool", qq[i][:], 0.0, [f"{nm}{i}"])
    NPT = 6
    LAG = 3
    NSB = 3
    pt = [K.sb(f"pt{i}", [128, 2, 512], BF16) for i in range(NPT)]
    osb = [K.sb(f"osb{i}", [128, 512], F32) for i in range(4)]
    rz = [K.sb(f"rz{i}", [128, 512], F32) for i in range(4)]
    ones512 = K.sb("ones512", [128, 512], F32)
    K.memset("pool", ones512[:], -1.0, ["ones512"])
    deferred = []
    deferred_a = []
    tA = K.sb("tA", [128, 512], F32)
    tB = K.sb("tB", [128, 512], F32)
    tO = K.sb("tO", [128, 512], F32)
    tS = K.sb("tS", [128, 512], F32)
    ostg = [K.sb(f"ostg{i}", [128, TL], BF16) for i in range(2)]
    lam = K.sb("lam", [128, 40], F32)

    K.tt("dve", lam[:, 0:32], vec[:, 32:64], vec[:, 64:96], ALU.mult, ["vec"], ["lam"])
    P.op("dve", lambda e: e.reduce_sum(out=lam[:, 32:33], in_=lam[:, 0:32], axis=mybir.AxisListType.X), ["lam"], ["lam1"])
    K.tt("dve", lam[:, 0:32], vec[:, 96:128], vec[:, 128:160], ALU.mult, ["vec", "lam1"], ["lam"])
    P.op("dve", lambda e: e.reduce_sum(out=lam[:, 33:34], in_=lam[:, 0:32], axis=mybir.AxisListType.X), ["lam"], ["lam2"])
    K.act(lam[:, 34:36], lam[:, 32:34], AF.Exp, ["lam1", "lam2"], ["lam3"])
    K.tt("dve", lam[:, 36:37], lam[:, 35:36], lam[:, 34:35], ALU.subtract, ["lam3"], ["lam4"])
    K.tt("dve", lam[:, 37:38], lam[:, 36:37], vec[:, 29:30], ALU.subtract, ["lam4", "vec"], ["neglam"])
    K.tt("dve", lam[:, 38:39], vec[:, 28:29], vec[:, 30:31], ALU.mult, ["vec"], ["gc"])
    neglam = lam[:, 37:38]
    gc = lam[:, 38:39]

    jobs = [("A", h) for h in range(6)] + [("C", h) for h in range(4)] + [("B", h) for h in range(6)]

    jobslot = {}

    def load_job(ji):
        kind, h = jobs[ji]
        s = ji % 2
        a = io["acc"](kind, h)
        rows = a["rows"]
        if kind == "A":
            qs = h % 2
            K.dma("sp", f"qa{qs}", qa[qs][qs * 64:qs * 64 + 64, :], a["q"], [], [f"qa{qs}"])
            jobslot[ji] = [(qa[qs], f"qa{qs}", None)]
        elif kind == "B":
            qs = h % 2
            K.dma("sp", f"qb{qs}", qb[qs][0:96, :], a["q"], [], [f"qb{qs}"])
            jobslot[ji] = [(qb[qs], f"qb{qs}", None)]
        else:
            qs = h % 2
            for mi in range(2):
                p0 = qs * 64 + mi * 32
                K.dma("sp", f"qc{qs}", qc[qs][p0:p0 + 32, mi, :], a["q"][mi * 32:(mi + 1) * 32, :], [], [f"qc{qs}"])
            jobslot[ji] = [(qc[qs], f"qc{qs}", 0), (qc[qs], f"qc{qs}", 1)]
        for r in range(4):
            K.dma("sp", f"kt{s}", kt[s][0:rows, r, :], a["k"](r), a["kdeps"], [f"kt{s}"])
        for r in range(4):
            K.dma("sp", f"vt{s}", vt[s][:, r * NBLK:(r + 1) * NBLK, :], a["v"](r), a["vdeps"], [f"vt{s}"])

    cnt = dict(s=0, p=0, o=0, f=0)
    out_ops = []
    K.dma("sp", "MAt_hi", MAt[:, 16:32, :], io["MA"][:, 16:32, :], [], ["MAt_hi"])
    load_job(0)
    K.dma("sp", "MAt_lo", MAt[:, 0:16, :], io["MA"][:, 0:16, :], [], ["MAt_lo"])
    K.dma("sp", "MBt", MBt[:], io["MB"], [], ["MBt"])
    for ji, (kind, h) in enumerate(jobs):
        if ji + 1 < len(jobs):
            load_job(ji + 1)
        s = ji % 2
        kk, vk = f"kt{s}", f"vt{s}"
        maps = jobslot[ji]
        if kind == "A":
            scale, row0 = 64 ** -0.5, h * 64
        elif kind == "B":
            scale, row0 = 96 ** -0.5, 384 + h * 64
        else:
            scale, row0 = 32 ** -0.5, 768 + h * 64
        og = ostg[ji % 2]
        ogk = f"ostg{ji % 2}"
        for g in range(4):
            blocks = []
            if kind == "A":
                for ak in (0, -1, -2, -3, -4, 1, 2, 3):
                    mloc = 4 * g + ak
                    if mloc < 0:
                        continue
                    c0, c1 = a_cols(ak)
                    for r in range(4):
                        blocks.append((r, mloc, c0, c1, MAt[:, (ak + 4) * 4 + r, :], "MAt_hi" if ak >= 0 else "MAt_lo"))
            else:
                for mloc in range(4 * g + 4):
                    ak = mloc - 4 * g
                    for r in range(4):
                        if ak < 0:
                            blocks.append((r, mloc, 0, 512, None, None))
                        else:
                            blocks.append((r, mloc, ak * 128, 512, MBt[:, ak * 4 + r, :], "MBt"))
            steps = [(b, mi) for b in blocks for mi in range(len(maps))]
            O = [banks[6 + mi] for mi in range(len(maps))]
            Ok = [f"bank{6 + mi}" for mi in range(len(maps))]
            pend = []
            nst = len(steps)
            first = [True] * len(maps)
            lastidx = [max(i for i, (b, mi) in enumerate(steps) if mi == m_) for m_ in range(len(maps))]
            assert nst % 2 == 0
            npair = nst // 2
            for ip in range(npair + LAG):
                if ip == min(1, npair - 1) and deferred_a:
                    for f in deferred_a:
                        f()
                    deferred_a.clear()
                if ip == min(7, npair - 1) and deferred:
                    for f in deferred:
                        f()
                    deferred.clear()
                if ip < npair:
                    sb_ = cnt["s"] % NSB
                    cnt["s"] += 1
                    pi = cnt["p"] % NPT
                    cnt["p"] += 1
                    SP = K.pairs[sb_]
                    spk, ptk = f"sp{sb_}", f"pt{pi}"
                    c0, c1 = steps[2 * ip][0][2], steps[2 * ip][0][3]
                    assert (steps[2 * ip + 1][0][2], steps[2 * ip + 1][0][3]) == (c0, c1)
                    for hf in range(2):
                        (r, mloc, _, _, mask, mkey), mi = steps[2 * ip + hf]
                        qbuf, qk, qm = maps[mi]
                        qsrc = qbuf[:, g * 512 + c0:g * 512 + c1] if qm is None else qbuf[:, qm, g * 512 + c0:g * 512 + c1]
                        K.mm(SP[:, hf, c0:c1], kt[s][:, r, mloc * 128:(mloc + 1) * 128], qsrc, True, True, [kk, qk], [spk])
                    K.act(pt[pi][:, :, c0:c1], SP[:, :, c0:c1], AF.Exp, [spk], [ptk], scale=float(scale))
                    for hf in range(2):
                        (r, mloc, _, _, mask, mkey), mi = steps[2 * ip + hf]
                        if mask is not None:
                            K.tt("dve", pt[pi][:, hf, c0:c1], pt[pi][:, hf, c0:c1], mask[:, c0:c1], ALU.mult, [ptk, mkey], [ptk])
                    pend.append((ip, c0, c1, pi))
                if ip >= LAG:
                    (ip0, c0, c1, pi) = pend.pop(0)
                    for hf in range(2):
                        i0 = 2 * ip0 + hf
                        (r, mloc, _, _, mask, mkey), mi = steps[i0]
                        K.mm(O[mi][0:65, c0:c1], vt[s][:, r * NBLK + mloc, 0:65], pt[pi][:, hf, c0:c1], first[mi], i0 == lastidx[mi],
                             [vk, f"pt{pi}"], [Ok[mi]])
                        first[mi] = False
            gsl = slice(g * 512, (g + 1) * 512)

            def fbank():
                b = cnt["s"] % NSB
                cnt["s"] += 1
                return K.pairs[b][:, 0, :], f"sp{b}"

            def prec(i):
                K.recip(rz[i][64:65, :], osb[i][64:65, :], [f"osb{i}"], [f"rz{i}"])

            if kind in ("A", "B"):
                fi = cnt["f"] % 4
                cnt["f"] += 1
                K.copy("act", osb[fi][0:65, :], O[0][0:65, :], [Ok[0]], [f"osb{fi}"])

                def part2a(fi=fi):
                    K.recip(rz[fi][64:65, :], osb[fi][64:65, :], [f"osb{fi}"], [f"rz{fi}"])

                def part2(fi=fi, og=og, ogk=ogk, gsl=gsl):
                    fb, fk = fbank()
                    K.mm(fb[0:64, :], C["one"][64:65, 0:64], rz[fi][64:65, :], True, True, [f"rz{fi}", "one"], [fk])
                    K.tt("dve", og[0:64, gsl], osb[fi][0:64, :], fb[0:64, :], ALU.mult, [f"osb{fi}", fk], [ogk])
            else:
                f0 = cnt["f"] % 4
                f1 = (cnt["f"] + 1) % 4
                cnt["f"] += 2
                K.copy("act", osb[f0][0:65, :], O[0][0:65, :], [Ok[0]], [f"osb{f0}"])
                K.copy("act", osb[f1][0:65, :], O[1][0:65, :], [Ok[1]], [f"osb{f1}"])

                def part2a(f0=f0, f1=f1):
                    K.recip(rz[f0][64:65, :], osb[f0][64:65, :], [f"osb{f0}"], [f"rz{f0}"])
                    K.recip(rz[f1][64:65, :], osb[f1][64:65, :], [f"osb{f1}"], [f"rz{f1}"])

                def part2(f0=f0, f1=f1, og=og, ogk=ogk, gsl=gsl):
                    fb, fk = fbank()
                    K.mm(fb[0:64, :], C["one"][64:65, 0:64], rz[f0][64:65, :], True, True, [f"rz{f0}", "one"], [fk])
                    K.tt("dve", tA[0:64, :], osb[f0][0:64, :], fb[0:64, :], ALU.mult, [f"osb{f0}", fk], ["tA"])
                    fb, fk = fbank()
                    K.mm(fb[0:64, :], C["one"][64:65, 0:64], rz[f1][64:65, :], True, True, [f"rz{f1}", "one"], [fk])
                    K.tt("dve", tB[0:64, :], osb[f1][0:64, :], fb[0:64, :], ALU.mult, [f"osb{f1}", fk], ["tB"])
                    K.stt("dve", tO[0:64, :], tB[0:64, :], neglam[0:64, 0:1], tA[0:64, :], ALU.mult, ALU.add, ["tA", "tB", "neglam"], ["tO"])
                    K.tt("pool", tS[0:64, :], tO[0:64, :], tO[0:64, :], ALU.mult, ["tO"], ["tS"])
                    fb, fk = fbank()
                    K.mm(fb[0:64, :], C["c64"][0:64, 0:64], tS[0:64, :], True, True, ["tS", "c64"], [fk])
                    K.act(tS[0:64, :], fb[0:64, :], AF.Sqrt, [fk, "eps5"], ["tS"], bias=C["eps5"][0:64, 0:1], scale=1.0)
                    K.recip(tS[0:64, :], tS[0:64, :], ["tS"], ["tS"])
                    K.stt("dve", og[0:64, gsl], tO[0:64, :], gc[0:64, 0:1], tS[0:64, :], ALU.mult, ALU.mult, ["tO", "tS", "gc"], [ogk])
            deferred_a.append(part2a)
            deferred.append(part2)
        deferred.append(lambda og=og, ogk=ogk, row0=row0: out_ops.append(
            K.dma("sp", ogk, io["mixT"][row0:row0 + 64, :], og[0:64, :], [ogk], [("mixT", row0)])))
    for f in deferred_a + deferred:
        f()
    deferred.clear()
    return out_ops


def simple_acc(io):
    def acc(kind, h):
        if kind == "A":
            return dict(rows=128, q=io["QA"][h * 64:(h + 1) * 64, :], k=lambda r: io["KAg"][r, (h // 2) * 128:(h // 2 + 1) * 128, :],
                        v=lambda r: io["VAg"][r, h], kdeps=[], vdeps=[])
        if kind == "B":
            return dict(rows=96, q=io["QB"][h], k=lambda r: io["KBg"][r, h], v=lambda r: io["VBg"][r, h], kdeps=[], vdeps=[])
        return dict(rows=128, q=io["QC"][h * 64:(h + 1) * 64, :], k=lambda r: io["KCg"][r, (h // 2) * 128:(h // 2 + 1) * 128, :],
                    v=lambda r: io["VCg"][r, h], kdeps=[], vdeps=[])
    return acc


def build_p2():
    nc = bass.Bass("TRN2", target_bir_lowering=False)
    with contextlib.ExitStack() as st:
        K = Ctx(nc, st)
        io = {}
        io["vecs"] = K.dram("vecs", [128, NVEC], F32, "ExternalInput")
        io["MA"] = K.dram("MA", [128, 32, 512], BF16, "ExternalInput")
        io["MB"] = K.dram("MB", [128, 16, 512], BF16, "ExternalInput")
        io["QA"] = K.dram("QA", [384, TL], BF16, "ExternalInput")
        io["QB"] = K.dram("QB", [6, 96, TL], BF16, "ExternalInput")
        io["QC"] = K.dram("QC", [256, TL], BF16, "ExternalInput")
        io["KAg"] = K.dram("KAg", [4, 384, TL], BF16, "ExternalInput")
        io["KBg"] = K.dram("KBg", [4, 6, 96, TL], BF16, "ExternalInput")
        io["KCg"] = K.dram("KCg", [4, 256, TL], BF16, "ExternalInput")
        io["VAg"] = K.dram("VAg", [4, 6, 128, NBLK, 65], BF16, "ExternalInput")
        io["VBg"] = K.dram("VBg", [4, 6, 128, NBLK, 65], BF16, "ExternalInput")
        io["VCg"] = K.dram("VCg", [4, 4, 128, NBLK, 65], BF16, "ExternalInput")
        io["mixT"] = K.dram("mixT", [1024, TL], BF16, "ExternalOutput")
        C = make_consts(K)
        io["acc"] = simple_acc(io)
        outs = phase2(K, C, io)
        K.P.emit(final_wait_ops=outs)
    return nc


def attn_masks(j):
    tk = np.arange(128)[:, None, None]
    aq = np.arange(4)[None, :, None]
    tq = np.arange(128)[None, None, :]
    MA = np.zeros((128, 32, 4, 128), np.float32)
    MB = np.zeros((128, 16, 4, 128), np.float32)
    for ak in range(-4, 4):
        for r in range(4):
            dist = (4 * (aq - ak) + j - r) * 128 + tq - tk
            cnt = ((dist >= 0) & (dist <= 128)).astype(np.float32)
            cnt += ((dist >= 0) & (dist <= 512) & (dist % 4 == 0))
            cnt += ((dist >= 0) & (dist <= 2048) & (dist % 16 == 0))
            MA[:, (ak + 4) * 4 + r] = cnt
            if ak >= 0:
                MB[:, ak * 4 + r] = (dist >= 0)
    return (MA.reshape(128, 32, 512).astype(ml_dtypes.bfloat16), MB.reshape(128, 16, 512).astype(ml_dtypes.bfloat16))


def phase3(K, C, io, last):
    P = K.P
    banks = K.banks
    vec = K.sb("vec", [128, NVEC], F32)
    K.dma("sp", "vec", vec[:], io["vecs"], [], ["vec"])
    xT = K.sb("xT", [128, 8, TL], F32)
    mx = K.sb("mx", [128, 8, TL], BF16)
    aT = [K.sb(f"aT{i}", [128, 4, TL], BF16) for i in range(2)]
    wsl = [K.sb(f"wsl{i}", [128, 4096], BF16) for i in range(2)]
    rl = [K.sb(f"rl{i}", [128, 512], F32) for i in range(2)]
    sq = [K.sb(f"sq{i}", [128, 512], F32) for i in range(2)]
    sqk = ["sq0", "sq1"]
    rstd = [K.sb(f"rstd{i}", [128, 512], F32) for i in range(2)]
    mixv = io["mixT"].rearrange("(c p) t -> p c t", p=128)
    for t in range(4):
        sl = slice(t * 512, (t + 1) * 512)
        K.dma("sp", f"mx{t}", mx[:, :, sl], mixv[:, :, sl], [], [("mx", t)])
        K.dma("sp", f"xT{t}", xT[:, :, sl], io["xT"][:, :, sl], [], [("xT", t)])
    items = [(io["Wo"][:, :, hf * 512:(hf + 1) * 512], [128, 8, 512]) for hf in range(2)]
    for fg in range(8):
        items.append((io["Wup"][:, :, fg * 512:(fg + 1) * 512], [128, 8, 512]))
        items.append((io["Wdown"][:, fg * 4:(fg + 1) * 4, :], [128, 4, 1024]))
    ws = WStream(K, wsl, ["wsl0", "wsl1"], items)
    pc = [0]

    def nbank():
        b = pc[0] % 4
        pc[0] += 1
        return banks[b], f"bank{b}"

    for hf in range(2):
        w, wk = ws.next()
        for dl in range(4):
            d = hf * 4 + dl
            for t in range(4):
                sl = slice(t * 512, (t + 1) * 512)
                ps, pk = nbank()
                for c in range(8):
                    K.mm(ps[:, :], w[:, c, dl * 128:(dl + 1) * 128], mx[:, c, sl], c == 0, c == 7, [wk, ("mx", t)], [pk])
                K.tt("dve", xT[:, d, sl], ps[:, :], xT[:, d, sl], ALU.add, [pk, ("xT", t)], [("xT", t)])
    for t in range(4):
        sl = slice(t * 512, (t + 1) * 512)
        rs, rk = rstd[t % 2], f"rstd{t % 2}"
        rms_stats(K, C, [xT[:, c, sl] for c in range(8)], [("xT", t)] * 8, 128, "c1024", "eps6", banks[7], "bank7", rs[:], rk, sq, sqk)
        for c in range(8):
            K.stt("dve", mx[:, c, sl], xT[:, c, sl], vec[:, 8 + c:9 + c], rs[:], ALU.mult, ALU.mult, [("xT", t), rk, "vec"], [("mx", t)])
    rc = 0
    for fg in range(8):
        wu, wuk = ws.next()
        a = aT[fg % 2]
        ak = f"aT{fg % 2}"
        for fl in range(4):
            for t in range(4):
                sl = slice(t * 512, (t + 1) * 512)
                ps, pk = nbank()
                for c in range(8):
                    K.mm(ps[:, :], wu[:, c, fl * 128:(fl + 1) * 128], mx[:, c, sl], c == 0, c == 7, [wuk, ("mx", t)], [pk])
                ri = rc % 2
                rc += 1
                K.act(rl[ri][:, :], ps[:, :], AF.Relu, [pk], [f"rl{ri}"])
                K.tt("pool", a[:, fl, sl], rl[ri][:, :], rl[ri][:, :], ALU.mult, [f"rl{ri}"], [(ak, t)])
        wd, wdk = ws.next()
        for d in range(8):
            for t in range(4):
                sl = slice(t * 512, (t + 1) * 512)
                ps, pk = nbank()
                for fl in range(4):
                    K.mm(ps[:, :], wd[:, fl, d * 128:(d + 1) * 128], a[:, fl, sl], fl == 0, fl == 3, [wdk, (ak, t)], [pk])
                K.tt("dve", xT[:, d, sl], ps[:, :], xT[:, d, sl], ALU.add, [pk, ("xT", t)], [("xT", t)])
    outs = []
    if last:
        for t in range(4):
            sl = slice(t * 512, (t + 1) * 512)
            rs, rk = rstd[t % 2], f"rstd{t % 2}"
            rms_stats(K, C, [xT[:, c, sl] for c in range(8)], [("xT", t)] * 8, 128, "c1024", "eps6", banks[7], "bank7", rs[:], rk, sq, sqk)
            for c in range(8):
                K.stt("dve", xT[:, c, sl], xT[:, c, sl], vec[:, 16 + c:17 + c], rs[:], ALU.mult, ALU.mult, [("xT", t), rk, "vec"], [("xT", t)])
    for t in range(4):
        sl = slice(t * 512, (t + 1) * 512)
        outs.append(K.dma("sp", "xTo", io["xout"][:, :, sl], xT[:, :, sl], [("xT", t)], [("xout", t)]))
    return outs


def build_p3(last):
    nc = bass.Bass("TRN2", target_bir_lowering=False)
    with contextlib.ExitStack() as st:
        K = Ctx(nc, st)
        io = {}
        io["vecs"] = K.dram("vecs", [128, NVEC], F32, "ExternalInput")
        io["xT"] = K.dram("xT", [128, 8, TL], F32, "ExternalInput")
        io["mixT"] = K.dram("mixT", [1024, TL], BF16, "ExternalInput")
        io["Wo"] = K.dram("Wo", [128, 8, 1024], F32, "ExternalInput")
        io["Wup"] = K.dram("Wup", [128, 8, DFF], F32, "ExternalInput")
        io["Wdown"] = K.dram("Wdown", [128, 32, 1024], F32, "ExternalInput")
        io["xout"] = K.dram("xout", [128, 8, TL], F32, "ExternalOutput")
        C = make_consts(K)
        outs = phase3(K, C, io, last)
        K.P.emit(final_wait_ops=outs)
    return nc


_PROGS = {}


def _prog(name):
    if name not in _PROGS:
        _PROGS[name] = dict(p1=build_p1, p2=build_p2, p3a=lambda: build_p3(False), p3b=lambda: build_p3(True))[name]()
    return _PROGS[name]


def _run(nc, in_maps):
    res = run_bass_kernel_spmd(nc, in_maps, core_ids=list(range(NCORE)))
    return res.results


def kernel_unfused(**inputs):
    x = np.asarray(inputs["x"], dtype=np.float32)
    xs = [x_to_core(x, c) for c in range(NCORE)]
    ropes = [rope_tables(j) for j in range(4)]
    masks = [attn_masks(j) for j in range(4)]
    for l in range(2):
        lw = prep_layer(inputs, l)
        r1 = _run(_prog("p1"), [dict(xT=xs[c], W1=lw["W1"], Wuq=lw["Wuq"], Wukv=lw["Wukv"], vecs=lw["vecs"], rope=ropes[c % 4])
                                for c in range(NCORE)])
        in2 = []
        gath = {}
        for b in range(2):
            for k in ("KA", "KB", "KC", "VA", "VB", "VC"):
                gath[(b, k)] = np.stack([np.asarray(r1[4 * b + j][k]) for j in range(4)], 0)
        for c in range(NCORE):
            b, j = c // 4, c % 4
            in2.append(dict(vecs=lw["vecs"], MA=masks[j][0], MB=masks[j][1],
                            QA=np.asarray(r1[c]["QA"]), QB=np.asarray(r1[c]["QB"]), QC=np.asarray(r1[c]["QC"]),
                            KAg=gath[(b, "KA")], KBg=gath[(b, "KB")], KCg=gath[(b, "KC")],
                            VAg=gath[(b, "VA")], VBg=gath[(b, "VB")], VCg=gath[(b, "VC")]))
        r2 = _run(_prog("p2"), in2)
        r3 = _run(_prog("p3b" if l == 1 else "p3a"),
                  [dict(vecs=lw["vecs"], xT=xs[c], mixT=np.asarray(r2[c]["mixT"]), Wo=lw["Wo"], Wup=lw["Wup"], Wdown=lw["Wdown"])
                   for c in range(NCORE)])
        xs = [np.asarray(r3[c]["xout"]) for c in range(NCORE)]
    out = np.empty((2, SEQ, D), np.float32)
    for c in range(NCORE):
        b, j = c // 4, c % 4
        xt = xs[c].transpose(1, 0, 2).reshape(D, TL)
        out[b, core_positions(j), :] = xt.T
    return out


KROWS = 384 + 576 + 256
RG = [[0, 1, 2, 3], [4, 5, 6, 7]]


def _phase(nc, tag, body, ext_sems=None):
    with nc.cleanup_on_exit():
        with contextlib.ExitStack() as st:
            K = Ctx(nc, st, tag, ext_sems)
            C = make_consts(K)
            outs = body(K, C)
            K.P.emit(final_wait_ops=outs, tag=tag, managed=False)
        nc.all_engine_barrier()


def build_fused():
    nc = bass.Bass("TRN2", target_bir_lowering=False, num_devices=NCORE)
    ext = lambda name, shape, dt: nc.dram_tensor(name, shape, dt, kind="ExternalInput").ap()
    internal = lambda name, shape, dt: nc.dram_tensor(name, shape, dt, kind="Internal").ap()
    xT_in = ext("xT", [128, 8, TL], F32)
    rope = ext("rope", [4, 128, TL], F32)
    MA = ext("MA", [128, 32, 512], BF16)
    MB = ext("MB", [128, 16, 512], BF16)
    xout = nc.dram_tensor("xout", [128, 8, TL], F32, kind="ExternalOutput").ap()
    x1T = internal("x1T", [128, 8, TL], F32)
    agsem = {g: nc.alloc_semaphore(name="agsem" + g) for g in "ABC"}
    ag_val = {g: 0 for g in "ABC"}
    for l in range(2):
        W = dict(W1=ext(f"W1_{l}", [128, 8, W1COLS], F32), Wuq=ext(f"Wuq_{l}", [128, 3, 1152], F32),
                 Wukv=ext(f"Wukv_{l}", [128, 768], F32), vecs=ext(f"vecs_{l}", [128, NVEC], F32),
                 Wo=ext(f"Wo_{l}", [128, 8, 1024], F32), Wup=ext(f"Wup_{l}", [128, 8, DFF], F32),
                 Wdown=ext(f"Wdown_{l}", [128, 32, 1024], F32))
        kloc2 = internal(f"kloc{l}", [KROWS * 8, 256], BF16)
        vloc2 = internal(f"vloc{l}", [16 * 520, 256], BF16)
        kloc = kloc2.rearrange("(r a) b -> r (a b)", a=8)
        QA = internal(f"QA{l}", [384, TL], BF16)
        QB = internal(f"QB{l}", [6, 96, TL], BF16)
        QC = internal(f"QC{l}", [256, TL], BF16)
        mixT = internal(f"mixT{l}", [1024, TL], BF16)
        x_cur = xT_in if l == 0 else x1T
        x_next = x1T if l == 0 else xout
        vl = vloc2.rearrange("(h x) b -> h (x b)", h=16).rearrange("h (p m e) -> h p m e", p=128, m=NBLK)
        io1 = dict(xT=x_cur, W1=W["W1"], Wuq=W["Wuq"], Wukv=W["Wukv"], vecs=W["vecs"], rope=[rope[i] for i in range(4)],
                   QA=QA, QB=QB, QC=QC, KA=kloc[0:384, :], KB=kloc[384:960, :].rearrange("(h d) t -> h d t", h=6),
                   KC=kloc[960:KROWS, :], VA=vl[0:6], VB=vl[6:12], VC=vl[12:16])

        pieces = []
        kp = {}
        for nm, r0, nrows, npc in (("KA", 0, 128, 3), ("KB", 384, 192, 3), ("KC", 960, 128, 2)):
            for p in range(npc):
                key = f"g{nm}{p}"
                g = internal(f"{key}_{l}", [4 * nrows * 8, 256], BF16)
                if nm == "KB":
                    deps = [("KB", 2 * p), ("KBr", 2 * p), ("KB", 2 * p + 1), ("KBr", 2 * p + 1)]
                else:
                    deps = [(nm, p * 128)]
                pieces.append((key, kloc2[(r0 + p * nrows) * 8:(r0 + (p + 1) * nrows) * 8, :], g, deps))
                kp[key] = g.rearrange("(r q a) b -> r q (a b)", r=4, a=8)
        for nm, h0, nh, npc in (("VA", 0, 3, 2), ("VB", 6, 3, 2), ("VC", 12, 2, 2)):
            for p in range(npc):
                key = f"g{nm}{p}"
                g = internal(f"{key}_{l}", [4 * nh * 520, 256], BF16)
                deps = [(nm, p * nh + i) for i in range(nh)]
                pieces.append((key, vloc2[(h0 + p * nh) * 520:(h0 + (p + 1) * nh) * 520, :], g, deps))
                kp[key] = g.rearrange("(r h x) b -> r h (x b)", r=4, h=nh).rearrange("r h (p m e) -> r h p m e", p=128, m=NBLK)
        io1["ag_pieces"] = pieces
        io1["ag_ops"] = {}

        def body1(K, C, io1=io1):
            phase1(K, C, io1)
            return [o for o in K.P.ops["sp"] if o.is_dma and o.dma_sem.startswith(("stg", "vst", "krT_out"))]

        _phase(nc, f"L{l}a_", body1, ext_sems={"ag" + g: (agsem[g], ag_val[g]) for g in "ABC"})
        assert len(io1["ag_ops"]) == len(pieces), (len(io1["ag_ops"]), len(pieces))
        for g in "ABC":
            ag_val[g] = max(op.dma_cnt for key, op in io1["ag_ops"].items() if key[2] == g)
        ag_done = {key: ag_val[key[2]] for key in io1["ag_ops"]}

        def acc(kind, h, kp=kp, QA=QA, QB=QB, QC=QC):
            if kind == "A":
                kk, vk = f"gKA{h // 2}", f"gVA{h // 3}"
                return dict(rows=128, q=QA[h * 64:(h + 1) * 64, :], k=lambda r: kp[kk][r, :, :],
                            v=lambda r: kp[vk][r, h % 3], kdeps=[kk], vdeps=[vk])
            if kind == "B":
                kk, vk = f"gKB{h // 2}", f"gVB{h // 3}"
                return dict(rows=96, q=QB[h], k=lambda r: kp[kk][r, (h % 2) * 96:(h % 2 + 1) * 96, :],
                            v=lambda r: kp[vk][r, h % 3], kdeps=[kk], vdeps=[vk])
            kk, vk = f"gKC{h // 2}", f"gVC{h // 2}"
            return dict(rows=128, q=QC[h * 64:(h + 1) * 64, :], k=lambda r: kp[kk][r, :, :],
                        v=lambda r: kp[vk][r, h % 2], kdeps=[kk], vdeps=[vk])

        io2 = dict(vecs=W["vecs"], MA=MA, MB=MB, mixT=mixT, acc=acc)

        def body2(K, C, io2=io2, ag_done=ag_done):
            for key, val in ag_done.items():
                K.P.external("ag" + key[2], val, [key])
            return phase2(K, C, io2)

        _phase(nc, f"L{l}b_", body2, ext_sems={"ag" + g: (agsem[g], ag_val[g]) for g in "ABC"})
        io3 = dict(vecs=W["vecs"], xT=x_cur, mixT=mixT, Wo=W["Wo"], Wup=W["Wup"], Wdown=W["Wdown"], xout=x_next)
        _phase(nc, f"L{l}c_", lambda K, C, io3=io3, l=l: phase3(K, C, io3, l == 1))
    return nc


def kernel(**inputs):
    x = np.asarray(inputs["x"], dtype=np.float32)
    if "fused" not in _PROGS:
        _PROGS["fused"] = build_fused()
    lws = [prep_layer(inputs, l) for l in range(2)]
    ropes = [rope_tables(j) for j in range(4)]
    masks = [attn_masks(j) for j in range(4)]
    in_maps = []
    for c in range(NCORE):
        j = c % 4
        m = dict(xT=x_to_core(x, c), rope=ropes[j], MA=masks[j][0], MB=masks[j][1])
        for l in range(2):
            for k in ("W1", "Wuq", "Wukv", "vecs", "Wo", "Wup", "Wdown"):
                m[f"{k}_{l}"] = lws[l][k]
        in_maps.append(m)
    res = _run(_PROGS["fused"], in_maps)
    out = np.empty((2, SEQ, D), np.float32)
    for c in range(NCORE):
        b, j = c // 4, c % 4
        xt = np.asarray(res[c]["xout"]).transpose(1, 0, 2).reshape(D, TL)
        out[b, core_positions(j), :] = xt.T
    return out
```

```python
import contextlib
import math
import numpy as np
import ml_dtypes
import concourse.bass as bass
import concourse.mybir as mybir
from concourse.bass_utils import run_bass_kernel_spmd

F32 = mybir.dt.float32
BF16 = mybir.dt.bfloat16
ALU = mybir.AluOpType
AF = mybir.ActivationFunctionType

D = 1024
SEQ = 8192
NCORE = 8
TL = 2048
NBLK = 16
DFF = 4096
OFF = dict(qa=0, ka=384, va=768, cq=1152, ckv=1536, kr=1664, qc=1696, kc=1952, vc=2208)
W1COLS = 3904
NVEC = 160

import os
PROBE_K128 = bool(os.environ.get("PROBE_K128"))
COMPUTE = ("pe", "act", "dve", "pool")
SEM_WRAP = 30000


class Op:
    __slots__ = ("eng", "fn", "deps", "idx", "sig", "dma_sem", "dma_cnt", "is_dma", "sigidx")

    def __init__(self, eng, fn, is_dma=False):
        self.eng = eng
        self.fn = fn
        self.deps = []
        self.sig = False
        self.is_dma = is_dma
        self.dma_sem = None
        self.dma_cnt = 0
        self.sigidx = -1
        self.idx = -1


class Prog:
    def __init__(self, nc, ext_sems=None):
        self.nc = nc
        self.ops = {e: [] for e in ("pe", "act", "dve", "pool", "sp")}
        self.lastw = {}
        self.readers = {}
        self.dma_sems = {}
        self.ext_sems = dict(ext_sems or {})
        for name, (h, base) in self.ext_sems.items():
            self.dma_sems[name] = base

    def external(self, semname, value, writes):
        o = Op("pool", None, is_dma=True)
        o.dma_sem = semname
        o.dma_cnt = value
        for k in writes:
            self.lastw[k] = o
            self.readers[k] = []
        return o

    def _add(self, op, reads, writes):
        deps = []
        for k in reads:
            w = self.lastw.get(k)
            if w is not None:
                deps.append(w)
        for k in writes:
            w = self.lastw.get(k)
            if w is not None:
                deps.append(w)
            deps.extend(self.readers.get(k, ()))
        seen = set()
        for d in deps:
            if d is op or id(d) in seen:
                continue
            seen.add(id(d))
            op.deps.append(d)
        for k in reads:
            lst = self.readers.setdefault(k, [])
            if not op.is_dma:
                for i, r in enumerate(lst):
                    if (not r.is_dma) and r.eng == op.eng:
                        lst[i] = op
                        break
                else:
                    lst.append(op)
            else:
                lst.append(op)
        for k in writes:
            self.lastw[k] = op
            self.readers[k] = []
        op.idx = len(self.ops[op.eng])
        self.ops[op.eng].append(op)
        return op

    def op(self, eng, fn, reads=(), writes=()):
        return self._add(Op(eng, fn), reads, writes)

    def dma(self, eng, semname, fn, reads=(), writes=(), inc=16):
        o = Op(eng, fn, is_dma=True)
        self.dma_sems[semname] = self.dma_sems.get(semname, 0) + inc
        o.dma_sem = semname
        o.dma_cnt = self.dma_sems[semname]
        o.sigidx = inc
        return self._add(o, reads, writes)

    @staticmethod
    def _skip_same(d, o):
        return (not d.is_dma) and d.eng == o.eng and (d.eng == "pe" or o.idx - d.idx > 2)

    def emit(self, final_wait_ops=(), tag="", managed=True):
        nc = self.nc
        for e, lst in self.ops.items():
            for o in lst:
                for d in o.deps:
                    if d.is_dma or self._skip_same(d, o):
                        continue
                    d.sig = True
        for o in final_wait_ops:
            if not o.is_dma:
                o.sig = True
        nsig = {}
        for e in COMPUTE:
            c = 0
            for o in self.ops[e]:
                if o.sig and not o.is_dma:
                    o.sigidx = c
                    c += 1
            nsig[e] = c
        with contextlib.ExitStack() as st:
            if managed:
                mksem = lambda nm: st.enter_context(nc.semaphore(tag + nm))
            else:
                mksem = lambda nm: nc.alloc_semaphore(name=tag + nm)
            csem = {}
            for e in COMPUTE:
                n = max(1, (nsig[e] + SEM_WRAP - 1) // SEM_WRAP)
                csem[e] = [mksem(f"c_{e}_{i}") for i in range(n)]
            dsem = {name: (self.ext_sems[name][0] if name in self.ext_sems else mksem(f"d_{name}")) for name in self.dma_sems}
            block = st.enter_context(nc.Block())
            engobj = {"pe": "tensor", "act": "scalar", "dve": "vector", "pool": "gpsimd", "sp": "sync"}

            def run_engine(e, eng):
                waited = {}
                for o in self.ops[e]:
                    need = {}
                    for d in o.deps:
                        if d.is_dma:
                            key = ("d", d.dma_sem)
                            val = d.dma_cnt
                            sem = dsem[d.dma_sem]
                        else:
                            if self._skip_same(d, o):
                                continue
                            si = d.sigidx // SEM_WRAP
                            key = ("c", d.eng, si)
                            val = d.sigidx % SEM_WRAP + 1
                            sem = csem[d.eng][si]
                        if waited.get(key, 0) >= val:
                            continue
                        if key not in need or need[key][1] < val:
                            need[key] = (sem, val)
                    for key, (sem, val) in need.items():
                        eng.wait_ge(sem, val)
                        waited[key] = val
                    ins = o.fn(eng)
                    if o.is_dma:
                        ins.then_inc(dsem[o.dma_sem], o.sigidx)
                    elif o.sig:
                        ins.then_inc(csem[e][o.sigidx // SEM_WRAP], 1)
                if e == "sp":
                    fin = {}
                    for o in final_wait_ops:
                        if o.is_dma:
                            key, sem, val = ("d", o.dma_sem), dsem[o.dma_sem], o.dma_cnt
                        else:
                            si = o.sigidx // SEM_WRAP
                            key, sem, val = ("c", o.eng, si), csem[o.eng][si], o.sigidx % SEM_WRAP + 1
                        if key not in fin or fin[key][1] < val:
                            fin[key] = (sem, val)
                    for sem, val in fin.values():
                        eng.wait_ge(sem, val)

            for e in ("sp", "pool", "act", "dve", "pe"):
                getattr(block, engobj[e])(lambda eng, e=e: run_engine(e, eng))


class Ctx:
    def __init__(self, nc, st, tag="", ext_sems=None):
        self.nc = nc
        self.st = st
        self.tag = tag
        self.P = Prog(nc, ext_sems)
        self.pairs = [st.enter_context(nc.psum_tensor(f"{tag}pbank{i}", [128, 2, 512], F32)) for i in range(4)]
        self.banks = [self.pairs[i // 2][:, i % 2, :] for i in range(8)]
        self.uid = 0

    def sb(self, name, shape, dt):
        return self.st.enter_context(self.nc.sbuf_tensor(self.tag + "s_" + name, shape, dt))

    def dram(self, name, shape, dt, kind):
        return self.nc.dram_tensor(name, shape, dt, kind=kind).ap()

    def mm(self, out, lhsT, rhs, start, stop, reads, writes, **kw):
        return self.P.op("pe", lambda e: e.matmul(out, lhsT, rhs, start=start, stop=stop, **kw), reads, writes)

    def act(self, out, in_, func, reads, writes, bias=None, scale=None):
        kw = {}
        if bias is not None:
            kw["bias"] = bias
        if scale is not None:
            kw["scale"] = scale
        return self.P.op("act", lambda e: e.activation(out=out, in_=in_, func=func, **kw), reads, writes)

    def tt(self, eng, out, in0, in1, op, reads, writes):
        return self.P.op(eng, lambda e: e.tensor_tensor(out=out, in0=in0, in1=in1, op=op), reads, writes)

    def ts(self, eng, out, in0, s1, op0, reads, writes, s2=None, op1=None):
        if op1 is None:
            return self.P.op(eng, lambda e: e.tensor_scalar(out=out, in0=in0, scalar1=s1, scalar2=None, op0=op0), reads, writes)
        return self.P.op(eng, lambda e: e.tensor_scalar(out=out, in0=in0, scalar1=s1, scalar2=s2, op0=op0, op1=op1), reads, writes)

    def stt(self, eng, out, in0, scalar, in1, op0, op1, reads, writes):
        return self.P.op(eng, lambda e: e.scalar_tensor_tensor(out=out, in0=in0, scalar=scalar, in1=in1, op0=op0, op1=op1), reads, writes)

    def copy(self, eng, out, in_, reads, writes):
        if eng == "act":
            return self.P.op("act", lambda e: e.copy(out=out, in_=in_), reads, writes)
        return self.P.op(eng, lambda e: e.tensor_copy(out=out, in_=in_), reads, writes)

    def recip(self, out, in_, reads, writes):
        return self.P.op("dve", lambda e: e.reciprocal(out=out, in_=in_), reads, writes)

    def memset(self, eng, ap, val, writes):
        return self.P.op(eng, lambda e: e.memset(ap, val), (), writes)

    def dma(self, q, sem, out, in_, reads, writes):
        return self.P.dma(q, sem, lambda e: e.dma_start(out=out, in_=in_), reads, writes)


class WStream:
    def __init__(self, K, slots, keys, items):
        self.K, self.slots, self.keys, self.items = K, slots, keys, items
        self.issued = 0
        self.cur = 0

    def _issue(self):
        i = self.issued
        if i >= len(self.items):
            return
        src, shape = self.items[i]
        n = 1
        for d in shape[1:]:
            n *= d
        ns = len(self.slots)
        dst = self.slots[i % ns][:, 0:n]
        if len(shape) == 3:
            dst = dst.rearrange("p (a b) -> p a b", a=shape[1])
        key = self.keys[i % ns]
        self.K.P.dma("pool", key, lambda e: e.dma_start(out=dst, in_=src), [], [key])
        self.views = getattr(self, "views", {})
        self.views[i] = (dst, key)
        self.issued += 1

    def next(self):
        while self.issued <= min(self.cur + 1, len(self.items) - 1):
            self._issue()
        v = self.views[self.cur]
        self.cur += 1
        return v


def make_consts(K):
    c = {}
    for name, val in (("c1024", 1.0 / 1024), ("c384", 1.0 / 384), ("c128", 1.0 / 128), ("c64", 1.0 / 64), ("one", 1.0)):
        t = K.sb(name, [128, 128], F32)
        K.memset("pool", t[:], val, [name])
        c[name] = t
    for name, val in (("eps6", 1e-6), ("eps5", 1e-5)):
        t = K.sb(name, [128, 1], F32)
        K.memset("pool", t[:], val, [name])
        c[name] = t
    return c


def rms_stats(K, C, srcs, src_keys, rows, cname, epsname, bank, bank_key, rstd, rstd_key, sq, sqkeys):
    n = len(srcs)
    for i, s in enumerate(srcs):
        q = sq[i % 2]
        K.act(q[0:rows, :], s, AF.Square, [src_keys[i]], [sqkeys[i % 2]])
        K.mm(bank[:, :], C[cname][0:rows, :], q[0:rows, :], i == 0, i == n - 1, [sqkeys[i % 2], cname], [bank_key])
    K.act(rstd, bank[:, :], AF.Sqrt, [bank_key, epsname], [rstd_key], bias=C[epsname][:, 0:1], scale=1.0)
    K.recip(rstd, rstd, [rstd_key], [rstd_key])


FM_GROUPS = [
    (0, 512, [("ropeA", 0, 128, "QA", 0), ("ropeA", 256, 128, "QA", 128)]),
    (512, 512, [("ropeA", 0, 128, "QA", 256), ("ropeA", 256, 128, "KA", 0)]),
    (1024, 512, [("ropeA", 0, 128, "KA", 128), ("ropeA", 256, 128, "KA", 256)]),
    (1536, 512, [("ropeC", 0, 128, "QC", 0), ("ropeC", 256, 128, "QC", 128)]),
    (2048, 512, [("ropeC", 0, 128, "KC", 0), ("ropeC", 256, 128, "KC", 128)]),
    (2560, 512, [("cq", 0, 128, None, 0), ("cq", 128, 128, None, 1), ("cq", 256, 128, None, 2), ("ckv", 384, 128, None, 0)]),
    (3072, 192, [("kr", 0, 96, None, 0)]),
]
TM_GROUPS = [(3264, 384, 6, "VA"), (3648, 256, 4, "VC")]


def phase1(K, C, io):
    P = K.P
    banks = K.banks
    xT = io["xT"]
    W1 = io["W1"]
    vec = K.sb("vec", [128, NVEC], F32)
    K.dma("sp", "vec", vec[:], io["vecs"], [], ["vec"])
    xs = [K.sb("xs0", [128, 8, 512], F32)] * 2
    hT = K.sb("hT", [128, 8, TL], BF16)
    rstd = [K.sb(f"rstd{i}", [128, 512], F32) for i in range(2)]
    sq = [K.sb(f"sq{i}", [128, 512], F32) for i in range(2)]
    sqk = ["sq0", "sq1"]
    wsl = [K.sb(f"wsl{i}", [128, 4096], BF16) for i in range(2)]
    t64c = K.sb("t64c", [128, TL], F32)
    t64s = K.sb("t64s", [128, TL], F32)
    t32c = K.sb("t32c", [128, TL], F32)
    t32s = K.sb("t32s", [128, TL], F32)
    NSTG = 6
    stg = [K.sb(f"stg{i}", [128, TL], BF16) for i in range(NSTG)]
    r1 = [K.sb(f"r1_{i}", [128, 512], F32) for i in range(2)]
    r2 = [K.sb(f"r2_{i}", [128, 512], F32) for i in range(2)]
    cqT = K.sb("cqT", [128, 3, TL], F32)
    ckvT = K.sb("ckvT", [128, TL], F32)
    cqn = K.sb("cqn", [128, 3, TL], BF16)
    ckvn = K.sb("ckvn", [128, TL], BF16)
    krT = K.sb("krT", [128, TL], BF16)
    vst = [K.sb("vst0", [128, 6, NBLK, 65], BF16)] * 2
    K.memset("pool", vst[0][:], 1.0, ["vst0"])

    for t in range(4):
        x = xs[t % 2]
        xk = "xs0"
        K.dma("sp", xk, x[:], xT[:, :, t * 512:(t + 1) * 512], [], [xk])
        rs = rstd[t % 2]
        rk = f"rstd{t % 2}"
        rms_stats(K, C, [x[:, c, :] for c in range(8)], [xk] * 8, 128, "c1024", "eps6", banks[7], "bank7", rs[:], rk, sq, sqk)
        for c in range(8):
            K.stt("dve", hT[:, c, t * 512:(t + 1) * 512], x[:, c, :], vec[:, c:c + 1], rs[:], ALU.mult, ALU.mult,
                  [xk, rk, "vec"], [("hT", t)])

    K.dma("sp", "t64c", t64c[:], io["rope"][0], [], ["t64c"])
    K.dma("sp", "t64s", t64s[:], io["rope"][1], [], ["t64s"])
    K.dma("sp", "t32c", t32c[:], io["rope"][2], [], ["t32c"])
    K.dma("sp", "t32s", t32s[:], io["rope"][3], [], ["t32s"])
    fmi = lambda gs: [(W1[:, :, goff:goff + gcols], [128, 8, gcols]) for (goff, gcols, _) in gs]
    tmi = lambda gs: [(W1[:, :, goff:goff + gcols], [128, 8, gcols]) for (goff, gcols, _, _) in gs]
    items = fmi(FM_GROUPS[5:7]) + tmi(TM_GROUPS[0:1]) + fmi(FM_GROUPS[0:3]) + fmi(FM_GROUPS[3:5]) + tmi(TM_GROUPS[1:2])
    items += [(io["Wuq"], [128, 3, 1152]), (io["Wukv"], [128, 768])]
    ws = WStream(K, wsl, ["wsl0", "wsl1"], items)
    wload = lambda src, shape: ws.next()

    pcnt = [0]
    scnt = [0]
    rcnt = [0]

    def rope(psA, psB, ka, kb, cosT, sinT, ck, sk, out, okey, p0, p1, t):
        i = rcnt[0] % 2
        rcnt[0] += 1
        sl = slice(t * 512, (t + 1) * 512)
        K.tt("dve", r1[i][p0:p1, :], psA[p0:p1, :], cosT[p0:p1, sl], ALU.mult, [ka, ck], [f"r1_{i}"])
        K.tt("dve", r2[i][p0:p1, :], psB[p0:p1, :], sinT[p0:p1, sl], ALU.mult, [kb, sk], [f"r2_{i}"])
        K.tt("pool", out, r1[i][p0:p1, :], r2[i][p0:p1, :], ALU.add, [f"r1_{i}", f"r2_{i}"], [okey])

    ag_pending = list(io.get("ag_pieces", []))
    ag_ready = []

    def try_ag(flush=False):
        for item in ag_ready[:]:
            key, src, dst, deps = item
            K.P.dma("pool", "ag" + key[2], lambda e, src=src, dst=dst: e.collective_compute(
                "AllGather", ALU.bypass, replica_groups=RG, ins=[src], outs=[dst[:, :]]), deps, [key], inc=1)
            io["ag_ops"][key] = K.P.lastw[key]
            ag_ready.remove(item)
        for item in ag_pending[:]:
            if all(k in K.P.lastw for k in item[3]):
                ag_pending.remove(item)
                ag_ready.append(item)
        if flush and (ag_ready or ag_pending):
            try_ag(flush=True)

    def fm_groups(groups):
        for (goff, gcols, tiles) in groups:
            w, wk = wload(W1[:, :, goff:goff + gcols], [128, 8, gcols])
            for (kind, loff, M, oname, orow) in tiles:
                roped = kind in ("ropeA", "ropeC", "kr")
                si = scnt[0] % NSTG
                if kind in ("ropeA", "ropeC"):
                    scnt[0] += 1
                for t in range(4):
                    sl = slice(t * 512, (t + 1) * 512)
                    ba = pcnt[0] % 4
                    pcnt[0] += 1
                    psA = banks[ba]
                    ka = f"bank{ba}"
                    for c in range(8):
                        K.mm(psA[0:M, :], w[:, c, loff:loff + M], hT[:, c, sl], c == 0, c == 7, [wk, ("hT", t)], [ka])
                    if roped:
                        bb = pcnt[0] % 4
                        pcnt[0] += 1
                        psB = banks[bb]
                        kb = f"bank{bb}"
                        so = loff + (128 if kind != "kr" else 96)
                        for c in range(8):
                            K.mm(psB[0:M, :], w[:, c, so:so + M], hT[:, c, sl], c == 0, c == 7, [wk, ("hT", t)], [kb])
                    if kind == "ropeA":
                        rope(psA, psB, ka, kb, t64c, t64s, "t64c", "t64s", stg[si][:, sl], f"stg{si}", 0, 128, t)
                    elif kind == "ropeC":
                        rope(psA, psB, ka, kb, t32c, t32s, "t32c", "t32s", stg[si][:, sl], f"stg{si}", 0, 128, t)
                    elif kind == "kr":
                        rope(psA, psB, ka, kb, t32c, t32s, "t32c", "t32s", krT[64:96, sl], "krT", 64, 96, t)
                    elif kind == "cq":
                        K.copy("act", cqT[:, orow, sl], psA[:, :], [ka], [("cqT", t)])
                    elif kind == "ckv":
                        K.copy("act", ckvT[:, sl], psA[:, :], [ka], [("ckvT", t)])
                if kind in ("ropeA", "ropeC"):
                    K.dma("sp", f"stg{si}", io[oname][orow:orow + 128, :], stg[si][:, :], [f"stg{si}"], [(oname, orow)])
                    try_ag()


    def tm_groups(groups, vcnt):
        for (goff, gcols, nh, oname) in groups:
            w, wk = wload(W1[:, :, goff:goff + gcols], [128, 8, gcols])
            vi = vcnt % 2
            vcnt += 1
            vk = "vst0"
            for m in range(NBLK):
                bi = 4 + (m % 2)
                ps = banks[bi]
                pk = f"bank{bi}"
                t = m // 4
                for c in range(8):
                    K.mm(ps[:, 0:gcols], hT[:, c, m * 128:(m + 1) * 128], w[:, c, 0:gcols], c == 0, c == 7, [wk, ("hT", t)], [pk])
                K.copy("act" if m % 2 == 0 else "dve", vst[vi][:, 0:nh, m, 0:64],
                       ps[:, 0:gcols].rearrange("p (h d) -> p h d", h=nh), [pk], [vk])
            for h in range(nh):
                K.dma("sp", f"{vk}_{h // (3 if nh == 6 else 2)}", io[oname][h], vst[vi][:, h, :, :], [vk], [(oname, h)])
            try_ag()

        return vcnt

    def mla_norms():
        for t in range(4):
            sl = slice(t * 512, (t + 1) * 512)
            rs = rstd[t % 2]
            rk = f"rstd{t % 2}"
            rms_stats(K, C, [cqT[:, c, sl] for c in range(3)], [("cqT", t)] * 3, 128, "c384", "eps6", banks[7], "bank7", rs[:], rk, sq, sqk)
            for c in range(3):
                K.stt("dve", cqn[:, c, sl], cqT[:, c, sl], vec[:, 24 + c:25 + c], rs[:], ALU.mult, ALU.mult,
                      [("cqT", t), rk, "vec"], [("cqn", t)])
            rs = rstd[(t + 1) % 2]
            rk = f"rstd{(t + 1) % 2}"
            rms_stats(K, C, [ckvT[:, sl]], [("ckvT", t)], 128, "c128", "eps6", banks[6], "bank6", rs[:], rk, sq, sqk)
            K.stt("dve", ckvn[:, sl], ckvT[:, sl], vec[:, 27:28], rs[:], ALU.mult, ALU.mult, [("ckvT", t), rk, "vec"], [("ckvn", t)])

    def mla(vcnt):
        wq, wqk = wload(io["Wuq"], [128, 3, 1152])
        for h in range(6):
            si = scnt[0] % NSTG
            scnt[0] += 1
            for t in range(4):
                sl = slice(t * 512, (t + 1) * 512)
                ba = pcnt[0] % 4
                pcnt[0] += 1
                bb = pcnt[0] % 4
                pcnt[0] += 1
                psA, psB = banks[ba], banks[bb]
                ka, kb = f"bank{ba}", f"bank{bb}"
                for c in range(3):
                    K.mm(psA[0:96, :], wq[:, c, h * 96:(h + 1) * 96], cqn[:, c, sl], c == 0, c == 2, [wqk, ("cqn", t)], [ka])
                for c in range(3):
                    K.mm(psB[0:96, :], wq[:, c, 576 + h * 96:576 + (h + 1) * 96], cqn[:, c, sl], c == 0, c == 2, [wqk, ("cqn", t)], [kb])
                K.copy("act", stg[si][0:64, sl], psA[0:64, :], [ka], [f"stg{si}"])
                rope(psA, psB, ka, kb, t32c, t32s, "t32c", "t32s", stg[si][64:96, sl], f"stg{si}", 64, 96, t)
            K.dma("sp", f"stg{si}", io["QB"][h], stg[si][0:96, :], [f"stg{si}"], [("QB", h)])

        wkv, wkvk = wload(io["Wukv"], [128, 768])
        for h in range(6):
            si = scnt[0] % NSTG
            scnt[0] += 1
            for t in range(4):
                sl = slice(t * 512, (t + 1) * 512)
                ba = pcnt[0] % 4
                pcnt[0] += 1
                psA = banks[ba]
                ka = f"bank{ba}"
                K.mm(psA[0:64, :], wkv[:, h * 64:(h + 1) * 64], ckvn[:, sl], True, True, [wkvk, ("ckvn", t)], [ka])
                K.copy("act" if t % 2 == 0 else "dve", stg[si][0:64, sl], psA[0:64, :], [ka], [f"stg{si}"])
            K.dma("sp", f"stg{si}", io["KB"][h, 0:64, :], stg[si][0:64, :], [f"stg{si}"], [("KB", h)])
            K.dma("sp", f"krT_out{h // 2}", io["KB"][h, 64:96, :], krT[64:96, :], ["krT"], [("KBr", h)])
        vi = vcnt % 2
        vcnt += 1
        vk = "vst0"
        for m in range(NBLK):
            bi = 4 + (m % 2)
            ps = banks[bi]
            pk = f"bank{bi}"
            t = m // 4
            K.mm(ps[:, 0:384], ckvn[:, m * 128:(m + 1) * 128], wkv[:, 384:768], True, True, [wkvk, ("ckvn", t)], [pk])
            K.copy("act" if m % 2 == 0 else "dve", vst[vi][:, 0:6, m, 0:64],
                   ps[:, 0:384].rearrange("p (h d) -> p h d", h=6), [pk], [vk])
        for h in range(6):
            K.dma("sp", f"{vk}_{h // 3}", io["VB"][h], vst[vi][:, h, :, :], [vk], [("VB", h)])
        try_ag()

    fm_groups(FM_GROUPS[5:7])
    mla_norms()
    vc = tm_groups(TM_GROUPS[0:1], 0)
    fm_groups(FM_GROUPS[0:3])
    fm_groups(FM_GROUPS[3:5])
    vc = tm_groups(TM_GROUPS[1:2], vc)
    mla(vc)
    try_ag(flush=True)


P1_OUT = dict(QA=[384, TL], KA=[384, TL], QC=[256, TL], KC=[256, TL], QB=[6, 96, TL], KB=[6, 96, TL],
              VA=[6, 128, NBLK, 65], VB=[6, 128, NBLK, 65], VC=[4, 128, NBLK, 65])


def build_p1():
    nc = bass.Bass("TRN2", target_bir_lowering=False)
    with contextlib.ExitStack() as st:
        K = Ctx(nc, st)
        io = {}
        io["xT"] = K.dram("xT", [128, 8, TL], F32, "ExternalInput")
        io["W1"] = K.dram("W1", [128, 8, W1COLS], F32, "ExternalInput")
        io["Wuq"] = K.dram("Wuq", [128, 3, 1152], F32, "ExternalInput")
        io["Wukv"] = K.dram("Wukv", [128, 768], F32, "ExternalInput")
        io["vecs"] = K.dram("vecs", [128, NVEC], F32, "ExternalInput")
        rope = K.dram("rope", [4, 128, TL], F32, "ExternalInput")
        io["rope"] = [rope[i] for i in range(4)]
        for name, shape in P1_OUT.items():
            io[name] = K.dram(name, shape, BF16, "ExternalOutput")
        C = make_consts(K)
        phase1(K, C, io)
        outs = [o for o in K.P.ops["sp"] if o.is_dma and o.dma_sem.startswith(("stg", "vst", "krT_out"))]
        K.P.emit(final_wait_ops=outs)
    return nc


def _swap_cols(cols, blk):
    cols = np.asarray(cols)
    out = cols.copy().reshape(-1, blk)
    half = blk // 2
    out = np.concatenate([out[:, half:], out[:, :half]], axis=1)
    return out.reshape(-1)


def w1_columns():
    cols = []
    for base in (OFF["qa"], OFF["ka"]):
        for t in range(3):
            c = np.arange(base + t * 128, base + (t + 1) * 128)
            cols += [c, _swap_cols(c, 64)]
    for base in (OFF["qc"], OFF["kc"]):
        for t in range(2):
            c = np.arange(base + t * 128, base + (t + 1) * 128)
            cols += [c, _swap_cols(c, 32)]
    cols.append(np.arange(OFF["cq"], OFF["cq"] + 384))
    cols.append(np.arange(OFF["ckv"], OFF["ckv"] + 128))
    filler = np.arange(OFF["ckv"], OFF["ckv"] + 64)
    kr = np.arange(OFF["kr"], OFF["kr"] + 32)
    cols += [filler, kr, filler, _swap_cols(kr, 32)]
    cols.append(np.arange(OFF["va"], OFF["va"] + 384))
    cols.append(np.arange(OFF["vc"], OFF["vc"] + 256))
    cols = np.concatenate(cols)
    assert cols.shape[0] == W1COLS
    return cols


def chunked(w, nchunk):
    n = w.shape[1]
    return np.ascontiguousarray(w.reshape(nchunk, 128, n).transpose(1, 0, 2))


def prep_layer(inp, l):
    f = lambda a: np.asarray(a, dtype=np.float32)
    out = {}
    w_in = f(inp["w_in"][l])
    out["W1"] = chunked(w_in[:, w1_columns()], 8)
    wuq = f(inp["mla_w_uq"][l])
    ncols, scols = [], []
    for h in range(6):
        c = np.arange(h * 96, (h + 1) * 96)
        ncols.append(c)
        scols.append(np.concatenate([c[:64], _swap_cols(c[64:], 32)]))
    out["Wuq"] = chunked(wuq[:, np.concatenate(ncols + scols)], 3)
    wukv = f(inp["mla_w_ukv"][l])
    kc = np.concatenate([np.arange(h * 128, h * 128 + 64) for h in range(6)])
    vc = np.concatenate([np.arange(h * 128 + 64, h * 128 + 128) for h in range(6)])
    out["Wukv"] = np.ascontiguousarray(wukv[:, np.concatenate([kc, vc])])
    vec = np.zeros((128, NVEC), np.float32)
    vec[:, 0:8] = f(inp["ln1_g"][l]).reshape(8, 128).T
    vec[:, 8:16] = f(inp["ln2_g"][l]).reshape(8, 128).T
    vec[:, 16:24] = f(inp["final_g"]).reshape(8, 128).T
    vec[:, 24:27] = f(inp["mla_q_norm_g"][l]).reshape(3, 128).T
    vec[:, 27] = f(inp["mla_kv_norm_g"][l])
    vec[:, 28] = np.tile(f(inp["diff_subln_g"][l]), 2)
    lam_init = 0.8 - 0.6 * math.exp(-0.3 * l)
    vec[:, 29] = lam_init
    vec[:, 30] = 1.0 - lam_init
    vec[:, 32:160] = f(inp["diff_lambda"][l]).reshape(1, 128)
    out["vecs"] = vec
    out["Wo"] = chunked(f(inp["w_o"][l]), 8)
    out["Wup"] = chunked(f(inp["w_up"][l]), 8)
    out["Wdown"] = chunked(f(inp["w_down"][l]), 32)
    return out


def core_positions(j):
    m = np.arange(NBLK)[:, None]
    t = np.arange(128)[None, :]
    return ((4 * m + j) * 128 + t).reshape(-1)


def rope_tables(j):
    pos = core_positions(j).astype(np.float32)
    tabs = []
    for dim in (64, 32):
        half = dim // 2
        inv = (1.0 / (np.float32(10000.0) ** (np.arange(half, dtype=np.float32) / np.float32(half)))).astype(np.float32)
        r = np.arange(128)
        i = r % dim
        fidx = i % half
        ang = (pos[None, :] * inv[fidx][:, None]).astype(np.float32)
        sign = np.where(i < half, -1.0, 1.0).astype(np.float32)[:, None]
        tabs.append(np.cos(ang).astype(np.float32))
        tabs.append((np.sin(ang) * sign).astype(np.float32))
    return np.stack(tabs, 0)


def x_to_core(x, c):
    b, j = c // 4, c % 4
    xs = np.asarray(x[b], dtype=np.float32)[core_positions(j)]
    return chunked(np.ascontiguousarray(xs.T), 8)


def a_cols(ak):
    return max(0, ak) * 128, min(4, ak + 5) * 128


def phase2(K, C, io):
    P = K.P
    banks = K.banks
    vec = K.sb("vec", [128, NVEC], F32)
    K.dma("sp", "vec", vec[:], io["vecs"], [], ["vec"])
    MAt = K.sb("MAt", [128, 32, 512], BF16)
    MBt = K.sb("MBt", [128, 16, 512], BF16)

    kt = [K.sb(f"kt{i}", [128, 4, TL], BF16) for i in range(2)]
    vt = [K.sb(f"vt{i}", [128, 4 * NBLK, 65], BF16) for i in range(2)]
    qa = [K.sb(f"qa{i}", [128, TL], BF16) for i in range(2)]
    qb = [K.sb(f"qb{i}", [128, TL], BF16) for i in range(2)]
    qc = [K.sb(f"qc{i}", [128, 2, TL], BF16) for i in range(2)]
    for qq, nm in ((qa, "qa"), (qc, "qc"), (qb, "qb")):
        for i in range(2):
            K.memset("pool", qq[i][:], 0.0, [f"{nm}{i}"])
    NPT = 6
    LAG = 3
    NSB = 3
    pt = [K.sb(f"pt{i}", [128, 2, 512], BF16) for i in range(NPT)]
    osb = [K.sb(f"osb{i}", [128, 512], F32) for i in range(4)]
    rz = [K.sb(f"rz{i}", [128, 512], F32) for i in range(4)]
    ones512 = K.sb("ones512", [128, 512], F32)
    K.memset("pool", ones512[:], -1.0, ["ones512"])
    deferred = []
    deferred_a = []
    tA = K.sb("tA", [128, 512], F32)
    tB = K.sb("tB", [128, 512], F32)
    tO = K.sb("tO", [128, 512], F32)
    tS = K.sb("tS", [128, 512], F32)
    ostg = [K.sb(f"ostg{i}", [128, TL], BF16) for i in range(2)]
    lam = K.sb("lam", [128, 40], F32)

    K.tt("dve", lam[:, 0:32], vec[:, 32:64], vec[:, 64:96], ALU.mult, ["vec"], ["lam"])
    P.op("dve", lambda e: e.reduce_sum(out=lam[:, 32:33], in_=lam[:, 0:32], axis=mybir.AxisListType.X), ["lam"], ["lam1"])
    K.tt("dve", lam[:, 0:32], vec[:, 96:128], vec[:, 128:160], ALU.mult, ["vec", "lam1"], ["lam"])
    P.op("dve", lambda e: e.reduce_sum(out=lam[:, 33:34], in_=lam[:, 0:32], axis=mybir.AxisListType.X), ["lam"], ["lam2"])
    K.act(lam[:, 34:36], lam[:, 32:34], AF.Exp, ["lam1", "lam2"], ["lam3"])
    K.tt("dve", lam[:, 36:37], lam[:, 35:36], lam[:, 34:35], ALU.subtract, ["lam3"], ["lam4"])
    K.tt("dve", lam[:, 37:38], lam[:, 36:37], vec[:, 29:30], ALU.subtract, ["lam4", "vec"], ["neglam"])
    K.tt("dve", lam[:, 38:39], vec[:, 28:29], vec[:, 30:31], ALU.mult, ["vec"], ["gc"])
    neglam = lam[:, 37:38]
    gc = lam[:, 38:39]

    jobs = [("A", h) for h in range(6)] + [("C", h) for h in range(4)] + [("B", h) for h in range(6)]

    jobslot = {}

    def load_job(ji):
        kind, h = jobs[ji]
        s = ji % 2
        a = io["acc"](kind, h)
        rows = a["rows"]
        if kind == "A":
            qs = h % 2
            K.dma("sp", f"qa{qs}", qa[qs][qs * 64:qs * 64 + 64, :], a["q"], [], [f"qa{qs}"])
            jobslot[ji] = [(qa[qs], f"qa{qs}", None)]
        elif kind == "B":
            qs = h % 2
            K.dma("sp", f"qb{qs}", qb[qs][0:96, :], a["q"], [], [f"qb{qs}"])
            jobslot[ji] = [(qb[qs], f"qb{qs}", None)]
        else:
            qs = h % 2
            for mi in range(2):
                p0 = qs * 64 + mi * 32
                K.dma("sp", f"qc{qs}", qc[qs][p0:p0 + 32, mi, :], a["q"][mi * 32:(mi + 1) * 32, :], [], [f"qc{qs}"])
            jobslot[ji] = [(qc[qs], f"qc{qs}", 0), (qc[qs], f"qc{qs}", 1)]
        for r in range(4):
            K.dma("sp", f"kt{s}", kt[s][0:rows, r, :], a["k"](r), a["kdeps"], [f"kt{s}"])
        for r in range(4):
            K.dma("sp", f"vt{s}", vt[s][:, r * NBLK:(r + 1) * NBLK, :], a["v"](r), a["vdeps"], [f"vt{s}"])

    cnt = dict(s=0, p=0, o=0, f=0)
    out_ops = []
    K.dma("act", "MAt_hi", MAt[:, 16:32, :], io["MA"][:, 16:32, :], [], ["MAt_hi"])
    K.dma("act", "MAt_lo", MAt[:, 0:16, :], io["MA"][:, 0:16, :], [], ["MAt_lo"])
    K.dma("act", "MBt", MBt[:], io["MB"], [], ["MBt"])
    load_job(0)
    for ji, (kind, h) in enumerate(jobs):
        if ji + 1 < len(jobs):
            load_job(ji + 1)
        s = ji % 2
        kk, vk = f"kt{s}", f"vt{s}"
        maps = jobslot[ji]
        if kind == "A":
            scale, row0 = 64 ** -0.5, h * 64
        elif kind == "B":
            scale, row0 = 96 ** -0.5, 384 + h * 64
        else:
            scale, row0 = 32 ** -0.5, 768 + h * 64
        og = ostg[ji % 2]
        ogk = f"ostg{ji % 2}"
        for g in range(4):
            blocks = []
            if kind == "A":
                for ak in (0, -1, -2, -3, -4, 1, 2, 3):
                    mloc = 4 * g + ak
                    if mloc < 0:
                        continue
                    c0, c1 = a_cols(ak)
                    for r in range(4):
                        blocks.append((r, mloc, c0, c1, MAt[:, (ak + 4) * 4 + r, :], "MAt_hi" if ak >= 0 else "MAt_lo"))
            else:
                for mloc in range(4 * g + 4):
                    ak = mloc - 4 * g
                    for r in range(4):
                        if ak < 0:
                            blocks.append((r, mloc, 0, 512, None, None))
                        else:
                            blocks.append((r, mloc, ak * 128, 512, MBt[:, ak * 4 + r, :], "MBt"))
            steps = [(b, mi) for b in blocks for mi in range(len(maps))]
            O = [banks[6 + mi] for mi in range(len(maps))]
            Ok = [f"bank{6 + mi}" for mi in range(len(maps))]
            pend = []
            nst = len(steps)
            first = [True] * len(maps)
            lastidx = [max(i for i, (b, mi) in enumerate(steps) if mi == m_) for m_ in range(len(maps))]
            assert nst % 2 == 0
            npair = nst // 2
            for ip in range(npair + LAG):
                if ip == min(1, npair - 1) and deferred_a:
                    for f in deferred_a:
                        f()
                    deferred_a.clear()
                if ip == min(7, npair - 1) and deferred:
                    for f in deferred:
                        f()
                    deferred.clear()
                if ip < npair:
                    sb_ = cnt["s"] % NSB
                    cnt["s"] += 1
                    pi = cnt["p"] % NPT
                    cnt["p"] += 1
                    SP = K.pairs[sb_]
                    spk, ptk = f"sp{sb_}", f"pt{pi}"
                    c0, c1 = steps[2 * ip][0][2], steps[2 * ip][0][3]
                    assert (steps[2 * ip + 1][0][2], steps[2 * ip + 1][0][3]) == (c0, c1)
                    for hf in range(2):
                        (r, mloc, _, _, mask, mkey), mi = steps[2 * ip + hf]
                        qbuf, qk, qm = maps[mi]
                        qsrc = qbuf[:, g * 512 + c0:g * 512 + c1] if qm is None else qbuf[:, qm, g * 512 + c0:g * 512 + c1]
                        K.mm(SP[:, hf, c0:c1], kt[s][:, r, mloc * 128:(mloc + 1) * 128], qsrc, True, True, [kk, qk], [spk])
                    K.act(pt[pi][:, :, c0:c1], SP[:, :, c0:c1], AF.Exp, [spk], [ptk], scale=float(scale))
                    for hf in range(2):
                        (r, mloc, _, _, mask, mkey), mi = steps[2 * ip + hf]
                        if mask is not None:
                            K.tt("dve", pt[pi][:, hf, c0:c1], pt[pi][:, hf, c0:c1], mask[:, c0:c1], ALU.mult, [ptk, mkey], [ptk])
                    pend.append((ip, c0, c1, pi))
                if ip >= LAG:
                    (ip0, c0, c1, pi) = pend.pop(0)
                    for hf in range(2):
                        i0 = 2 * ip0 + hf
                        (r, mloc, _, _, mask, mkey), mi = steps[i0]
                        K.mm(O[mi][0:65, c0:c1], vt[s][:, r * NBLK + mloc, 0:65], pt[pi][:, hf, c0:c1], first[mi], i0 == lastidx[mi],
                             [vk, f"pt{pi}"], [Ok[mi]])
                        first[mi] = False
            gsl = slice(g * 512, (g + 1) * 512)

            def fbank():
                b = cnt["s"] % NSB
                cnt["s"] += 1
                return K.pairs[b][:, 0, :], f"sp{b}"

            def prec(i):
                K.recip(rz[i][64:65, :], osb[i][64:65, :], [f"osb{i}"], [f"rz{i}"])

            if kind in ("A", "B"):
                fi = cnt["f"] % 4
                cnt["f"] += 1
                K.copy("act", osb[fi][0:65, :], O[0][0:65, :], [Ok[0]], [f"osb{fi}"])

                def part2a(fi=fi):
                    K.recip(rz[fi][64:65, :], osb[fi][64:65, :], [f"osb{fi}"], [f"rz{fi}"])

                def part2(fi=fi, og=og, ogk=ogk, gsl=gsl):
                    fb, fk = fbank()
                    K.mm(fb[0:64, :], C["one"][64:65, 0:64], rz[fi][64:65, :], True, True, [f"rz{fi}", "one"], [fk])
                    K.tt("dve", og[0:64, gsl], osb[fi][0:64, :], fb[0:64, :], ALU.mult, [f"osb{fi}", fk], [ogk])
            else:
                f0 = cnt["f"] % 4
                f1 = (cnt["f"] + 1) % 4
                cnt["f"] += 2
                K.copy("act", osb[f0][0:65, :], O[0][0:65, :], [Ok[0]], [f"osb{f0}"])
                K.copy("act", osb[f1][0:65, :], O[1][0:65, :], [Ok[1]], [f"osb{f1}"])

                def part2a(f0=f0, f1=f1):
                    K.recip(rz[f0][64:65, :], osb[f0][64:65, :], [f"osb{f0}"], [f"rz{f0}"])
                    K.recip(rz[f1][64:65, :], osb[f1][64:65, :], [f"osb{f1}"], [f"rz{f1}"])

                def part2(f0=f0, f1=f1, og=og, ogk=ogk, gsl=gsl):
                    fb, fk = fbank()
                    K.mm(fb[0:64, :], C["one"][64:65, 0:64], rz[f0][64:65, :], True, True, [f"rz{f0}", "one"], [fk])
                    K.tt("dve", tA[0:64, :], osb[f0][0:64, :], fb[0:64, :], ALU.mult, [f"osb{f0}", fk], ["tA"])
                    fb, fk = fbank()
                    K.mm(fb[0:64, :], C["one"][64:65, 0:64], rz[f1][64:65, :], True, True, [f"rz{f1}", "one"], [fk])
                    K.tt("dve", tB[0:64, :], osb[f1][0:64, :], fb[0:64, :], ALU.mult, [f"osb{f1}", fk], ["tB"])
                    K.stt("dve", tO[0:64, :], tB[0:64, :], neglam[0:64, 0:1], tA[0:64, :], ALU.mult, ALU.add, ["tA", "tB", "neglam"], ["tO"])
                    K.tt("pool", tS[0:64, :], tO[0:64, :], tO[0:64, :], ALU.mult, ["tO"], ["tS"])
                    fb, fk = fbank()
                    K.mm(fb[0:64, :], C["c64"][0:64, 0:64], tS[0:64, :], True, True, ["tS", "c64"], [fk])
                    K.act(tS[0:64, :], fb[0:64, :], AF.Sqrt, [fk, "eps5"], ["tS"], bias=C["eps5"][0:64, 0:1], scale=1.0)
                    K.recip(tS[0:64, :], tS[0:64, :], ["tS"], ["tS"])
                    K.stt("dve", og[0:64, gsl], tO[0:64, :], gc[0:64, 0:1], tS[0:64, :], ALU.mult, ALU.mult, ["tO", "tS", "gc"], [ogk])
            deferred_a.append(part2a)
            deferred.append(part2)
        deferred.append(lambda og=og, ogk=ogk, row0=row0: out_ops.append(
            K.dma("sp", ogk, io["mixT"][row0:row0 + 64, :], og[0:64, :], [ogk], [("mixT", row0)])))
    for f in deferred_a + deferred:
        f()
    deferred.clear()
    return out_ops


def simple_acc(io):
    def acc(kind, h):
        if kind == "A":
            return dict(rows=128, q=io["QA"][h * 64:(h + 1) * 64, :], k=lambda r: io["KAg"][r, (h // 2) * 128:(h // 2 + 1) * 128, :],
                        v=lambda r: io["VAg"][r, h], kdeps=[], vdeps=[])
        if kind == "B":
            return dict(rows=96, q=io["QB"][h], k=lambda r: io["KBg"][r, h], v=lambda r: io["VBg"][r, h], kdeps=[], vdeps=[])
        return dict(rows=128, q=io["QC"][h * 64:(h + 1) * 64, :], k=lambda r: io["KCg"][r, (h // 2) * 128:(h // 2 + 1) * 128, :],
                    v=lambda r: io["VCg"][r, h], kdeps=[], vdeps=[])
    return acc


def build_p2():
    nc = bass.Bass("TRN2", target_bir_lowering=False)
    with contextlib.ExitStack() as st:
        K = Ctx(nc, st)
        io = {}
        io["vecs"] = K.dram("vecs", [128, NVEC], F32, "ExternalInput")
        io["MA"] = K.dram("MA", [128, 32, 512], BF16, "ExternalInput")
        io["MB"] = K.dram("MB", [128, 16, 512], BF16, "ExternalInput")
        io["QA"] = K.dram("QA", [384, TL], BF16, "ExternalInput")
        io["QB"] = K.dram("QB", [6, 96, TL], BF16, "ExternalInput")
        io["QC"] = K.dram("QC", [256, TL], BF16, "ExternalInput")
        io["KAg"] = K.dram("KAg", [4, 384, TL], BF16, "ExternalInput")
        io["KBg"] = K.dram("KBg", [4, 6, 96, TL], BF16, "ExternalInput")
        io["KCg"] = K.dram("KCg", [4, 256, TL], BF16, "ExternalInput")
        io["VAg"] = K.dram("VAg", [4, 6, 128, NBLK, 65], BF16, "ExternalInput")
        io["VBg"] = K.dram("VBg", [4, 6, 128, NBLK, 65], BF16, "ExternalInput")
        io["VCg"] = K.dram("VCg", [4, 4, 128, NBLK, 65], BF16, "ExternalInput")
        io["mixT"] = K.dram("mixT", [1024, TL], BF16, "ExternalOutput")
        C = make_consts(K)
        io["acc"] = simple_acc(io)
        outs = phase2(K, C, io)
        K.P.emit(final_wait_ops=outs)
    return nc


def attn_masks(j):
    tk = np.arange(128)[:, None, None]
    aq = np.arange(4)[None, :, None]
    tq = np.arange(128)[None, None, :]
    MA = np.zeros((128, 32, 4, 128), np.float32)
    MB = np.zeros((128, 16, 4, 128), np.float32)
    for ak in range(-4, 4):
        for r in range(4):
            dist = (4 * (aq - ak) + j - r) * 128 + tq - tk
            cnt = ((dist >= 0) & (dist <= 128)).astype(np.float32)
            cnt += ((dist >= 0) & (dist <= 512) & (dist % 4 == 0))
            cnt += ((dist >= 0) & (dist <= 2048) & (dist % 16 == 0))
            MA[:, (ak + 4) * 4 + r] = cnt
            if ak >= 0:
                MB[:, ak * 4 + r] = (dist >= 0)
    return (MA.reshape(128, 32, 512).astype(ml_dtypes.bfloat16), MB.reshape(128, 16, 512).astype(ml_dtypes.bfloat16))


def phase3(K, C, io, last):
    P = K.P
    banks = K.banks
    vec = K.sb("vec", [128, NVEC], F32)
    K.dma("sp", "vec", vec[:], io["vecs"], [], ["vec"])
    xT = K.sb("xT", [128, 8, TL], F32)
    mx = K.sb("mx", [128, 8, TL], BF16)
    aT = [K.sb(f"aT{i}", [128, 4, TL], BF16) for i in range(2)]
    wsl = [K.sb(f"wsl{i}", [128, 4096], BF16) for i in range(2)]
    rl = [K.sb(f"rl{i}", [128, 512], F32) for i in range(2)]
    sq = [K.sb(f"sq{i}", [128, 512], F32) for i in range(2)]
    sqk = ["sq0", "sq1"]
    rstd = [K.sb(f"rstd{i}", [128, 512], F32) for i in range(2)]
    mixv = io["mixT"].rearrange("(c p) t -> p c t", p=128)
    for t in range(4):
        sl = slice(t * 512, (t + 1) * 512)
        K.dma("sp", f"mx{t}", mx[:, :, sl], mixv[:, :, sl], [], [("mx", t)])
        K.dma("sp", f"xT{t}", xT[:, :, sl], io["xT"][:, :, sl], [], [("xT", t)])
    items = [(io["Wo"][:, :, hf * 512:(hf + 1) * 512], [128, 8, 512]) for hf in range(2)]
    for fg in range(8):
        items.append((io["Wup"][:, :, fg * 512:(fg + 1) * 512], [128, 8, 512]))
        items.append((io["Wdown"][:, fg * 4:(fg + 1) * 4, :], [128, 4, 1024]))
    ws = WStream(K, wsl, ["wsl0", "wsl1"], items)
    pc = [0]

    def nbank():
        b = pc[0] % 4
        pc[0] += 1
        return banks[b], f"bank{b}"

    for hf in range(2):
        w, wk = ws.next()
        for dl in range(4):
            d = hf * 4 + dl
            for t in range(4):
                sl = slice(t * 512, (t + 1) * 512)
                ps, pk = nbank()
                for c in range(8):
                    K.mm(ps[:, :], w[:, c, dl * 128:(dl + 1) * 128], mx[:, c, sl], c == 0, c == 7, [wk, ("mx", t)], [pk])
                K.tt("dve", xT[:, d, sl], ps[:, :], xT[:, d, sl], ALU.add, [pk, ("xT", t)], [("xT", t)])
    for t in range(4):
        sl = slice(t * 512, (t + 1) * 512)
        rs, rk = rstd[t % 2], f"rstd{t % 2}"
        rms_stats(K, C, [xT[:, c, sl] for c in range(8)], [("xT", t)] * 8, 128, "c1024", "eps6", banks[7], "bank7", rs[:], rk, sq, sqk)
        for c in range(8):
            K.stt("dve", mx[:, c, sl], xT[:, c, sl], vec[:, 8 + c:9 + c], rs[:], ALU.mult, ALU.mult, [("xT", t), rk, "vec"], [("mx", t)])
    rc = 0
    for fg in range(8):
        wu, wuk = ws.next()
        a = aT[fg % 2]
        ak = f"aT{fg % 2}"
        for fl in range(4):
            for t in range(4):
                sl = slice(t * 512, (t + 1) * 512)
                ps, pk = nbank()
                for c in range(8):
                    K.mm(ps[:, :], wu[:, c, fl * 128:(fl + 1) * 128], mx[:, c, sl], c == 0, c == 7, [wuk, ("mx", t)], [pk])
                ri = rc % 2
                rc += 1
                K.act(rl[ri][:, :], ps[:, :], AF.Relu, [pk], [f"rl{ri}"])
                K.tt("pool", a[:, fl, sl], rl[ri][:, :], rl[ri][:, :], ALU.mult, [f"rl{ri}"], [(ak, t)])
        wd, wdk = ws.next()
        for d in range(8):
            for t in range(4):
                sl = slice(t * 512, (t + 1) * 512)
                ps, pk = nbank()
                for fl in range(4):
                    K.mm(ps[:, :], wd[:, fl, d * 128:(d + 1) * 128], a[:, fl, sl], fl == 0, fl == 3, [wdk, (ak, t)], [pk])
                K.tt("dve", xT[:, d, sl], ps[:, :], xT[:, d, sl], ALU.add, [pk, ("xT", t)], [("xT", t)])
    outs = []
    if last:
        for t in range(4):
            sl = slice(t * 512, (t + 1) * 512)
            rs, rk = rstd[t % 2], f"rstd{t % 2}"
            rms_stats(K, C, [xT[:, c, sl] for c in range(8)], [("xT", t)] * 8, 128, "c1024", "eps6", banks[7], "bank7", rs[:], rk, sq, sqk)
            for c in range(8):
                K.stt("dve", xT[:, c, sl], xT[:, c, sl], vec[:, 16 + c:17 + c], rs[:], ALU.mult, ALU.mult, [("xT", t), rk, "vec"], [("xT", t)])
    for t in range(4):
        sl = slice(t * 512, (t + 1) * 512)
        outs.append(K.dma("sp", "xTo", io["xout"][:, :, sl], xT[:, :, sl], [("xT", t)], [("xout", t)]))
    return outs


def build_p3(last):
    nc = bass.Bass("TRN2", target_bir_lowering=False)
    with contextlib.ExitStack() as st:
        K = Ctx(nc, st)
        io = {}
        io["vecs"] = K.dram("vecs", [128, NVEC], F32, "ExternalInput")
        io["xT"] = K.dram("xT", [128, 8, TL], F32, "ExternalInput")
        io["mixT"] = K.dram("mixT", [1024, TL], BF16, "ExternalInput")
        io["Wo"] = K.dram("Wo", [128, 8, 1024], F32, "ExternalInput")
        io["Wup"] = K.dram("Wup", [128, 8, DFF], F32, "ExternalInput")
        io["Wdown"] = K.dram("Wdown", [128, 32, 1024], F32, "ExternalInput")
        io["xout"] = K.dram("xout", [128, 8, TL], F32, "ExternalOutput")
        C = make_consts(K)
        outs = phase3(K, C, io, last)
        K.P.emit(final_wait_ops=outs)
    return nc


_PROGS = {}


def _prog(name):
    if name not in _PROGS:
        _PROGS[name] = dict(p1=build_p1, p2=build_p2, p3a=lambda: build_p3(False), p3b=lambda: build_p3(True))[name]()
    return _PROGS[name]


def _run(nc, in_maps):
    res = run_bass_kernel_spmd(nc, in_maps, core_ids=list(range(NCORE)))
    return res.results


def kernel_unfused(**inputs):
    x = np.asarray(inputs["x"], dtype=np.float32)
    xs = [x_to_core(x, c) for c in range(NCORE)]
    ropes = [rope_tables(j) for j in range(4)]
    masks = [attn_masks(j) for j in range(4)]
    for l in range(2):
        lw = prep_layer(inputs, l)
        r1 = _run(_prog("p1"), [dict(xT=xs[c], W1=lw["W1"], Wuq=lw["Wuq"], Wukv=lw["Wukv"], vecs=lw["vecs"], rope=ropes[c % 4])
                                for c in range(NCORE)])
        in2 = []
        gath = {}
        for b in range(2):
            for k in ("KA", "KB", "KC", "VA", "VB", "VC"):
                gath[(b, k)] = np.stack([np.asarray(r1[4 * b + j][k]) for j in range(4)], 0)
        for c in range(NCORE):
            b, j = c // 4, c % 4
            in2.append(dict(vecs=lw["vecs"], MA=masks[j][0], MB=masks[j][1],
                            QA=np.asarray(r1[c]["QA"]), QB=np.asarray(r1[c]["QB"]), QC=np.asarray(r1[c]["QC"]),
                            KAg=gath[(b, "KA")], KBg=gath[(b, "KB")], KCg=gath[(b, "KC")],
                            VAg=gath[(b, "VA")], VBg=gath[(b, "VB")], VCg=gath[(b, "VC")]))
        r2 = _run(_prog("p2"), in2)
        r3 = _run(_prog("p3b" if l == 1 else "p3a"),
                  [dict(vecs=lw["vecs"], xT=xs[c], mixT=np.asarray(r2[c]["mixT"]), Wo=lw["Wo"], Wup=lw["Wup"], Wdown=lw["Wdown"])
                   for c in range(NCORE)])
        xs = [np.asarray(r3[c]["xout"]) for c in range(NCORE)]
    out = np.empty((2, SEQ, D), np.float32)
    for c in range(NCORE):
        b, j = c // 4, c % 4
        xt = xs[c].transpose(1, 0, 2).reshape(D, TL)
        out[b, core_positions(j), :] = xt.T
    return out


KROWS = 384 + 576 + 256
RG = [[0, 1, 2, 3], [4, 5, 6, 7]]


def _phase(nc, tag, body, ext_sems=None):
    with nc.cleanup_on_exit():
        with contextlib.ExitStack() as st:
            K = Ctx(nc, st, tag, ext_sems)
            C = make_consts(K)
            outs = body(K, C)
            K.P.emit(final_wait_ops=outs, tag=tag, managed=False)
        nc.all_engine_barrier()


def build_fused():
    nc = bass.Bass("TRN2", target_bir_lowering=False, num_devices=NCORE)
    ext = lambda name, shape, dt: nc.dram_tensor(name, shape, dt, kind="ExternalInput").ap()
    internal = lambda name, shape, dt: nc.dram_tensor(name, shape, dt, kind="Internal").ap()
    xT_in = ext("xT", [128, 8, TL], F32)
    rope = ext("rope", [4, 128, TL], F32)
    MA = ext("MA", [128, 32, 512], BF16)
    MB = ext("MB", [128, 16, 512], BF16)
    xout = nc.dram_tensor("xout", [128, 8, TL], F32, kind="ExternalOutput").ap()
    x1T = internal("x1T", [128, 8, TL], F32)
    agsem = {g: nc.alloc_semaphore(name="agsem" + g) for g in "ABC"}
    ag_val = {g: 0 for g in "ABC"}
    for l in range(2):
        W = dict(W1=ext(f"W1_{l}", [128, 8, W1COLS], F32), Wuq=ext(f"Wuq_{l}", [128, 3, 1152], F32),
                 Wukv=ext(f"Wukv_{l}", [128, 768], F32), vecs=ext(f"vecs_{l}", [128, NVEC], F32),
                 Wo=ext(f"Wo_{l}", [128, 8, 1024], F32), Wup=ext(f"Wup_{l}", [128, 8, DFF], F32),
                 Wdown=ext(f"Wdown_{l}", [128, 32, 1024], F32))
        kloc2 = internal(f"kloc{l}", [KROWS * 8, 256], BF16)
        vloc2 = internal(f"vloc{l}", [16 * 520, 256], BF16)
        kloc = kloc2.rearrange("(r a) b -> r (a b)", a=8)
        QA = internal(f"QA{l}", [384, TL], BF16)
        QB = internal(f"QB{l}", [6, 96, TL], BF16)
        QC = internal(f"QC{l}", [256, TL], BF16)
        mixT = internal(f"mixT{l}", [1024, TL], BF16)
        x_cur = xT_in if l == 0 else x1T
        x_next = x1T if l == 0 else xout
        vl = vloc2.rearrange("(h x) b -> h (x b)", h=16).rearrange("h (p m e) -> h p m e", p=128, m=NBLK)
        io1 = dict(xT=x_cur, W1=W["W1"], Wuq=W["Wuq"], Wukv=W["Wukv"], vecs=W["vecs"], rope=[rope[i] for i in range(4)],
                   QA=QA, QB=QB, QC=QC, KA=kloc[0:384, :], KB=kloc[384:960, :].rearrange("(h d) t -> h d t", h=6),
                   KC=kloc[960:KROWS, :], VA=vl[0:6], VB=vl[6:12], VC=vl[12:16])

        pieces = []
        kp = {}
        for nm, r0, nrows, npc in (("KA", 0, 128, 3), ("KB", 384, 192, 3), ("KC", 960, 128, 2)):
            for p in range(npc):
                key = f"g{nm}{p}"
                g = internal(f"{key}_{l}", [4 * nrows * 8, 256], BF16)
                if nm == "KB":
                    deps = [("KB", 2 * p), ("KBr", 2 * p), ("KB", 2 * p + 1), ("KBr", 2 * p + 1)]
                else:
                    deps = [(nm, p * 128)]
                pieces.append((key, kloc2[(r0 + p * nrows) * 8:(r0 + (p + 1) * nrows) * 8, :], g, deps))
                kp[key] = g.rearrange("(r q a) b -> r q (a b)", r=4, a=8)
        for nm, h0, nh, npc in (("VA", 0, 3, 2), ("VB", 6, 3, 2), ("VC", 12, 2, 2)):
            for p in range(npc):
                key = f"g{nm}{p}"
                g = internal(f"{key}_{l}", [4 * nh * 520, 256], BF16)
                deps = [(nm, p * nh + i) for i in range(nh)]
                pieces.append((key, vloc2[(h0 + p * nh) * 520:(h0 + (p + 1) * nh) * 520, :], g, deps))
                kp[key] = g.rearrange("(r h x) b -> r h (x b)", r=4, h=nh).rearrange("r h (p m e) -> r h p m e", p=128, m=NBLK)
        io1["ag_pieces"] = pieces
        io1["ag_ops"] = {}

        def body1(K, C, io1=io1):
            phase1(K, C, io1)
            return [o for o in K.P.ops["sp"] if o.is_dma and o.dma_sem.startswith(("stg", "vst", "krT_out"))]

        _phase(nc, f"L{l}a_", body1, ext_sems={"ag" + g: (agsem[g], ag_val[g]) for g in "ABC"})
        assert len(io1["ag_ops"]) == len(pieces), (len(io1["ag_ops"]), len(pieces))
        for g in "ABC":
            ag_val[g] = max(op.dma_cnt for key, op in io1["ag_ops"].items() if key[2] == g)
        ag_done = {key: ag_val[key[2]] for key in io1["ag_ops"]}

        def acc(kind, h, kp=kp, QA=QA, QB=QB, QC=QC):
            if kind == "A":
                kk, vk = f"gKA{h // 2}", f"gVA{h // 3}"
                return dict(rows=128, q=QA[h * 64:(h + 1) * 64, :], k=lambda r: kp[kk][r, :, :],
                            v=lambda r: kp[vk][r, h % 3], kdeps=[kk], vdeps=[vk])
            if kind == "B":
                kk, vk = f"gKB{h // 2}", f"gVB{h // 3}"
                return dict(rows=96, q=QB[h], k=lambda r: kp[kk][r, (h % 2) * 96:(h % 2 + 1) * 96, :],
                            v=lambda r: kp[vk][r, h % 3], kdeps=[kk], vdeps=[vk])
            kk, vk = f"gKC{h // 2}", f"gVC{h // 2}"
            return dict(rows=128, q=QC[h * 64:(h + 1) * 64, :], k=lambda r: kp[kk][r, :, :],
                        v=lambda r: kp[vk][r, h % 2], kdeps=[kk], vdeps=[vk])

        io2 = dict(vecs=W["vecs"], MA=MA, MB=MB, mixT=mixT, acc=acc)

        def body2(K, C, io2=io2, ag_done=ag_done):
            for key, val in ag_done.items():
                K.P.external("ag" + key[2], val, [key])
            return phase2(K, C, io2)

        _phase(nc, f"L{l}b_", body2, ext_sems={"ag" + g: (agsem[g], ag_val[g]) for g in "ABC"})
        io3 = dict(vecs=W["vecs"], xT=x_cur, mixT=mixT, Wo=W["Wo"], Wup=W["Wup"], Wdown=W["Wdown"], xout=x_next)
        _phase(nc, f"L{l}c_", lambda K, C, io3=io3, l=l: phase3(K, C, io3, l == 1))
    return nc


def kernel(**inputs):
    x = np.asarray(inputs["x"], dtype=np.float32)
    if "fused" not in _PROGS:
        _PROGS["fused"] = build_fused()
    lws = [prep_layer(inputs, l) for l in range(2)]
    ropes = [rope_tables(j) for j in range(4)]
    masks = [attn_masks(j) for j in range(4)]
    in_maps = []
    for c in range(NCORE):
        j = c % 4
        m = dict(xT=x_to_core(x, c), rope=ropes[j], MA=masks[j][0], MB=masks[j][1])
        for l in range(2):
            for k in ("W1", "Wuq", "Wukv", "vecs", "Wo", "Wup", "Wdown"):
                m[f"{k}_{l}"] = lws[l][k]
        in_maps.append(m)
    res = _run(_PROGS["fused"], in_maps)
    out = np.empty((2, SEQ, D), np.float32)
    for c in range(NCORE):
        b, j = c // 4, c % 4
        xt = np.asarray(res[c]["xout"]).transpose(1, 0, 2).reshape(D, TL)
        out[b, core_positions(j), :] = xt.T
    return out
```
